# Optimizing a Trainium2 kernel written in Bass

```python
import jax, jax.numpy as jnp
from jax import lax
import numpy as np

D_MODEL = 2048
BATCH = 2
SEQ = 4096
DEPTH = 4

N_MIXERS = 3
EPS = 1e-6
NEG_INF = -1e30

_FF_RAW = -(-8 * D_MODEL // 3)
D_FF = -(-_FF_RAW // 256) * 256

HD_A = 64
N_Q_A = D_MODEL // HD_A
N_KV_A = N_Q_A // 8
WINDOW = 128
BLOCK_A = 128

N_HEADS_B = 4
DK_B = D_MODEL // 2 // N_HEADS_B
DV_B = D_MODEL // N_HEADS_B
GATE_RANK = 16
GATE_TAU = 16.0
CHUNK = 64

N_HEADS_C = 16
QK_HEAD_C = 128
V_HEAD_C = 128
Q_RANK_C = 512
KV_RANK_C = 256
N_IDX_HEADS = 16
IDX_DIM = 64
TOPK_MAX = 256
C_BLOCK = 128
ATTN_SCALE_C = KV_RANK_C ** -0.5
IDX_W_SCALE = N_IDX_HEADS ** -0.5 * IDX_DIM ** -0.5

kernel_name = "hybrid_swa_gla_dsa_trunk"


def rms_norm(x, g):
    xf = x.astype(jnp.float32)
    y = xf * lax.rsqrt(jnp.mean(xf * xf, axis=-1, keepdims=True) + EPS)
    return (y * g.astype(jnp.float32)).astype(x.dtype)


def layer_norm(x, g, b):
    xf = x.astype(jnp.float32)
    mu = jnp.mean(xf, axis=-1, keepdims=True)
    xc = xf - mu
    y = xc * lax.rsqrt(jnp.mean(xc * xc, axis=-1, keepdims=True) + EPS)
    return (y * g.astype(jnp.float32) + b.astype(jnp.float32)).astype(x.dtype)


def swiglu(h, w_gate_up, w_down):
    g, u = jnp.split(h @ w_gate_up, 2, axis=-1)
    return (jax.nn.silu(g) * u) @ w_down


def _band(t):
    prev = jnp.concatenate([jnp.zeros_like(t[:, :1]), t[:, :-1]], axis=1)
    return jnp.concatenate([prev, t], axis=2)


def swa_sink_attention(h, w_in, q_norm, k_norm, sinks, wo):
    B, L, _ = h.shape
    nb = L // BLOCK_A
    G = N_Q_A // N_KV_A
    q, k, v = jnp.split(h @ w_in, [N_Q_A * HD_A, (N_Q_A + N_KV_A) * HD_A], axis=-1)
    q = rms_norm(q.reshape(B, nb, BLOCK_A, N_KV_A, G, HD_A), q_norm)
    k = rms_norm(k.reshape(B, nb, BLOCK_A, N_KV_A, HD_A), k_norm)
    v = v.reshape(B, nb, BLOCK_A, N_KV_A, HD_A)
    k_band, v_band = _band(k), _band(v)
    logits = jnp.einsum('bnqhgd,bnshd->bnhgqs', q, k_band).astype(jnp.float32) * HD_A ** -0.5
    i = jnp.arange(BLOCK_A)[:, None]
    j = jnp.arange(2 * BLOCK_A)[None, :]
    diff = i + BLOCK_A - j
    in_window = (diff >= 0) & (diff < WINDOW)
    not_pad = (jnp.arange(nb)[:, None, None] > 0) | (j[None] >= BLOCK_A)
    mask = in_window[None] & not_pad
    logits = jnp.where(mask[None, :, None, None], logits, NEG_INF)
    sink = jnp.broadcast_to(sinks.astype(jnp.float32).reshape(1, 1, N_KV_A, G, 1, 1),
                            logits.shape[:-1] + (1,))
    p = jax.nn.softmax(jnp.concatenate([logits, sink], axis=-1), axis=-1)[..., :-1]
    o = jnp.einsum('bnhgqs,bnshd->bnqhgd', p.astype(v.dtype), v_band)
    return o.reshape(B, L, N_Q_A * HD_A) @ wo


def gla_attention(h, w_in, w_gate_b, gate_bias, o_norm, wo):
    B, L, _ = h.shape
    nc = L // CHUNK
    f32 = jnp.float32
    dk_all, dv_all = N_HEADS_B * DK_B, N_HEADS_B * DV_B
    q, k, v, r, g_low = jnp.split(
        h @ w_in, [dk_all, 2 * dk_all, 2 * dk_all + dv_all, 2 * dk_all + 2 * dv_all], axis=-1)

    def heads(t, d):
        return t.astype(f32).reshape(B, nc, CHUNK, N_HEADS_B, d)

    q = heads(q, DK_B) * DK_B ** -0.5
    k = heads(k, DK_B)
    v = heads(v, DV_B)
    log_a = jax.nn.log_sigmoid((g_low @ w_gate_b + gate_bias).astype(f32)) / GATE_TAU
    bcum = jnp.cumsum(heads(log_a, DK_B), axis=2)
    b_last = bcum[:, :, -1:]
    q_dec = q * jnp.exp(bcum)
    k_inv = k * jnp.exp(-bcum)
    k_end = k * jnp.exp(b_last - bcum)
    causal = jnp.tril(jnp.ones((CHUNK, CHUNK), dtype=bool))
    a_intra = jnp.where(causal, jnp.einsum('bnthk,bnshk->bnhts', q_dec, k_inv), 0.0)
    o_intra = jnp.einsum('bnhts,bnshv->bnthv', a_intra, v)
    chunk_decay = jnp.exp(b_last[:, :, 0])

    def step(S, xs):
        qc, kc, vc, dc = xs
        o = jnp.einsum('bthk,bhkv->bthv', qc, S)
        S = S * dc[..., None] + jnp.einsum('bshk,bshv->bhkv', kc, vc)
        return S, o

    S0 = jnp.zeros((B, N_HEADS_B, DK_B, DV_B), f32)
    front = lambda t: jnp.moveaxis(t, 1, 0)
    _, o_inter = lax.scan(step, S0, (front(q_dec), front(k_end), front(v), front(chunk_decay)))
    o = o_intra + jnp.moveaxis(o_inter, 0, 1)
    o = rms_norm(o.reshape(B, L, N_HEADS_B, DV_B), o_norm).reshape(B, L, dv_all)
    o = o.astype(h.dtype) * jax.nn.silu(r)
    return o @ wo


def dsa_attention(h, w_in, q_lat_norm, kv_norm, w_uq, w_uk, q_abs_norm, w_iq,
                  idx_k_norm, idx_k_bias, w_uv, wo):
    B, L, _ = h.shape
    nb = L // C_BLOCK
    topk = min(TOPK_MAX, L // 4)
    f32 = jnp.float32
    c_q, c_kv, k_idx, w_idx = jnp.split(
        h @ w_in, [Q_RANK_C, Q_RANK_C + KV_RANK_C, Q_RANK_C + KV_RANK_C + IDX_DIM], axis=-1)
    c_q = rms_norm(c_q, q_lat_norm)
    c_kv = rms_norm(c_kv, kv_norm)
    q_nope = (c_q @ w_uq).reshape(B, L, N_HEADS_C, QK_HEAD_C)
    q_abs = rms_norm(jnp.einsum('blhd,hdc->blhc', q_nope, w_uk), q_abs_norm)
    q_idx = (c_q @ w_iq).reshape(B, L, N_IDX_HEADS, IDX_DIM)
    k_idx = layer_norm(k_idx, idx_k_norm, idx_k_bias)
    w_idx = w_idx * IDX_W_SCALE
    key_pos = jnp.arange(L, dtype=jnp.int32)

    def blocks(t):
        return jnp.moveaxis(t.reshape(B, nb, C_BLOCK, *t.shape[2:]), 1, 0)

    def attend_block(xs):
        qa, qi, wi, t0 = xs
        q_pos = t0 + jnp.arange(C_BLOCK, dtype=jnp.int32)
        rel = jax.nn.relu(jnp.einsum('bqhd,bsd->bqhs', qi, k_idx).astype(f32))
        score = jnp.einsum('bqhs,bqh->bqs', rel, wi.astype(f32))
        score = jnp.where((key_pos[None, :] <= q_pos[:, None])[None], score, NEG_INF)
        _, idx = lax.top_k(score, topk)
        kv_sel = jax.vmap(lambda c, i: c[i])(c_kv, idx)
        logits = jnp.einsum('bqhc,bqjc->bhqj', qa, kv_sel).astype(f32) * ATTN_SCALE_C
        valid = idx <= q_pos[None, :, None]
        logits = jnp.where(valid[:, None], logits, NEG_INF)
        p = jax.nn.softmax(logits, axis=-1).astype(kv_sel.dtype)
        return jnp.einsum('bhqj,bqjc->bqhc', p, kv_sel)

    starts = jnp.arange(nb, dtype=jnp.int32) * C_BLOCK
    o_lat = lax.map(attend_block, (blocks(q_abs), blocks(q_idx), blocks(w_idx), starts))
    o_lat = jnp.moveaxis(o_lat, 0, 1).reshape(B, L, N_HEADS_C, KV_RANK_C)
    o = jnp.einsum('blhc,hcv->blhv', o_lat, w_uv).reshape(B, L, N_HEADS_C * V_HEAD_C)
    return o @ wo


def setup_inputs(seed: int = 0) -> dict:
    key = jax.random.key(seed)
    keys = iter(jax.random.split(key, 128))

    def normal(shape, fan_in):
        return jax.random.normal(next(keys), shape, jnp.float32) * fan_in ** -0.5

    def gain(n):
        return 1.0 + 0.02 * jax.random.normal(next(keys), (n,), jnp.float32)

    def small(shape, scale):
        return scale * jax.random.normal(next(keys), shape, jnp.float32)

    p = {"x": jax.random.normal(next(keys), (BATCH, SEQ, D_MODEL), jnp.float32)}
    dk_all, dv_all = N_HEADS_B * DK_B, N_HEADS_B * DV_B
    for i in range(DEPTH):
        pre = f"l{i}_"
        kind = i % N_MIXERS
        p[pre + "norm_mix"] = gain(D_MODEL)
        if kind == 0:
            p[pre + "w_in"] = normal((D_MODEL, (N_Q_A + 2 * N_KV_A) * HD_A), D_MODEL)
            p[pre + "q_norm"] = gain(HD_A)
            p[pre + "k_norm"] = gain(HD_A)
            p[pre + "sinks"] = small((N_Q_A,), 0.5)
            p[pre + "wo"] = normal((N_Q_A * HD_A, D_MODEL), N_Q_A * HD_A)
        elif kind == 1:
            p[pre + "w_in"] = normal((D_MODEL, 2 * dk_all + 2 * dv_all + GATE_RANK), D_MODEL)
            p[pre + "w_gate_b"] = normal((GATE_RANK, dk_all), GATE_RANK)
            p[pre + "gate_bias"] = small((dk_all,), 0.1)
            p[pre + "o_norm"] = gain(DV_B)
            p[pre + "wo"] = normal((dv_all, D_MODEL), dv_all)
        else:
            p[pre + "w_in"] = normal((D_MODEL, Q_RANK_C + KV_RANK_C + IDX_DIM + N_IDX_HEADS), D_MODEL)
            p[pre + "q_lat_norm"] = gain(Q_RANK_C)
            p[pre + "kv_norm"] = gain(KV_RANK_C)
            p[pre + "w_uq"] = normal((Q_RANK_C, N_HEADS_C * QK_HEAD_C), Q_RANK_C)
            p[pre + "w_uk"] = normal((N_HEADS_C, QK_HEAD_C, KV_RANK_C), QK_HEAD_C)
            p[pre + "q_abs_norm"] = gain(KV_RANK_C)
            p[pre + "w_iq"] = normal((Q_RANK_C, N_IDX_HEADS * IDX_DIM), Q_RANK_C)
            p[pre + "idx_k_norm"] = gain(IDX_DIM)
            p[pre + "idx_k_bias"] = small((IDX_DIM,), 0.02)
            p[pre + "w_uv"] = normal((N_HEADS_C, KV_RANK_C, V_HEAD_C), KV_RANK_C)
            p[pre + "wo"] = normal((N_HEADS_C * V_HEAD_C, D_MODEL), N_HEADS_C * V_HEAD_C)
        p[pre + "norm_ffn"] = gain(D_MODEL)
        p[pre + "w_gate_up"] = normal((D_MODEL, 2 * D_FF), D_MODEL)
        p[pre + "w_down"] = normal((D_FF, D_MODEL), D_FF)
    return p


def reference(x,
              l0_norm_mix, l0_w_in, l0_q_norm, l0_k_norm, l0_sinks, l0_wo,
              l0_norm_ffn, l0_w_gate_up, l0_w_down,
              l1_norm_mix, l1_w_in, l1_w_gate_b, l1_gate_bias, l1_o_norm, l1_wo,
              l1_norm_ffn, l1_w_gate_up, l1_w_down,
              l2_norm_mix, l2_w_in, l2_q_lat_norm, l2_kv_norm, l2_w_uq, l2_w_uk, l2_q_abs_norm,
              l2_w_iq, l2_idx_k_norm, l2_idx_k_bias, l2_w_uv, l2_wo,
              l2_norm_ffn, l2_w_gate_up, l2_w_down,
              l3_norm_mix, l3_w_in, l3_q_norm, l3_k_norm, l3_sinks, l3_wo,
              l3_norm_ffn, l3_w_gate_up, l3_w_down):
    mixers = (swa_sink_attention, gla_attention, dsa_attention)
    layers = (
        (l0_norm_mix, (l0_w_in, l0_q_norm, l0_k_norm, l0_sinks, l0_wo),
         l0_norm_ffn, l0_w_gate_up, l0_w_down),
        (l1_norm_mix, (l1_w_in, l1_w_gate_b, l1_gate_bias, l1_o_norm, l1_wo),
         l1_norm_ffn, l1_w_gate_up, l1_w_down),
        (l2_norm_mix, (l2_w_in, l2_q_lat_norm, l2_kv_norm, l2_w_uq, l2_w_uk, l2_q_abs_norm,
                       l2_w_iq, l2_idx_k_norm, l2_idx_k_bias, l2_w_uv, l2_wo),
         l2_norm_ffn, l2_w_gate_up, l2_w_down),
        (l3_norm_mix, (l3_w_in, l3_q_norm, l3_k_norm, l3_sinks, l3_wo),
         l3_norm_ffn, l3_w_gate_up, l3_w_down),
    )
    h = x
    for i in range(DEPTH):
        norm_mix, mix_params, norm_ffn, w_gate_up, w_down = layers[i]
        h = h + mixers[i % N_MIXERS](rms_norm(h, norm_mix), *mix_params)
        h = h + swiglu(rms_norm(h, norm_ffn), w_gate_up, w_down)
    return h
```

```python
import numpy as np
from contextlib import ExitStack
import concourse.bass as bass
import concourse.mybir as mybir
from concourse.bass_utils import run_bass_kernel_spmd

F32 = mybir.dt.float32
BF16 = mybir.dt.bfloat16
AF = mybir.ActivationFunctionType
ALU = mybir.AluOpType
AX = mybir.AxisListType

D = 2048
KC = D // 128
T = 1024
NTH = T // 512
DFF = 5632
NFC = DFF // 128
EPS = 1e-6
NCORES = 8


DBG_WAITS = None


class Op:
    __slots__ = ("eng", "fn", "deps", "ticket", "is_dma", "ndma", "slot")


class Sched:
    ENGS = ("pe", "act", "dve", "pool", "sp")

    def __init__(self, nc, stack):
        self.nc = nc
        self.stack = stack
        self.ops = {e: [] for e in self.ENGS}
        self.last_w = {}
        self.readers = {}
        self.cnt = {e: 0 for e in self.ENGS}
        self.eng_sem = {}
        self.epoch = 0
        self.slot_sem = {}
        self.slot_cnt = {}
        self.nsem = 0
        self.bar_deps = []
        self.all_dma = []
        self.new_epoch()

    def _sem(self, name):
        self.nsem += 1
        return self.stack.enter_context(self.nc.semaphore(name))

    def new_epoch(self):
        self.epoch += 1
        for e in ("pe", "act", "dve", "pool"):
            self.eng_sem[e] = self._sem(f"s_{e}_{self.epoch}")
            self.cnt[e] = 0

    def add(self, eng, fn, reads=(), writes=(), dma_slot=None, ndma=1):
        op = Op()
        op.eng = eng
        op.fn = fn
        op.is_dma = dma_slot is not None
        op.ndma = ndma
        deps = list(self.bar_deps)
        for r in reads:
            w = self.last_w.get(r)
            if w is not None:
                deps.append(w)
        for w_ in writes:
            w = self.last_w.get(w_)
            if w is not None:
                deps.append(w)
            deps.extend(self.readers.get(w_, ()))
        if op.is_dma:
            if dma_slot not in self.slot_sem:
                self.slot_sem[dma_slot] = self._sem(f"d_{len(self.slot_sem)}")
                self.slot_cnt[dma_slot] = 0
            self.slot_cnt[dma_slot] += 16 * ndma
            op.ticket = (self.slot_sem[dma_slot], self.slot_cnt[dma_slot])
        else:
            self.cnt[eng] += 1
            op.ticket = (self.eng_sem[eng], self.cnt[eng])
        seen = set()
        dd = []
        for d_ in deps:
            if d_ is op or id(d_) in seen:
                continue
            seen.add(id(d_))
            if d_.eng == "pe" and eng == "pe" and not d_.is_dma:
                continue
            dd.append(d_)
        op.deps = dd
        for r in reads:
            self.readers.setdefault(r, []).append(op)
        for w_ in writes:
            self.last_w[w_] = op
            self.readers[w_] = []
        self.ops[eng].append(op)
        if op.is_dma:
            self.all_dma.append(op)
        return op

    def barrier(self):
        deps = []
        for e in self.ENGS:
            for op in reversed(self.ops[e]):
                if not op.is_dma:
                    deps.append(op)
                    break
        latest = {}
        for op in self.all_dma:
            latest[id(op.ticket[0])] = op
        deps.extend(latest.values())
        self.bar_deps = deps
        self.last_w = {}
        self.readers = {}

    def emit(self, eng, e):
        waited = {}
        for op in self.ops[eng]:
            need = {}
            for d_ in op.deps:
                s, v = d_.ticket
                k = id(s)
                if waited.get(k, 0) >= v:
                    continue
                if k not in need or need[k][1] < v:
                    need[k] = (s, v)
            for k, (s, v) in need.items():
                e.wait_ge(s, v)
                waited[k] = v
                if DBG_WAITS is not None:
                    DBG_WAITS.append((eng, self.ops[eng].index(op), getattr(s, "name", str(s)), v))
            res = op.fn(e)
            s, v = op.ticket
            if op.is_dma:
                assert isinstance(res, (list, tuple)) and len(res) == op.ndma, (len(res), op.ndma)
                for ins in res:
                    ins.then_inc(s, 16)
            else:
                if isinstance(res, (list, tuple)):
                    res = res[-1]
                res.then_inc(s, 1)

    def final_wait(self, e, ops):
        for op in ops:
            s, v = op.ticket
            e.wait_ge(s, v)


def kr(name, *idx):
    return (name,) + idx


HD_A = 64
NQ_A = 32
NKV_A = 4
TK = T + 128


class Builder:
    GW = 256
    GRP = 4

    def __init__(self):
        self.nc = bass.Bass("TRN2", target_bir_lowering=False)
        self.stack = ExitStack()
        self.S = Sched(self.nc, self.stack)
        self.out_ops = []
        self.cnt = {}

    def din(self, name, shape, dt=F32):
        return self.nc.dram_tensor(name, list(shape), dt, kind="ExternalInput").ap()

    def dout(self, name, shape, dt=F32):
        return self.nc.dram_tensor(name, list(shape), dt, kind="ExternalOutput").ap()

    def sb(self, name, shape, dt):
        return self.stack.enter_context(self.nc.sbuf_tensor(name, list(shape), dt))

    def ps(self, name, shape=(128, 512), dt=F32):
        return self.stack.enter_context(self.nc.psum_tensor(name, list(shape), dt))

    def rot(self, name, n):
        v = self.cnt.get(name, 0)
        self.cnt[name] = v + 1
        return v % n

    def alloc(self):
        self.hT = self.sb("hT", [128, KC, T], F32)
        self.xn = self.sb("xn", [128, KC, T], BF16)
        self.BIG = self.sb("BIG", [128, 16384], BF16)
        self.WA = self.sb("WA", [128, 16384], BF16)
        self.WB = self.sb("WB", [128, 16384], BF16)
        self.MISC = self.sb("MISC", [128, 6144], BF16)
        self.sb_masks = self.sb("masks", [128, 384], BF16)
        self.ones = self.sb("ones", [128, 128], BF16)
        self.bones = self.sb("bones", [128, 128], BF16)
        self.ident = self.sb("ident_sb", [128, 128], BF16)
        self.epsc = self.sb("epsc", [128, 1], F32)
        self.gcol = self.sb("gcolt", [128, KC], F32)
        self.smallf = self.sb("smallf", [128, 64], F32)
        self.psum = [self.ps(f"ps{i}") for i in range(8)]
        S = self.S
        S.add("dve", lambda e: e.memset(self.ones[:], 1.0), writes=["ones"])
        S.add("dve", lambda e: e.memset(self.epsc[:], EPS), writes=["epsc"])
        S.add("dve", lambda e: e.memset(self.bones[:], 0.0), writes=["bones"])
        S.add("dve", lambda e: e.memset(self.bones[0:64, 0:64], 1.0), reads=["bones"], writes=["bones"])
        S.add("dve", lambda e: e.memset(self.bones[64:128, 64:128], 1.0), reads=["bones"], writes=["bones"])

    def load_ident(self, ident_d):
        self.S.add("pool", lambda e: [e.dma_start(out=self.ident[:], in_=ident_d)], writes=["ident"],
                   dma_slot="ident")

    def load_h(self, xT):
        S = self.S
        src = xT.rearrange("(c p) t -> p c t", p=128)
        for q in range(4):
            sl = slice(q * 4, (q + 1) * 4)
            S.add("sp", lambda e, sl=sl: [e.dma_start(out=self.hT[:, sl, :], in_=src[:, sl, :])],
                  writes=[kr("h", c) for c in range(q * 4, q * 4 + 4)], dma_slot=("ldh", q))

    def store_h(self, outT):
        S = self.S
        dst = outT.rearrange("(c p) t -> p c t", p=128)
        for q in range(4):
            sl = slice(q * 4, (q + 1) * 4)
            op = S.add("sp", lambda e, sl=sl: [e.dma_start(out=dst[:, sl, :], in_=self.hT[:, sl, :])],
                       reads=[kr("h", c) for c in range(q * 4, q * 4 + 4)], dma_slot=("sth", q))
            self.out_ops.append(op)

    def rmsnorm(self, src, dst, n, gvec_d, tag, skey, dkey):
        S = self.S
        rstd = self.BIG[:, 0:2 * n].bitcast(F32)
        sq = [self.BIG[:, 2 * n + i * n: 2 * n + (i + 1) * n] for i in range(2)]
        S.add("sp", lambda e: [e.dma_start(out=self.gcol[:], in_=gvec_d)], writes=["gcol"], dma_slot="gcol")
        nh = (n + 511) // 512
        w = n // nh
        pss = [self.psum[6], self.psum[7]]
        for c in range(KC):
            S.add("act", lambda e, c=c: e.activation(out=sq[c % 2], in_=src[:, c, :], func=AF.Square),
                  reads=[kr(skey, c)], writes=[kr(tag + "sq", c % 2)])
            for th in range(nh):
                S.add("pe", lambda e, c=c, th=th: e.matmul(
                    pss[th][:, 0:w], lhsT=self.ones[:], rhs=sq[c % 2][:, th * w:(th + 1) * w],
                    start=(c == 0), stop=(c == KC - 1)),
                    reads=[kr(tag + "sq", c % 2), "ones"], writes=[kr("ps", 6 + th)])
        for th in range(nh):
            sl = slice(th * w, (th + 1) * w)
            S.add("act", lambda e, th=th, sl=sl: e.activation(
                out=rstd[:, sl], in_=pss[th][:, 0:w], func=AF.Sqrt, bias=self.epsc[:], scale=1.0 / D),
                reads=[kr("ps", 6 + th), "epsc"], writes=[kr(tag + "rstd", th)])
            S.add("dve", lambda e, sl=sl: e.reciprocal(out=rstd[:, sl], in_=rstd[:, sl]),
                  reads=[kr(tag + "rstd", th)], writes=[kr(tag + "rstd", th)])
        for c in range(KC):
            S.add("dve", lambda e, c=c: e.scalar_tensor_tensor(
                out=dst[:, c, :], in0=src[:, c, :], scalar=self.gcol[:, c:c + 1], in1=rstd,
                op0=ALU.mult, op1=ALU.mult),
                reads=[kr(skey, c), "gcol"] + [kr(tag + "rstd", th) for th in range(nh)], writes=[kr(dkey, c)])

    def ffn(self, gvec, wgu_d, wd_d):
        S = self.S
        GW, GRP = self.GW, self.GRP
        G = GW // 128
        hT, xn = self.hT, self.xn
        S.barrier()
        self.rmsnorm(hT, xn, T, gvec, "f", "h", "xn")
        S.barrier()
        wgu = [self.WA[:, i * 8192:(i + 1) * 8192].rearrange("p (c f) -> p c f", c=KC) for i in range(2)]
        wd = [self.WB[:, i * 8192:(i + 1) * 8192].rearrange("p (c f) -> p c f", c=GRP) for i in range(2)]
        actb = [self.BIG[:, i * 4096:(i + 1) * 4096].rearrange("p (c f) -> p c f", c=GRP) for i in range(2)]
        sg = [self.BIG[:, 8192 + i * 1024: 8192 + (i + 1) * 1024].bitcast(F32) for i in range(2)]
        nslab = NFC // G
        ngrp = NFC // GRP
        pg = [self.psum[0], self.psum[1]]
        pu = [self.psum[2], self.psum[3]]
        pd = [self.psum[4], self.psum[5]]

        def load_wgu(s):
            b = self.rot("wgu", 2)
            S.add("pool", lambda e, s=s, b=b: [e.dma_start(
                out=self.WA[:, b * 8192:(b + 1) * 8192], in_=wgu_d[s])],
                writes=[kr("wgu", b)], dma_slot=("wgu", b))
            return b

        def load_wd(g):
            b = self.rot("wd", 2)
            S.add("pool", lambda e, g=g, b=b: [e.dma_start(
                out=self.WB[:, b * 8192:(b + 1) * 8192], in_=wd_d[g])],
                writes=[kr("wd", b)], dma_slot=("wd", b))
            return b

        slab_buf = {}
        wd_buf = {}
        slab_buf[0] = load_wgu(0)
        slab_buf[1] = load_wgu(1)
        wd_buf[0] = load_wd(0)
        for g in range(ngrp):
            ab = self.rot("act", 2)
            for j in range(GRP):
                fc = g * GRP + j
                s = fc // G
                jj = fc % G
                wb = slab_buf[s]
                for th in range(NTH):
                    tsl = slice(th * 512, (th + 1) * 512)

                    def mm_g(e, wb=wb, jj=jj, th=th, tsl=tsl):
                        r = None
                        for c in range(KC):
                            r = e.matmul(pg[th][:], lhsT=wgu[wb][:, c, jj * 128:(jj + 1) * 128],
                                         rhs=xn[:, c, tsl], start=(c == 0), stop=(c == KC - 1))
                        return r

                    def mm_u(e, wb=wb, jj=jj, th=th, tsl=tsl):
                        r = None
                        for c in range(KC):
                            r = e.matmul(pu[th][:], lhsT=wgu[wb][:, c, GW + jj * 128:GW + (jj + 1) * 128],
                                         rhs=xn[:, c, tsl], start=(c == 0), stop=(c == KC - 1))
                        return r
                    xr = [kr("xn", c) for c in range(KC)]
                    S.add("pe", mm_g, reads=[kr("wgu", wb)] + xr, writes=[kr("ps", th)])
                    S.add("pe", mm_u, reads=[kr("wgu", wb)] + xr, writes=[kr("ps", 2 + th)])
                    S.add("act", lambda e, th=th: e.activation(out=sg[th], in_=pg[th][:], func=AF.Silu),
                          reads=[kr("ps", th)], writes=[kr("sg", th)])
                    S.add("dve", lambda e, th=th, ab=ab, j=j, tsl=tsl: e.tensor_tensor(
                        out=actb[ab][:, j, tsl], in0=sg[th], in1=pu[th][:], op=ALU.mult),
                        reads=[kr("sg", th), kr("ps", 2 + th)], writes=[kr("act", ab, j, th)])
                if jj == G - 1 and s + 2 < nslab:
                    slab_buf[s + 2] = load_wgu(s + 2)
            if g + 1 < ngrp:
                wd_buf[g + 1] = load_wd(g + 1)
            db = wd_buf[g]
            for dc in range(KC):
                for th in range(NTH):
                    tsl = slice(th * 512, (th + 1) * 512)
                    pb = self.rot("pd", 2)

                    def mm_d(e, db=db, dc=dc, tsl=tsl, pb=pb, ab=ab):
                        r = None
                        for j in range(GRP):
                            r = e.matmul(pd[pb][:], lhsT=wd[db][:, j, dc * 128:(dc + 1) * 128],
                                         rhs=actb[ab][:, j, tsl], start=(j == 0), stop=(j == GRP - 1))
                        return r
                    S.add("pe", mm_d, reads=[kr("wd", db)] + [kr("act", ab, j, th) for j in range(GRP)],
                          writes=[kr("ps", 4 + pb)])
                    S.add("dve", lambda e, dc=dc, tsl=tsl, pb=pb: e.tensor_tensor(
                        out=hT[:, dc, tsl], in0=hT[:, dc, tsl], in1=pd[pb][:], op=ALU.add),
                        reads=[kr("ps", 4 + pb), kr("h", dc)], writes=[kr("h", dc)])

    def proj_headnorm(self, wv, srcs, dsts, gain_col, wkey):
        S = self.S
        for (rhs_fn, n, rkeys), (dst, wkeys) in zip(srcs, dsts):
            pb = self.rot("phn", 2)
            praw = self.psum[pb]
            pssq = self.psum[2 + pb]
            sqs = self.MISC[:, pb * 512: pb * 512 + n]
            rs = self.MISC[:, 1024 + pb * 1024: 1024 + pb * 1024 + 2 * n].bitcast(F32)

            def mm(e, rhs_fn=rhs_fn, n=n, praw=praw):
                r = None
                for c in range(KC):
                    r = e.matmul(praw[:, 0:n], lhsT=wv[:, c, :], rhs=rhs_fn(c), start=(c == 0), stop=(c == KC - 1))
                return r
            S.add("pe", mm, reads=[wkey] + list(rkeys), writes=[kr("ps", pb)])
            S.add("act", lambda e, n=n, praw=praw, sqs=sqs: e.activation(out=sqs, in_=praw[:, 0:n], func=AF.Square),
                  reads=[kr("ps", pb)], writes=[kr("hsq", pb)])
            S.add("pe", lambda e, n=n, pssq=pssq, sqs=sqs: e.matmul(pssq[:, 0:n], lhsT=self.bones[:], rhs=sqs,
                                                                     start=True, stop=True),
                  reads=[kr("hsq", pb), "bones"], writes=[kr("ps", 2 + pb)])
            S.add("act", lambda e, n=n, pssq=pssq, rs=rs: e.activation(
                out=rs, in_=pssq[:, 0:n], func=AF.Sqrt, bias=self.epsc[:], scale=1.0 / HD_A),
                reads=[kr("ps", 2 + pb), "epsc"], writes=[kr("hrs", pb)])
            S.add("dve", lambda e, rs=rs: e.reciprocal(out=rs, in_=rs), reads=[kr("hrs", pb)], writes=[kr("hrs", pb)])
            S.add("dve", lambda e, n=n, praw=praw, rs=rs, dst=dst: e.scalar_tensor_tensor(
                out=dst, in0=praw[:, 0:n], scalar=gain_col, in1=rs, op0=ALU.mult, op1=ALU.mult),
                reads=[kr("ps", pb), kr("hrs", pb), "gains"], writes=list(wkeys))

    def swa(self, p):
        S = self.S
        hT, xn = self.hT, self.xn
        S.barrier()
        halo_h = self.MISC[:, 0:4096].bitcast(F32).rearrange("p (c t) -> p c t", c=KC)
        kT = self.WB[:, 0:8 * TK].rearrange("p (c t) -> p c t", c=8)
        o_v = 8 * TK
        vext = self.WB[:, o_v:o_v + 9 * 4 * 80].rearrange("p (b h d) -> p b h d", b=9, h=4)
        o_x = o_v + 9 * 4 * 80
        xnh = self.WB[:, o_x:o_x + KC * 128].rearrange("p (c t) -> p c t", c=KC)
        assert o_x + KC * 128 <= 16384
        halo_src = p["haloT"].rearrange("(c p) t -> p c t", p=128)
        S.add("sp", lambda e: [e.dma_start(out=halo_h, in_=halo_src)], writes=[kr("hh", c) for c in range(KC)],
              dma_slot="halo")
        gains = self.smallf[:, 0:2]
        S.add("sp", lambda e: [e.dma_start(out=gains, in_=p["qkn"])], writes=["gains"], dma_slot="gains")
        sinks = self.smallf[:, 8:40]
        S.add("sp", lambda e: [e.dma_start(out=sinks, in_=p["sinks"])], writes=["sinks"], dma_slot="sinks")
        S.add("act", lambda e: e.activation(out=sinks, in_=sinks, func=AF.Exp), reads=["sinks"], writes=["sinks"])
        S.add("pool", lambda e: [e.dma_start(out=self.sb_masks[:], in_=p["masks"])], writes=["masks"], dma_slot="masks")
        masks = self.sb_masks[:, :].rearrange("p (m q) -> p m q", m=3)
        self.osb_full = [self.MISC[:, 2048:4096], self.MISC[:, 4096:6144]]
        self.rmsnorm(halo_h, xnh, 128, p["g_mix"], "hn", "hh", "xnh")
        S.barrier()
        self.rmsnorm(hT, xn, T, p["g_mix"], "m", "h", "xn")
        S.barrier()
        qT = self.BIG[:, :].rearrange("p (c t) -> p c t", c=KC)
        wa = [self.WA[:, i * 8192:(i + 1) * 8192].rearrange("p (c f) -> p c f", c=KC) for i in range(2)]

        def load_wa(src, ncol=512):
            b = self.rot("swa_wa", 2)
            S.add("pool", lambda e, b=b: [e.dma_start(out=self.WA[:, b * 8192: b * 8192 + KC * ncol], in_=src)],
                  writes=[kr("wa", b)], dma_slot=("wa", b))
            return b
        xr = [kr("xn", c) for c in range(KC)]
        xhr = [kr("xnh", c) for c in range(KC)]
        slabs = [("q", i) for i in range(4)] + [("k", i) for i in range(2)]
        bufs = {}
        bufs[0] = load_wa(p["wq"][0])
        for si, (kind, i) in enumerate(slabs):
            if si + 1 < len(slabs):
                k2, i2 = slabs[si + 1]
                bufs[si + 1] = load_wa(p["wq"][i2] if k2 == "q" else p["wk"][i2])
            b = bufs[si]
            for j in range(4):
                wv = wa[b][:, :, j * 128:(j + 1) * 128]
                if kind == "q":
                    ch = i * 4 + j
                    srcs = [(lambda c, th=th: xn[:, c, th * 512:(th + 1) * 512], 512, xr) for th in range(NTH)]
                    dsts = [(qT[:, ch, th * 512:(th + 1) * 512], [kr("qT", ch, th)]) for th in range(NTH)]
                    self.proj_headnorm(wv, srcs, dsts, gains[:, 0:1], kr("wa", b))
                else:
                    ch = i * 4 + j
                    srcs = [(lambda c: xnh[:, c, :], 128, xhr)] + \
                           [(lambda c, th=th: xn[:, c, th * 512:(th + 1) * 512], 512, xr) for th in range(NTH)]
                    dsts = [(kT[:, ch, 0:128], [kr("kT", ch, 0)])] + \
                           [(kT[:, ch, 128 + th * 512:128 + (th + 1) * 512], [kr("kT", ch, 1 + th)]) for th in range(NTH)]
                    self.proj_headnorm(wv, srcs, dsts, gains[:, 1:2], kr("wa", b))
        bv = load_wa(p["wv"], 256)
        wvv = self.WA[:, bv * 8192: bv * 8192 + KC * 256].rearrange("p (c f) -> p c f", c=KC)
        S.add("pool", lambda e: e.memset(vext[:, :, :, 64:65], 1.0), writes=["vones"])
        for blk in range(9):
            pb = 4 + self.rot("pv", 2)

            def mmv(e, blk=blk, pb=pb):
                r = None
                for c in range(KC):
                    lhs = xnh[:, c, :] if blk == 0 else xn[:, c, (blk - 1) * 128: blk * 128]
                    r = e.matmul(self.psum[pb][:, 0:256], lhsT=lhs, rhs=wvv[:, c, :], start=(c == 0), stop=(c == KC - 1))
                return r
            S.add("pe", mmv, reads=[kr("wa", bv)] + (xhr if blk == 0 else xr), writes=[kr("ps", pb)])
            S.add("act", lambda e, blk=blk, pb=pb: e.activation(
                out=vext[:, blk, :, 0:64], in_=self.psum[pb][:, 0:256].rearrange("p (h d) -> p h d", h=4),
                func=AF.Copy), reads=[kr("ps", pb)], writes=[kr("v", blk)])
        S.barrier()
        OT = xn
        et = [self.MISC[:, i * 512:(i + 1) * 512] for i in range(4)]
        den = self.smallf[:, 40:48]
        wo_bufs = {}
        wo_v = [self.WA[:, i * 8192:(i + 1) * 8192].rearrange("p (j c f) -> p j c f", j=4, c=KC) for i in range(2)]

        def load_wo(dg):
            b = self.rot("swa_wo", 2)
            S.add("pool", lambda e, b=b, dg=dg: [e.dma_start(out=self.WA[:, b * 8192:(b + 1) * 8192], in_=p["wo"][dg])],
                  writes=[kr("wo", b)], dma_slot=("wob", b))
            return b
        wo_bufs[0] = load_wo(0)
        wo_bufs[1] = load_wo(1)
        tp_bf = [self.psum[6][:].bitcast(BF16), self.psum[7][:].bitcast(BF16)]
        for n in range(8):
            qs = slice(n * 128, (n + 1) * 128)
            for hk in range(4):
                for half in range(2):
                    grp = hk * 2 + half
                    sp_i = self.rot("sps", 2)
                    ps_prev = self.psum[sp_i * 2]
                    ps_cur = self.psum[sp_i * 2 + 1]
                    e_prev = et[sp_i * 2]
                    e_cur = et[sp_i * 2 + 1]
                    for kb, pst in ((0, ps_prev), (1, ps_cur)):
                        ks = slice((n + kb) * 128, (n + kb + 1) * 128)

                        def mms(e, hk=hk, half=half, ks=ks, pst=pst, qs=qs):
                            r = None
                            for j in range(4):
                                h = hk * 8 + half * 4 + j
                                r = e.matmul(pst[:, j * 128:(j + 1) * 128], lhsT=kT[:, hk * 2 + (h % 2), ks],
                                             rhs=qT[:, h // 2, qs], start=True, stop=True)
                            return r
                        S.add("pe", mms, reads=["kTall", "qTall"], writes=[kr("ps", sp_i * 2 + kb)])
                    midx_prev = 0 if n == 0 else 1
                    for kb, pst, ee, mi in ((0, ps_prev, e_prev, midx_prev), (1, ps_cur, e_cur, 2)):
                        S.add("act", lambda e, pst=pst, ee=ee: e.activation(out=ee, in_=pst[:], func=AF.Exp, scale=HD_A ** -0.5),
                              reads=[kr("ps", sp_i * 2 + kb)], writes=[kr("et", sp_i * 2 + kb)])
                        S.add("pool", lambda e, ee=ee, mi=mi: e.tensor_tensor(
                            out=ee.rearrange("p (j q) -> p j q", j=4), in0=ee.rearrange("p (j q) -> p j q", j=4),
                            in1=masks[:, mi:mi + 1, :].to_broadcast([128, 4, 128]), op=ALU.mult),
                            reads=[kr("et", sp_i * 2 + kb), "masks"], writes=[kr("et", sp_i * 2 + kb)])
                    po_i = 4 + self.rot("spo", 2)
                    po = self.psum[po_i]

                    def mmo(e, n=n, hk=hk, po=po, e_prev=e_prev, e_cur=e_cur):
                        r = None
                        for j in range(4):
                            e.matmul(po[:, j * 128:j * 128 + 65], lhsT=e_prev[:, j * 128:(j + 1) * 128],
                                     rhs=vext[:, n, hk, 0:65], start=True, stop=False)
                            r = e.matmul(po[:, j * 128:j * 128 + 65], lhsT=e_cur[:, j * 128:(j + 1) * 128],
                                         rhs=vext[:, n + 1, hk, 0:65], start=False, stop=True)
                        return r
                    S.add("pe", mmo, reads=[kr("et", sp_i * 2), kr("et", sp_i * 2 + 1), "vall"], writes=[kr("ps", po_i)])
                    h0 = hk * 8 + half * 4
                    pov = po[:, :].rearrange("p (j d) -> p j d", j=4)
                    dn = den[:, 0:4] if (grp % 2 == 0) else den[:, 4:8]
                    dk = kr("den", grp % 2)
                    S.add("dve", lambda e, pov=pov, dn=dn, h0=h0: e.tensor_tensor(
                        out=dn.rearrange("p (j o) -> p j o", o=1), in0=pov[:, :, 64:65],
                        in1=sinks[:, h0:h0 + 4].rearrange("p (j o) -> p j o", o=1), op=ALU.add),
                        reads=[kr("ps", po_i), "sinks"], writes=[dk])
                    S.add("dve", lambda e, dn=dn: e.reciprocal(out=dn, in_=dn), reads=[dk], writes=[dk])
                    ob = self.osb_full[n % 2][:, h0 * 64:(h0 + 4) * 64].rearrange("p (j d) -> p j d", j=4)
                    S.add("dve", lambda e, pov=pov, dn=dn, ob=ob: e.tensor_tensor(
                        out=ob, in0=pov[:, :, 0:64], in1=dn.rearrange("p (j o) -> p j o", o=1).to_broadcast([128, 4, 64]),
                        op=ALU.mult),
                        reads=[kr("ps", po_i), dk], writes=[kr("osb", n % 2, grp)])
            for c4 in range(4):
                tp_i = self.rot("tp", 2)
                tpv = tp_bf[tp_i]

                def mmt(e, c4=c4, tpv=tpv, n=n):
                    r = None
                    for j in range(4):
                        c = c4 * 4 + j
                        r = e.transpose(tpv[:, j * 128:(j + 1) * 128], self.osb_full[n % 2][:, c * 128:(c + 1) * 128],
                                        self.ident[:])
                    return r
                S.add("pe", mmt, reads=[kr("osb", n % 2, g_) for g_ in range(8)] + ["ident"], writes=[kr("ps", 6 + tp_i)])
                S.add("act", lambda e, c4=c4, tpv=tpv, qs=qs: e.activation(
                    out=OT[:, c4 * 4:(c4 + 1) * 4, qs], in_=tpv[:, 0:512].rearrange("p (j q) -> p j q", j=4), func=AF.Copy),
                    reads=[kr("ps", 6 + tp_i)], writes=[kr("OT", n, c4)])
        S.barrier()
        for dg in range(4):
            b = wo_bufs[dg]
            for j in range(4):
                dc = dg * 4 + j
                for th in range(NTH):
                    tsl = slice(th * 512, (th + 1) * 512)
                    pb = self.rot("pwo", 2)

                    def mmw(e, b=b, j=j, tsl=tsl, pb=pb):
                        r = None
                        for c in range(KC):
                            r = e.matmul(self.psum[pb][:], lhsT=wo_v[b][:, j, c, :], rhs=OT[:, c, tsl],
                                         start=(c == 0), stop=(c == KC - 1))
                        return r
                    S.add("pe", mmw, reads=[kr("wo", b)], writes=[kr("ps", pb)])
                    S.add("dve", lambda e, dc=dc, tsl=tsl, pb=pb: e.tensor_tensor(
                        out=hT[:, dc, tsl], in0=hT[:, dc, tsl], in1=self.psum[pb][:], op=ALU.add),
                        reads=[kr("ps", pb), kr("h", dc)], writes=[kr("h", dc)])
            if dg + 2 < 4:
                wo_bufs[dg + 2] = load_wo(dg + 2)

    def gla(self, p, with_output):
        S = self.S
        hT, xn = self.hT, self.xn
        S.barrier()
        self.rmsnorm(hT, xn, T, p["g_mix"], "g", "h", "xn")
        S.barrier()
        BIG, MISC = self.BIG, self.MISC
        S32 = BIG[:, 13568:15616].bitcast(F32).rearrange("p (c f) -> p c f", c=2)
        Sbf = BIG[:, 8192:9216].rearrange("p (c f) -> p c f", c=2)
        OTh = BIG[:, 3072:7168].rearrange("p (c t) -> p c t", c=4)
        glT = BIG[0:64, 9216:10240]
        wgb = BIG[0:64, 10240:11264]
        wgl = BIG[:, 11264:11520].rearrange("p (c f) -> p c f", c=KC)
        gain = BIG[:, 11520:12544].bitcast(F32)
        Ltmp = BIG[:, 12544:13568].bitcast(F32)
        sp_tok = MISC[:, 0:256]
        eb = MISC[:, 256:768].bitcast(F32)
        enb = MISC[:, 768:1280].bitcast(F32)
        expf = MISC[:, 1280:1792].bitcast(F32)
        qd = MISC[:, 1792:2048].rearrange("p (c t) -> p c t", c=2)
        ki = MISC[:, 2048:2304].rearrange("p (c t) -> p c t", c=2)
        kend = MISC[:, 2304:2560]
        vsb = MISC[:, 2560:3072]
        am = MISC[:, 3072:3200]
        e1 = MISC[:, 3200:3712].bitcast(F32)
        on = MISC[:, 3712:4736].bitcast(F32)
        sr = MISC[:, 4736:5248]
        og = MISC[:, 5248:5760]
        dcols = self.smallf[:, 0:8]
        dprev = self.smallf[:, 8:32].rearrange("p (s c) -> p s c", s=3)
        ssq = self.smallf[:, 32:33]
        rstd = self.smallf[:, 33:34]
        masks = self.sb_masks[:, :].rearrange("p (m q) -> p m q", m=3)
        ps = self.psum
        import os
        dbg = int(os.environ.get("GLA_DBG", "0"))
        if dbg == 10:
            return
        S.add("pool", lambda e: [e.dma_start(out=self.sb_masks[:], in_=p["masks"])], writes=["masks"], dma_slot="masks")
        S.add("pool", lambda e: [e.dma_start(out=wgb, in_=p["wgb"])], writes=["wgb"], dma_slot="wgb")
        S.add("pool", lambda e: [e.dma_start(out=BIG[:, 11264:11520], in_=p["wgl"])], writes=["wgl"], dma_slot="wgl")
        S.add("sp", lambda e: [e.dma_start(out=gain, in_=p["onorm"])], writes=["gain"], dma_slot="gain")
        S.add("sp", lambda e: [e.dma_start(out=self.smallf[:, 8:32], in_=p["dprev"].rearrange("s p c -> p s c"))],
              writes=["dprev"], dma_slot="dprev") if False else None
        for s_ in range(3):
            S.add("sp", lambda e, s_=s_: [e.dma_start(out=dprev[:, s_, :], in_=p["dprev"][s_])], writes=[kr("dprev", s_)],
                  dma_slot=("dprev", s_))
        if dbg == 11:
            S.barrier()
            S.add("dve", lambda e: e.memset(dcols, 1.0), writes=["dcols"])
            return
        S.add("dve", lambda e: e.memset(glT, 0.0), writes=["glT"])
        S.add("dve", lambda e: e.memset(glT[32:64, :], 1.0), reads=["glT"], writes=["glT"])
        xr = [kr("xn", c) for c in range(KC)]
        for th in range(NTH):
            tsl = slice(th * 512, (th + 1) * 512)

            def mmgl(e, tsl=tsl, th=th):
                r = None
                for c in range(KC):
                    r = e.matmul(ps[th][0:16, :], lhsT=wgl[:, c, :], rhs=xn[:, c, tsl], start=(c == 0), stop=(c == KC - 1))
                return r
            S.add("pe", mmgl, reads=["wgl"] + xr, writes=[kr("ps", th)])
            S.add("act", lambda e, tsl=tsl, th=th: e.activation(out=glT[0:16, tsl], in_=ps[th][0:16, :], func=AF.Copy),
                  reads=[kr("ps", th), "glT"], writes=["glT"])
        S.add("dve", lambda e: e.memset(dcols, 1.0), writes=["dcols"])
        S.barrier()
        import os
        dbg = int(os.environ.get("GLA_DBG", "0"))
        if dbg == 1:
            return
        slabv = [self.WA[:, 0:8192], self.WA[:, 8192:16384], self.WB[:, 0:8192], self.WB[:, 8192:16384]]
        nbuf = 4

        def load_slab(src):
            b = self.rot("gla_slab", nbuf)
            S.add("pool", lambda e, b=b: [e.dma_start(out=slabv[b], in_=src)], writes=[kr("slab", b)], dma_slot=("gslab", b))
            return b
        for hd in range(4):
            bA = load_slab(p["wA"][hd])
            bB = load_slab(p["wB"][hd])
            if with_output:
                bC = load_slab(p["wC"][hd])
                bO = load_slab(p["woh"][hd])
            wA = slabv[bA].rearrange("p (c f) -> p c f", c=KC)
            wBv = slabv[bB].rearrange("p (c f) -> p c f", c=KC)
            if with_output:
                wCv = slabv[bC].rearrange("p (c f) -> p c f", c=KC)
                wOv = slabv[bO].rearrange("p (c f) -> p c f", c=4)
            for dkc in range(2):
                idx = hd * 2 + dkc
                if with_output:
                    S.add("sp", lambda e, idx=idx, dkc=dkc: [e.dma_start(out=S32[:, dkc, :], in_=p["sprev"][0, idx])],
                          writes=[kr("S32", dkc)], dma_slot=("s32", dkc))
                    for s_ in (1, 2):
                        S.add("sp", lambda e, idx=idx, s_=s_: [e.dma_start(out=Ltmp, in_=p["sprev"][s_, idx])],
                              writes=["Ltmp"], dma_slot="ltmp")
                        S.add("dve", lambda e, idx=idx, s_=s_, dkc=dkc: e.scalar_tensor_tensor(
                            out=S32[:, dkc, :], in0=S32[:, dkc, :], scalar=dprev[:, s_, idx:idx + 1], in1=Ltmp,
                            op0=ALU.mult, op1=ALU.add),
                            reads=[kr("S32", dkc), "Ltmp", kr("dprev", s_)], writes=[kr("S32", dkc)])
                elif dbg == 36:
                    altm = BIG[:, 0:2048].bitcast(F32).rearrange("p (c f) -> p c f", c=2)
                    S.add("dve", lambda e, dkc=dkc, altm=altm: e.memset(altm[:, dkc, :], 0.0), writes=[kr("alt", dkc)])
                    S.add("dve", lambda e, dkc=dkc: e.memset(S32[:, dkc, :], 0.0), writes=[kr("S32", dkc)])
                elif dbg == 35:
                    S.add("dve", lambda e, dkc=dkc: e.tensor_copy(out=S32[:, dkc, :], in_=gain), reads=["gain"], writes=[kr("S32", dkc)])
                else:
                    S.add("dve", lambda e, dkc=dkc: e.memset(S32[:, dkc, :], 0.0), writes=[kr("S32", dkc)])
                if with_output:
                    S.add("act", lambda e, dkc=dkc: e.activation(out=Sbf[:, dkc, :], in_=S32[:, dkc, :], func=AF.Copy),
                          reads=[kr("S32", dkc)], writes=[kr("Sbf", dkc)])
            for blk in range(8):
                if dbg == 2 and (hd, blk) != (0, 0):
                    continue
                bs = slice(blk * 128, (blk + 1) * 128)
                if with_output:
                    def mmqk(e, bs=bs, wA=wA):
                        r = None
                        for j in range(4):
                            for c in range(KC):
                                r = e.matmul(ps[0][:, j * 128:(j + 1) * 128], lhsT=wA[:, c, j * 128:(j + 1) * 128],
                                             rhs=xn[:, c, bs], start=(c == 0), stop=(c == KC - 1))
                        return r
                    S.add("pe", mmqk, reads=[kr("slab", bA)] + xr, writes=[kr("ps", 0)])

                def mmk(e, bs=bs, wA=wA):
                    r = None
                    for c in range(KC):
                        r = e.matmul(ps[1][:, 0:256], lhsT=xn[:, c, bs], rhs=wA[:, c, 256:512], start=(c == 0), stop=(c == KC - 1))
                    return r
                S.add("pe", mmk, reads=[kr("slab", bA)] + xr, writes=[kr("ps", 1)])
                S.add("pe", lambda e, bs=bs, hd=hd: e.matmul(ps[1][:, 256:512], lhsT=glT[0:33, bs],
                                                             rhs=wgb[0:33, hd * 256:(hd + 1) * 256], start=True, stop=True),
                      reads=["glT", "wgb"], writes=[kr("ps", 1)])

                def mmv(e, bs=bs, wBv=wBv):
                    r = None
                    for c in range(KC):
                        r = e.matmul(ps[2][:], lhsT=xn[:, c, bs], rhs=wBv[:, c, :], start=(c == 0), stop=(c == KC - 1))
                    return r
                S.add("pe", mmv, reads=[kr("slab", bB)] + xr, writes=[kr("ps", 2)])
                S.add("act", lambda e: e.activation(out=vsb, in_=ps[2][:], func=AF.Copy), reads=[kr("ps", 2)], writes=["vsb"])
                if dbg == 20:
                    break
                S.add("act", lambda e: e.activation(out=e1, in_=ps[1][:, 256:512], func=AF.Exp, scale=-1.0),
                      reads=[kr("ps", 1)], writes=["e1"])
                S.add("act", lambda e: e.activation(out=sp_tok, in_=e1, func=AF.Ln, bias=1.0), reads=["e1"], writes=["sp"])
                if dbg == 21:
                    break

                def mmb(e):
                    r = None
                    for dkc in range(2):
                        r = e.matmul(ps[4][:, dkc * 128:(dkc + 1) * 128], lhsT=sp_tok[:, dkc * 128:(dkc + 1) * 128],
                                     rhs=masks[:, 0, :], start=True, stop=True)
                    return r
                S.add("pe", mmb, reads=["sp", "masks"], writes=[kr("ps", 4)])
                S.add("pe", lambda e: e.matmul(ps[4][:, 256:512], lhsT=masks[:, 1, :], rhs=sp_tok, start=True, stop=True),
                      reads=["sp", "masks"], writes=[kr("ps", 4)])
                S.add("act", lambda e: e.activation(out=eb, in_=ps[4][:, 0:256], func=AF.Exp), reads=[kr("ps", 4)], writes=["eb"])
                S.add("act", lambda e: e.activation(out=expf, in_=ps[4][:, 256:512], func=AF.Exp), reads=[kr("ps", 4)],
                      writes=["expf"])
                S.add("dve", lambda e: e.tensor_tensor(out=kend, in0=ps[1][:, 0:256], in1=expf, op=ALU.mult),
                      reads=[kr("ps", 1), "expf"], writes=["kend"])
                if with_output:
                    S.add("act", lambda e: e.activation(out=enb, in_=ps[4][:, 0:256], func=AF.Exp, scale=-1.0),
                          reads=[kr("ps", 4)], writes=["enb"])
                    S.add("dve", lambda e: e.scalar_tensor_tensor(
                        out=qd, in0=ps[0][:, 0:256].rearrange("p (c t) -> p c t", c=2), scalar=256.0 ** -0.5,
                        in1=eb.rearrange("p (c t) -> p c t", c=2), op0=ALU.mult, op1=ALU.mult),
                        reads=[kr("ps", 0), "eb"], writes=["qd"])
                    S.add("dve", lambda e: e.tensor_tensor(
                        out=ki, in0=ps[0][:, 256:512].rearrange("p (c t) -> p c t", c=2),
                        in1=enb.rearrange("p (c t) -> p c t", c=2), op=ALU.mult),
                        reads=[kr("ps", 0), "enb"], writes=["ki"])

                    def mma(e):
                        r = None
                        for dkc in range(2):
                            r = e.matmul(ps[5][:, 0:128], lhsT=ki[:, dkc, :], rhs=qd[:, dkc, :], start=(dkc == 0), stop=(dkc == 1))
                        return r
                    S.add("pe", mma, reads=["ki", "qd"], writes=[kr("ps", 5)])
                    S.add("dve", lambda e: e.tensor_tensor(out=am, in0=ps[5][:, 0:128], in1=masks[:, 2, :], op=ALU.mult),
                          reads=[kr("ps", 5), "masks"], writes=["am"])

                    def mmo(e):
                        e.matmul(ps[6][:], lhsT=am, rhs=vsb, start=True, stop=False)
                        e.matmul(ps[6][:], lhsT=qd[:, 0, :], rhs=Sbf[:, 0, :], start=False, stop=False)
                        return e.matmul(ps[6][:], lhsT=qd[:, 1, :], rhs=Sbf[:, 1, :], start=False, stop=True)
                    S.add("pe", mmo, reads=["am", "vsb", "qd", kr("Sbf", 0), kr("Sbf", 1)], writes=[kr("ps", 6)])

                    def mmr(e, bs=bs, wCv=wCv):
                        r = None
                        for c in range(KC):
                            r = e.matmul(ps[3][:], lhsT=xn[:, c, bs], rhs=wCv[:, c, :], start=(c == 0), stop=(c == KC - 1))
                        return r
                    S.add("pe", mmr, reads=[kr("slab", bC)] + xr, writes=[kr("ps", 3)])
                if dbg == 22:
                    break
                for dkc in range(2):
                    S.add("pe", lambda e, dkc=dkc: e.matmul(ps[7][:], lhsT=kend[:, dkc * 128:(dkc + 1) * 128], rhs=vsb,
                                                            start=True, stop=True),
                          reads=["kend", "vsb"], writes=[kr("ps", 7)])
                    if dbg == 24:
                        continue
                    dec = self.smallf[:, 34 + dkc:35 + dkc]
                    S.add("dve", lambda e, dkc=dkc, dec=dec: e.tensor_copy(out=dec, in_=eb[:, dkc * 128 + 127: dkc * 128 + 128]),
                          reads=["eb"], writes=[kr("dec", dkc)])
                    if dbg == 25:
                        continue
                    if dbg == 26:
                        S.add("dve", lambda e, dkc=dkc, dec=dec: e.tensor_tensor(
                            out=S32[:, dkc, :], in0=S32[:, dkc, :], in1=ps[7][:], op=ALU.add),
                            reads=[kr("S32", dkc), kr("ps", 7)], writes=[kr("S32", dkc)])
                        continue
                    if dbg == 28:
                        S.add("dve", lambda e, dkc=dkc, dec=dec: e.tensor_copy(out=on, in_=ps[7][:]),
                            reads=[kr("ps", 7)], writes=["on"])
                        continue
                    if dbg == 32:
                        S.add("act", lambda e, dkc=dkc, dec=dec: e.activation(out=S32[:, dkc, :], in_=gain, func=AF.Copy),
                            reads=[kr("S32", dkc), "gain"], writes=[kr("S32", dkc)])
                        continue
                    if dbg in (33, 36):
                        alt = BIG[:, 0:2048].bitcast(F32).rearrange("p (c f) -> p c f", c=2)
                        S.add("dve", lambda e, dkc=dkc, alt=alt: e.tensor_copy(out=alt[:, dkc, :], in_=gain),
                            reads=[kr("alt", dkc), "gain"], writes=[kr("alt", dkc)])
                        continue
                    if dbg == 34:
                        S.add("dve", lambda e, dkc=dkc: e.tensor_copy(out=S32[:, dkc, :], in_=gain),
                            reads=["gain"], writes=[kr("S32x", dkc)])
                        continue
                    if dbg in (31, 35):
                        S.add("dve", lambda e, dkc=dkc, dec=dec: e.tensor_copy(out=S32[:, dkc, :], in_=gain),
                            reads=[kr("S32", dkc), "gain"], writes=[kr("S32", dkc)])
                        continue
                    if dbg in (29, 30):
                        S.add("dve", lambda e, dkc=dkc, dec=dec: e.tensor_copy(out=S32[:, dkc, :], in_=on),
                            reads=[kr("S32", dkc), "on"], writes=[kr("S32", dkc)])
                        continue
                    if dbg == 27:
                        S.add("dve", lambda e, dkc=dkc, dec=dec: e.tensor_copy(out=S32[:, dkc, :], in_=ps[7][:]),
                            reads=[kr("S32", dkc), kr("ps", 7)], writes=[kr("S32", dkc)])
                        continue
                    S.add("dve", lambda e, dkc=dkc, dec=dec: e.scalar_tensor_tensor(
                        out=S32[:, dkc, :], in0=S32[:, dkc, :], scalar=dec, in1=ps[7][:],
                        op0=ALU.mult, op1=ALU.add),
                        reads=[kr("S32", dkc), kr("dec", dkc), kr("ps", 7)], writes=[kr("S32", dkc)])
                    if with_output:
                        S.add("act", lambda e, dkc=dkc: e.activation(out=Sbf[:, dkc, :], in_=S32[:, dkc, :], func=AF.Copy),
                              reads=[kr("S32", dkc)], writes=[kr("Sbf", dkc)])
                    elif dbg != 23:
                        idx = hd * 2 + dkc
                        S.add("dve", lambda e, dkc=dkc, idx=idx: e.tensor_tensor(
                            out=dcols[:, idx:idx + 1], in0=dcols[:, idx:idx + 1],
                            in1=self.smallf[:, 34 + dkc:35 + dkc], op=ALU.mult),
                            reads=[kr("dec", dkc), "dcols"], writes=["dcols"])
                if with_output:
                    S.add("act", lambda e: e.activation(out=on, in_=ps[6][:], func=AF.Square, accum_out=ssq),
                          reads=[kr("ps", 6)], writes=["on", "ssq"])
                    S.add("act", lambda e: e.activation(out=rstd, in_=ssq, func=AF.Sqrt, bias=self.epsc[:], scale=1.0 / 512),
                          reads=["ssq"], writes=["rstd"])
                    S.add("dve", lambda e: e.reciprocal(out=rstd, in_=rstd), reads=["rstd"], writes=["rstd"])
                    S.add("dve", lambda e: e.scalar_tensor_tensor(out=on, in0=ps[6][:], scalar=rstd, in1=gain,
                                                                  op0=ALU.mult, op1=ALU.mult),
                          reads=[kr("ps", 6), "rstd", "gain", "on"], writes=["on"])
                    S.add("act", lambda e: e.activation(out=sr, in_=ps[3][:], func=AF.Silu), reads=[kr("ps", 3)], writes=["sr"])
                    S.add("dve", lambda e: e.tensor_tensor(out=og, in0=on, in1=sr, op=ALU.mult), reads=["on", "sr"], writes=["og"])
                    tpv = ps[5][:].bitcast(BF16)[:, 512:1024]

                    def mmt(e, tpv=tpv):
                        r = None
                        for j in range(4):
                            r = e.transpose(tpv[:, j * 128:(j + 1) * 128], og[:, j * 128:(j + 1) * 128], self.ident[:])
                        return r
                    S.add("pe", mmt, reads=["og", "ident"], writes=[kr("ps", 5)])
                    S.add("act", lambda e, bs=bs, tpv=tpv: e.activation(
                        out=OTh[:, :, bs], in_=tpv.rearrange("p (j q) -> p j q", j=4), func=AF.Copy),
                        reads=[kr("ps", 5)], writes=[kr("OTh", blk)])
            if with_output:
                for dc in range(KC):
                    for th in range(NTH):
                        tsl = slice(th * 512, (th + 1) * 512)
                        pb = 2 + self.rot("gwo", 2)

                        def mmw(e, dc=dc, tsl=tsl, pb=pb, wOv=wOv):
                            r = None
                            for c in range(4):
                                r = e.matmul(ps[pb][:], lhsT=wOv[:, c, dc * 128:(dc + 1) * 128], rhs=OTh[:, c, tsl],
                                             start=(c == 0), stop=(c == 3))
                            return r
                        S.add("pe", mmw, reads=[kr("slab", bO)] + [kr("OTh", b_) for b_ in range(8)], writes=[kr("ps", pb)])
                        S.add("dve", lambda e, dc=dc, tsl=tsl, pb=pb: e.tensor_tensor(
                            out=hT[:, dc, tsl], in0=hT[:, dc, tsl], in1=ps[pb][:], op=ALU.add),
                            reads=[kr("ps", pb), kr("h", dc)], writes=[kr("h", dc)])
            elif dbg != 30:
                for dkc in range(2):
                    idx = hd * 2 + dkc
                    op = S.add("sp", lambda e, idx=idx, dkc=dkc: [e.dma_start(out=p["sloc"][idx], in_=S32[:, dkc, :])],
                               reads=[kr("S32", dkc)], dma_slot=("sloc", dkc))
                    self.out_ops.append(op)
        if not with_output:
            op = S.add("sp", lambda e: [e.dma_start(out=p["dtot"], in_=dcols)], reads=["dcols"], dma_slot="dtot")
            self.out_ops.append(op)

    def wo_proj(self, wo_d):
        S = self.S
        hT, OT = self.hT, self.xn
        wo_v = [self.WA[:, i * 8192:(i + 1) * 8192].rearrange("p (j c f) -> p j c f", j=4, c=KC) for i in range(2)]
        bufs = {}

        def load_wo(dg):
            b = self.rot("wo_g", 2)
            S.add("pool", lambda e, b=b, dg=dg: [e.dma_start(out=self.WA[:, b * 8192:(b + 1) * 8192], in_=wo_d[dg])],
                  writes=[kr("wog", b)], dma_slot=("wog", b))
            return b
        bufs[0] = load_wo(0)
        bufs[1] = load_wo(1)
        for dg in range(4):
            b = bufs[dg]
            for j in range(4):
                dc = dg * 4 + j
                for th in range(NTH):
                    tsl = slice(th * 512, (th + 1) * 512)
                    pb = self.rot("pwog", 2)

                    def mmw(e, b=b, j=j, tsl=tsl, pb=pb):
                        r = None
                        for c in range(KC):
                            r = e.matmul(self.psum[pb][:], lhsT=wo_v[b][:, j, c, :], rhs=OT[:, c, tsl],
                                         start=(c == 0), stop=(c == KC - 1))
                        return r
                    S.add("pe", mmw, reads=[kr("wog", b)], writes=[kr("ps", pb)])
                    S.add("dve", lambda e, dc=dc, tsl=tsl, pb=pb: e.tensor_tensor(
                        out=hT[:, dc, tsl], in0=hT[:, dc, tsl], in1=self.psum[pb][:], op=ALU.add),
                        reads=[kr("ps", pb), kr("h", dc)], writes=[kr("h", dc)])
            if dg + 2 < 4:
                bufs[dg + 2] = load_wo(dg + 2)

    def dsa_prep(self, p):
        S = self.S
        S.barrier()
        self.rmsnorm(self.hT, self.xn, T, p["g_mix"], "dp", "h", "xn")
        S.barrier()
        xn, ps = self.xn, self.psum
        wkk = self.WA[:, 0:KC * 320].rearrange("p (c f) -> p c f", c=KC)
        S.add("pool", lambda e: [e.dma_start(out=self.WA[:, 0:KC * 320], in_=p["wkk"])], writes=["wkk"], dma_slot="wkk")
        gb_ = self.MISC[:, 0:768].bitcast(F32)
        S.add("sp", lambda e: [e.dma_start(out=gb_[:, 0:256], in_=p["kvg"])], writes=["kvg"], dma_slot="kvg")
        S.add("sp", lambda e: [e.dma_start(out=gb_[:, 256:320], in_=p["ikg"])], writes=["ikg"], dma_slot="ikg")
        S.add("sp", lambda e: [e.dma_start(out=gb_[:, 320:384], in_=p["ikb"])], writes=["ikb"], dma_slot="ikb")
        junk = self.MISC[:, 768:1280].bitcast(F32)
        st = self.smallf
        for blk in range(8):
            bs = slice(blk * 128, (blk + 1) * 128)
            pb = self.rot("dpp", 2)
            ob = self.rot("dpo", 2)
            oc = self.MISC[:, 1280 + ob * 640: 1280 + ob * 640 + 512].bitcast(F32)
            ok = self.MISC[:, 1280 + ob * 640 + 512: 1280 + (ob + 1) * 640].bitcast(F32)
            xc = self.MISC[:, 2560:2688].bitcast(F32)

            def mm(e, bs=bs, pb=pb):
                r = None
                for c in range(KC):
                    r = e.matmul(ps[pb][:, 0:320], lhsT=xn[:, c, bs], rhs=wkk[:, c, :], start=(c == 0), stop=(c == KC - 1))
                return r
            S.add("pe", mm, reads=["wkk"] + [kr("xn", c) for c in range(KC)], writes=[kr("ps", pb)])
            S.add("act", lambda e, pb=pb: e.activation(out=junk, in_=ps[pb][:, 0:256], func=AF.Square, accum_out=st[:, 0:1]),
                  reads=[kr("ps", pb)], writes=["junk", "st0"])
            S.add("act", lambda e: e.activation(out=st[:, 1:2], in_=st[:, 0:1], func=AF.Sqrt, bias=self.epsc[:], scale=1.0 / 256),
                  reads=["st0"], writes=["st1"])
            S.add("dve", lambda e: e.reciprocal(out=st[:, 1:2], in_=st[:, 1:2]), reads=["st1"], writes=["st1"])
            S.add("dve", lambda e, pb=pb, oc=oc: e.scalar_tensor_tensor(out=oc, in0=ps[pb][:, 0:256], scalar=st[:, 1:2],
                                                                        in1=gb_[:, 0:256], op0=ALU.mult, op1=ALU.mult),
                  reads=[kr("ps", pb), "st1", "kvg"], writes=[kr("oc", ob)])
            op = S.add("sp", lambda e, bs=bs, oc=oc: [e.dma_start(out=p["ckv"][bs, :], in_=oc)], reads=[kr("oc", ob)],
                       dma_slot=("oc", ob))
            self.out_ops.append(op)
            S.add("dve", lambda e, pb=pb: e.tensor_reduce(out=st[:, 2:3], in_=ps[pb][:, 256:320], axis=AX.X, op=ALU.add),
                  reads=[kr("ps", pb)], writes=["st2"])
            S.add("dve", lambda e: e.tensor_scalar(out=st[:, 2:3], in0=st[:, 2:3], scalar1=1.0 / 64, scalar2=None, op0=ALU.mult),
                  reads=["st2"], writes=["st2"])
            S.add("dve", lambda e, pb=pb, xc=xc: e.tensor_scalar(out=xc, in0=ps[pb][:, 256:320], scalar1=st[:, 2:3], scalar2=None,
                                                                 op0=ALU.subtract),
                  reads=[kr("ps", pb), "st2"], writes=["xc"])
            S.add("act", lambda e, xc=xc: e.activation(out=junk[:, 0:64], in_=xc, func=AF.Square, accum_out=st[:, 3:4]),
                  reads=["xc"], writes=["junk", "st3"])
            S.add("act", lambda e: e.activation(out=st[:, 4:5], in_=st[:, 3:4], func=AF.Sqrt, bias=self.epsc[:], scale=1.0 / 64),
                  reads=["st3"], writes=["st4"])
            S.add("dve", lambda e: e.reciprocal(out=st[:, 4:5], in_=st[:, 4:5]), reads=["st4"], writes=["st4"])
            S.add("dve", lambda e, xc=xc, ok=ok: e.scalar_tensor_tensor(out=ok, in0=xc, scalar=st[:, 4:5], in1=gb_[:, 256:320],
                                                                        op0=ALU.mult, op1=ALU.mult),
                  reads=["xc", "st4", "ikg"], writes=[kr("ok", ob)])
            S.add("dve", lambda e, ok=ok: e.tensor_tensor(out=ok, in0=ok, in1=gb_[:, 320:384], op=ALU.add),
                  reads=[kr("ok", ob), "ikb"], writes=[kr("ok", ob)])
            op = S.add("sp", lambda e, bs=bs, ok=ok: [e.dma_start(out=p["kidx"][bs, :], in_=ok)], reads=[kr("ok", ob)],
                       dma_slot=("ok", ob))
            self.out_ops.append(op)

    def dsa_main(self, p, j, xT):
        S = self.S
        hT, xn, ps = self.hT, self.xn, self.psum
        BIG, MISC, WA, WB = self.BIG, self.MISC, self.WA, self.WB
        nk = (j + 1) * T
        S.barrier()
        self.rmsnorm(hT, xn, T, p["g_mix"], "dm", "h", "xn")
        S.barrier()
        Hraw = hT[:, :, :].rearrange("p c t -> p (c t)")
        rawq = Hraw[:, 0:4096].rearrange("p (c t) -> p c t", c=4)
        rstdq = Hraw[:, 4096:5120]
        cqT = BIG[:, 0:4096].rearrange("p (c t) -> p c t", c=4)
        selT = BIG[:, 4096:8192].rearrange("p (k q) -> p k q", q=128)
        qabsT = BIG[:, 8192:12288].rearrange("p (h c q) -> p h c q", h=16, c=2)
        qnT = BIG[:, 12288:14336].rearrange("p (h q) -> p h q", h=16)
        qidxT = BIG[:, 14336:16384].rearrange("p (h q) -> p h q", h=16)
        wcq = WA[:, 0:8192].rearrange("p (c f) -> p c f", c=KC)
        wwi = WA[:, 8192:8448].rearrange("p (c f) -> p c f", c=KC)
        S.add("pool", lambda e: [e.dma_start(out=WA[:, 0:8192], in_=p["wcq"])], writes=["wcq"], dma_slot="wcq")
        S.add("pool", lambda e: [e.dma_start(out=WA[:, 8192:8448], in_=p["wwi"])], writes=["wwi"], dma_slot="wwi")
        gq = self.smallf[:, 0:4]
        gqa = self.smallf[:, 4:6]
        S.add("sp", lambda e: [e.dma_start(out=gq, in_=p["gq"])], writes=["gq"], dma_slot="gq")
        S.add("sp", lambda e: [e.dma_start(out=gqa, in_=p["gqa"])], writes=["gqa"], dma_slot="gqa")
        widx = MISC[:, 5888:6144].bitcast(F32)
        sqt = [MISC[:, 0:512], MISC[:, 512:1024]]
        xr = [kr("xn", c) for c in range(KC)]
        for th in range(NTH):
            tsl = slice(th * 512, (th + 1) * 512)
            for jc in range(4):
                pb = self.rot("dq", 2)

                def mm(e, jc=jc, tsl=tsl, pb=pb):
                    r = None
                    for c in range(KC):
                        r = e.matmul(ps[pb][:], lhsT=wcq[:, c, jc * 128:(jc + 1) * 128], rhs=xn[:, c, tsl],
                                     start=(c == 0), stop=(c == KC - 1))
                    return r
                S.add("pe", mm, reads=["wcq"] + xr, writes=[kr("ps", pb)])
                S.add("act", lambda e, jc=jc, tsl=tsl, pb=pb: e.activation(out=rawq[:, jc, tsl], in_=ps[pb][:], func=AF.Copy),
                      reads=[kr("ps", pb)], writes=[kr("rawq", jc, th)])
                S.add("act", lambda e, jc=jc, pb=pb: e.activation(out=sqt[jc % 2], in_=ps[pb][:], func=AF.Square),
                      reads=[kr("ps", pb)], writes=[kr("sqt", jc % 2)])
                S.add("pe", lambda e, jc=jc: e.matmul(ps[2][:], lhsT=self.ones[:], rhs=sqt[jc % 2], start=(jc == 0), stop=(jc == 3)),
                      reads=[kr("sqt", jc % 2), "ones"], writes=[kr("ps", 2)])
            S.add("act", lambda e, tsl=tsl: e.activation(out=rstdq[:, tsl], in_=ps[2][:], func=AF.Sqrt, bias=self.epsc[:],
                                                         scale=1.0 / 512), reads=[kr("ps", 2)], writes=[kr("rstdq", th)])
            S.add("dve", lambda e, tsl=tsl: e.reciprocal(out=rstdq[:, tsl], in_=rstdq[:, tsl]), reads=[kr("rstdq", th)],
                  writes=[kr("rstdq", th)])
            for jc in range(4):
                S.add("dve", lambda e, jc=jc, tsl=tsl: e.scalar_tensor_tensor(
                    out=cqT[:, jc, tsl], in0=rawq[:, jc, tsl], scalar=gq[:, jc:jc + 1], in1=rstdq[:, tsl],
                    op0=ALU.mult, op1=ALU.mult), reads=[kr("rawq", jc, th), kr("rstdq", th), "gq"], writes=[kr("cqT", jc, th)])
        for blk in range(8):
            bs = slice(blk * 128, (blk + 1) * 128)
            pb = 3

            def mmwi(e, bs=bs):
                r = None
                for c in range(KC):
                    r = e.matmul(ps[3][:, 0:16], lhsT=xn[:, c, bs], rhs=wwi[:, c, :], start=(c == 0), stop=(c == KC - 1))
                return r
            S.add("pe", mmwi, reads=["wwi"] + xr, writes=[kr("ps", 3)])
            S.add("act", lambda e, blk=blk: e.activation(out=widx[:, blk * 16:(blk + 1) * 16], in_=ps[3][:, 0:16], func=AF.Copy,
                                                         scale=16 ** -0.5 * 64 ** -0.5),
                  reads=[kr("ps", 3)], writes=[kr("widx", blk)])
        S.barrier()
        H16 = Hraw.bitcast(BF16)
        Wsc = Hraw[:, 0:4096]
        ckvT = H16[:, 8192:8192 + 2 * nk].rearrange("p (c k) -> p c k", c=2)
        ckv1 = H16[:, 16384:16384 + (nk // 128) * 260].rearrange("p (k f) -> p k f", f=260)
        kidx = H16[0:64, 24832:24832 + nk]
        dmask = H16[:, 29184:31232].rearrange("p (m k) -> p m k", m=4)
        S.add("pool", lambda e: [e.dma_start(out=H16[:, 8192:8192 + 2 * nk], in_=p["ckvT"])], writes=["ckvT"], dma_slot="ckvT")
        S.add("pool", lambda e: [e.dma_start(out=H16[:, 16384:16384 + (nk // 128) * 260], in_=p["ckv1"])], writes=["ckv1"],
              dma_slot="ckv1")
        S.add("pool", lambda e: [e.dma_start(out=kidx, in_=p["kidxT"])], writes=["kidx"], dma_slot="kidxT")
        S.add("pool", lambda e: [e.dma_start(out=H16[:, 29184:31232], in_=p["dmask"])], writes=["dmask"], dma_slot="dmask")
        S.add("pool", lambda e: [e.dma_start(out=WB[:, 8192:10240], in_=p["negmask"])], writes=["negmask"], dma_slot="negm")
        negmask = WB[:, 8192:10240].rearrange("p (m k) -> p m k", m=4)
        wuq = WA[:, 0:8192].rearrange("p (c f) -> p c f", c=4)
        wuk = WA[:, 8192:12288].rearrange("p (h c) -> p h c", h=16)
        wiq = WA[:, 12288:16384].rearrange("p (c f) -> p c f", c=4)
        wuv = WB[:, 0:4096].rearrange("p (h c v) -> p h c v", h=16, c=2)
        S.add("pool", lambda e: [e.dma_start(out=WA[:, 0:8192], in_=p["wuq"])], writes=["wuq"], dma_slot="wuq")
        S.add("pool", lambda e: [e.dma_start(out=WA[:, 8192:12288], in_=p["wuk"])], writes=["wuk"], dma_slot="wuk")
        S.add("pool", lambda e: [e.dma_start(out=WA[:, 12288:16384], in_=p["wiq"])], writes=["wiq"], dma_slot="wiq")
        S.add("pool", lambda e: [e.dma_start(out=WB[:, 0:4096], in_=p["wuv"])], writes=["wuv"], dma_slot="wuv")
        S.barrier()
        OT = xn
        rt = [MISC[:, 0:1024].bitcast(F32), MISC[:, 1024:2048].bitcast(F32)]
        et = [MISC[:, 2048:2560], MISC[:, 2560:3072]]
        selt = MISC[:, 3072:3584]
        olat = MISC[:, 3584:4608].rearrange("p (h c) -> p h c", h=4)
        olT = MISC[:, 4608:5632].rearrange("p (h c q) -> p h c q", h=4, c=2)
        sqa = MISC[:, 5632:5888]
        mx8 = self.smallf[:, 8:16]
        den = self.smallf[:, 16:20]
        rstda = MISC[:, 0:256].bitcast(F32)
        for blk in range(8):
            gb = j * 8 + blk
            bs = slice(blk * 128, (blk + 1) * 128)
            nkt = gb // 4 + 1
            dio = gb % 4
            S.barrier()
            for h4 in range(4):
                pb = self.rot("qi", 2)

                def mmqi(e, h4=h4, pb=pb, bs=bs):
                    r = None
                    for jh in range(4):
                        h = h4 * 4 + jh
                        for c in range(4):
                            r = e.matmul(ps[pb][0:64, jh * 128:(jh + 1) * 128], lhsT=wiq[:, c, h * 64:(h + 1) * 64],
                                         rhs=cqT[:, c, bs], start=(c == 0), stop=(c == 3))
                    return r
                S.add("pe", mmqi, reads=["wiq"], writes=[kr("ps", pb)])
                S.add("act", lambda e, h4=h4, pb=pb: e.activation(
                    out=qidxT[0:64, h4 * 4:(h4 + 1) * 4, :], in_=ps[pb][0:64, :].rearrange("p (h q) -> p h q", h=4), func=AF.Copy),
                    reads=[kr("ps", pb)], writes=[kr("qidxT", h4)])
            for h4 in range(4):
                pb = self.rot("qi", 2)

                def mmqn(e, h4=h4, pb=pb, bs=bs):
                    r = None
                    for jh in range(4):
                        h = h4 * 4 + jh
                        for c in range(4):
                            r = e.matmul(ps[pb][:, jh * 128:(jh + 1) * 128], lhsT=wuq[:, c, h * 128:(h + 1) * 128],
                                         rhs=cqT[:, c, bs], start=(c == 0), stop=(c == 3))
                    return r
                S.add("pe", mmqn, reads=["wuq"], writes=[kr("ps", pb)])
                S.add("act", lambda e, h4=h4, pb=pb: e.activation(
                    out=qnT[:, h4 * 4:(h4 + 1) * 4, :], in_=ps[pb][:, :].rearrange("p (h q) -> p h q", h=4), func=AF.Copy),
                    reads=[kr("ps", pb)], writes=[kr("qnT", h4)])
            for h in range(16):
                pb = 2 + self.rot("qa", 2)
                pq = 4 + self.rot("qas", 2)

                def mmqa(e, h=h, pb=pb):
                    r = None
                    for cc in range(2):
                        r = e.matmul(ps[pb][:, cc * 128:(cc + 1) * 128], lhsT=wuk[:, h, cc * 128:(cc + 1) * 128], rhs=qnT[:, h, :],
                                     start=True, stop=True)
                    return r
                S.add("pe", mmqa, reads=["wuk", kr("qnT", h // 4)], writes=[kr("ps", pb)])
                S.add("act", lambda e, pb=pb: e.activation(out=sqa, in_=ps[pb][:, 0:256], func=AF.Square), reads=[kr("ps", pb)],
                      writes=["sqa"])

                def mmss(e, pq=pq):
                    e.matmul(ps[pq][:, 0:128], lhsT=self.ones[:], rhs=sqa[:, 0:128], start=True, stop=False)
                    return e.matmul(ps[pq][:, 0:128], lhsT=self.ones[:], rhs=sqa[:, 128:256], start=False, stop=True)
                S.add("pe", mmss, reads=["sqa", "ones"], writes=[kr("ps", pq)])
                S.add("act", lambda e, pq=pq: e.activation(out=rstda[:, 0:128], in_=ps[pq][:, 0:128], func=AF.Sqrt, bias=self.epsc[:],
                                                           scale=1.0 / 256), reads=[kr("ps", pq)], writes=["rstda"])
                S.add("dve", lambda e: e.reciprocal(out=rstda[:, 0:128], in_=rstda[:, 0:128]), reads=["rstda"], writes=["rstda"])
                for cc in range(2):
                    S.add("dve", lambda e, h=h, cc=cc, pb=pb: e.scalar_tensor_tensor(
                        out=qabsT[:, h, cc, :], in0=ps[pb][:, cc * 128:(cc + 1) * 128], scalar=gqa[:, cc:cc + 1],
                        in1=rstda[:, 0:128], op0=ALU.mult, op1=ALU.mult),
                        reads=[kr("ps", pb), "rstda", "gqa"], writes=[kr("qabsT", h)])
            S.barrier()
            for kt in range(nkt):
                ksl = slice(kt * 512, (kt + 1) * 512)
                for h in range(16):
                    pb = self.rot("sc", 2)
                    S.add("pe", lambda e, h=h, ksl=ksl, pb=pb: e.matmul(ps[pb][:], lhsT=qidxT[0:64, h, :], rhs=kidx[:, ksl],
                                                                        start=True, stop=True),
                          reads=["kidx", kr("qidxT", h // 4)], writes=[kr("ps", pb)])
                    S.add("act", lambda e, pb=pb: e.activation(out=rt[pb], in_=ps[pb][:], func=AF.Relu), reads=[kr("ps", pb)],
                          writes=[kr("rt", pb)])
                    wcol = widx[:, blk * 16 + h: blk * 16 + h + 1]
                    if h == 0:
                        S.add("dve", lambda e, pb=pb, ksl=ksl, wcol=wcol: e.tensor_scalar(
                            out=Wsc[:, ksl], in0=rt[pb], scalar1=wcol, scalar2=None, op0=ALU.mult),
                            reads=[kr("rt", pb), kr("widx", blk)], writes=[kr("W", kt)])
                    else:
                        S.add("dve", lambda e, pb=pb, ksl=ksl, wcol=wcol: e.scalar_tensor_tensor(
                            out=Wsc[:, ksl], in0=rt[pb], scalar=wcol, in1=Wsc[:, ksl], op0=ALU.mult, op1=ALU.add),
                            reads=[kr("rt", pb), kr("widx", blk), kr("W", kt)], writes=[kr("W", kt)])
                if kt == nkt - 1:
                    S.add("dve", lambda e, ksl=ksl, dio=dio: e.tensor_tensor(out=Wsc[:, ksl], in0=Wsc[:, ksl], in1=negmask[:, dio, :], op=ALU.min),
                          reads=[kr("W", kt), "negmask"], writes=[kr("W", kt)])
            wkeys = [kr("W", kt) for kt in range(nkt)]
            Wv = Wsc[:, 0:nkt * 512]
            for it in range(32):
                S.add("dve", lambda e, Wv=Wv: e.max(out=mx8, in_=Wv), reads=wkeys, writes=["mx8"])
                S.add("dve", lambda e, Wv=Wv: e.match_replace(out=Wv, in_to_replace=mx8, in_values=Wv, imm_value=-3.0e38),
                      reads=wkeys + ["mx8"], writes=wkeys)
            for kt in range(nkt):
                ksl = slice(kt * 512, (kt + 1) * 512)
                S.add("dve", lambda e, ksl=ksl: e.tensor_scalar(out=selt, in0=Wsc[:, ksl], scalar1=-2.0e38, scalar2=None, op0=ALU.is_le),
                      reads=[kr("W", kt)], writes=["selt"])
                if kt == nkt - 1:
                    S.add("dve", lambda e, dio=dio: e.tensor_tensor(out=selt, in0=selt, in1=dmask[:, dio, :], op=ALU.mult),
                          reads=["selt", "dmask"], writes=["selt"])
                nb = 4 if kt < nkt - 1 else dio + 1
                tp_i = 2 + self.rot("st", 2)
                tpv = ps[tp_i][:].bitcast(BF16)

                def mmt(e, nb=nb, tpv=tpv):
                    r = None
                    for i in range(nb):
                        r = e.transpose(tpv[:, i * 128:(i + 1) * 128], selt[:, i * 128:(i + 1) * 128], self.ident[:])
                    return r
                S.add("pe", mmt, reads=["selt", "ident"], writes=[kr("ps", tp_i)])
                S.add("act", lambda e, kt=kt, nb=nb, tpv=tpv: e.activation(
                    out=selT[:, kt * 4:kt * 4 + nb, :], in_=tpv[:, 0:nb * 128].rearrange("p (k q) -> p k q", q=128), func=AF.Copy),
                    reads=[kr("ps", tp_i)], writes=[kr("selT", kt)])
            S.barrier()
            for hg in range(4):
                for kb in range(gb + 1):
                    kbs = slice(kb * 128, (kb + 1) * 128)
                    pb = self.rot("lg", 2)

                    def mml(e, hg=hg, kbs=kbs, pb=pb):
                        r = None
                        for jh in range(4):
                            for cc in range(2):
                                r = e.matmul(ps[pb][:, jh * 128:(jh + 1) * 128], lhsT=ckvT[:, cc, kbs], rhs=qabsT[:, hg * 4 + jh, cc, :],
                                             start=(cc == 0), stop=(cc == 1))
                        return r
                    S.add("pe", mml, reads=["ckvT"], writes=[kr("ps", pb)])
                    S.add("act", lambda e, pb=pb: e.activation(out=et[pb], in_=ps[pb][:], func=AF.Exp, scale=256.0 ** -0.5),
                          reads=[kr("ps", pb)], writes=[kr("et", pb)])
                    S.add("pool", lambda e, pb=pb, kb=kb: e.tensor_tensor(
                        out=et[pb].rearrange("p (j q) -> p j q", j=4), in0=et[pb].rearrange("p (j q) -> p j q", j=4),
                        in1=selT[:, kb:kb + 1, :].to_broadcast([128, 4, 128]), op=ALU.mult),
                        reads=[kr("et", pb)], writes=[kr("et", pb)])

                    def mmo(e, pb=pb, kb=kb, gb=gb):
                        r = None
                        for jh in range(4):
                            r = e.matmul(ps[4 + jh][:, 0:257], lhsT=et[pb][:, jh * 128:(jh + 1) * 128], rhs=ckv1[:, kb, 0:257],
                                         start=(kb == 0), stop=(kb == gb))
                        return r
                    S.add("pe", mmo, reads=[kr("et", pb), "ckv1"], writes=[kr("ps", 4 + jh_) for jh_ in range(4)])
                for jh in range(4):
                    S.add("dve", lambda e, jh=jh: e.reciprocal(out=den[:, jh:jh + 1], in_=ps[4 + jh][:, 256:257]),
                          reads=[kr("ps", 4 + jh)], writes=[kr("den", jh)])
                    S.add("dve", lambda e, jh=jh: e.tensor_scalar(out=olat[:, jh, :], in0=ps[4 + jh][:, 0:256], scalar1=den[:, jh:jh + 1],
                                                                 scalar2=None, op0=ALU.mult),
                          reads=[kr("ps", 4 + jh), kr("den", jh)], writes=[kr("olat", jh)])
                tpv = ps[2][:].bitcast(BF16)

                def mmt2(e, tpv=tpv):
                    r = None
                    for jh in range(4):
                        for cc in range(2):
                            r = e.transpose(tpv[:, (jh * 2 + cc) * 128:(jh * 2 + cc + 1) * 128], olat[:, jh, cc * 128:(cc + 1) * 128],
                                            self.ident[:])
                    return r
                S.add("pe", mmt2, reads=[kr("olat", jh_) for jh_ in range(4)] + ["ident"], writes=[kr("ps", 2)])
                S.add("act", lambda e, tpv=tpv: e.activation(out=olT, in_=tpv.rearrange("p (h c q) -> p h c q", h=4, c=2), func=AF.Copy),
                      reads=[kr("ps", 2)], writes=["olT"])

                def mmv(e, hg=hg):
                    r = None
                    for jh in range(4):
                        for cc in range(2):
                            r = e.matmul(ps[3][:, jh * 128:(jh + 1) * 128], lhsT=wuv[:, hg * 4 + jh, cc, :], rhs=olT[:, jh, cc, :],
                                         start=(cc == 0), stop=(cc == 1))
                    return r
                S.add("pe", mmv, reads=["olT", "wuv"], writes=[kr("ps", 3)])
                S.add("act", lambda e, hg=hg, bs=bs: e.activation(
                    out=OT[:, hg * 4:(hg + 1) * 4, bs], in_=ps[3][:, :].rearrange("p (h q) -> p h q", h=4), func=AF.Copy),
                    reads=[kr("ps", 3)], writes=[kr("OTd", blk, hg)])
        S.barrier()
        self.load_h(xT)
        S.barrier()
        self.wo_proj(p["wo"])

    def finish(self):
        S = self.S
        nc = self.nc
        with nc.Block() as block:
            @block.tensor
            def _(e):
                S.emit("pe", e)

            @block.scalar
            def _(e):
                S.emit("act", e)

            @block.vector
            def _(e):
                S.emit("dve", e)

            @block.gpsimd
            def _(e):
                S.emit("pool", e)

            @block.sync
            def _(e):
                S.emit("sp", e)
                S.final_wait(e, self.out_ops)
                import os
                if os.environ.get("GLA_FW"):
                    for en in os.environ["GLA_FW"].split(","):
                        cands = [o for o in S.ops[en] if not o.is_dma]
                        if cands:
                            S.final_wait(e, [cands[-1]])
        self.stack.close()
        return nc


def lay_vec(g):
    return np.ascontiguousarray(np.asarray(g, np.float32).reshape(KC, 128).T)


def lay_kslab(w, ncol):
    w = np.asarray(w, np.float32)
    K_, N_ = w.shape
    ns = N_ // ncol
    out = w.reshape(K_ // 128, 128, ns, ncol).transpose(2, 1, 0, 3).reshape(ns, 128, (K_ // 128) * ncol)
    return np.ascontiguousarray(out)


def lay_wgu(w, GW=256):
    w = np.asarray(w, np.float32)
    nslab = DFF // GW
    g = w[:, :DFF].reshape(KC, 128, nslab, GW)
    u = w[:, DFF:].reshape(KC, 128, nslab, GW)
    gu = np.stack([g, u], axis=3)
    out = gu.transpose(2, 1, 0, 3, 4).reshape(nslab, 128, KC * 2 * GW)
    return np.ascontiguousarray(out)


def lay_wd(w, GRP=4):
    w = np.asarray(w, np.float32)
    ngrp = NFC // GRP
    out = w.reshape(ngrp, GRP, 128, D).transpose(0, 2, 1, 3).reshape(ngrp, 128, GRP * D)
    return np.ascontiguousarray(out)


def lay_wo(w):
    w = np.asarray(w, np.float32)
    out = w.reshape(KC, 128, 4, 4, 128).transpose(2, 1, 3, 0, 4).reshape(4, 128, 4 * KC * 128)
    return np.ascontiguousarray(out)


def swa_host_inputs(pre, inp, halo_rows, first):
    w_in = np.asarray(inp[pre + "w_in"], np.float32)
    wq = w_in[:, :2048]
    wk = w_in[:, 2048:2304]
    wv = w_in[:, 2304:2560]
    wkp = np.zeros((2048, 8, 128), np.float32)
    for hk in range(4):
        wkp[:, hk * 2 + 0, 0:64] = wk[:, hk * 64:(hk + 1) * 64]
        wkp[:, hk * 2 + 1, 64:128] = wk[:, hk * 64:(hk + 1) * 64]
    s_idx = np.arange(128)[:, None]
    q_idx = np.arange(128)[None, :]
    mprev = (s_idx > q_idx).astype(np.float32)
    mcur = (s_idx <= q_idx).astype(np.float32)
    m0 = np.zeros_like(mprev) if first else mprev
    masks = np.stack([m0, mprev, mcur], axis=1).reshape(128, 3 * 128)
    qkn = np.stack([np.tile(np.asarray(inp[pre + "q_norm"], np.float32), 2),
                    np.tile(np.asarray(inp[pre + "k_norm"], np.float32), 2)], axis=1)
    return {
        "g_mix": lay_vec(inp[pre + "norm_mix"]),
        "haloT": np.ascontiguousarray(halo_rows.T),
        "wq": lay_kslab(wq, 512),
        "wk": lay_kslab(wkp.reshape(2048, 1024), 512),
        "wv": lay_kslab(wv, 256)[0],
        "wo": lay_wo(inp[pre + "wo"]),
        "qkn": np.ascontiguousarray(qkn),
        "sinks": np.ascontiguousarray(np.broadcast_to(np.asarray(inp[pre + "sinks"], np.float32)[None, :], (128, 32))),
        "masks": np.ascontiguousarray(masks),
    }


def swa_dram(B, pfx):
    return {
        "g_mix": B.din(pfx + "g_mix", [128, KC]),
        "haloT": B.din(pfx + "haloT", [D, 128]),
        "wq": B.din(pfx + "wq", [4, 128, KC * 512]),
        "wk": B.din(pfx + "wk", [2, 128, KC * 512]),
        "wv": B.din(pfx + "wv", [128, KC * 256]),
        "wo": B.din(pfx + "wo", [4, 128, 4 * KC * 128]),
        "qkn": B.din(pfx + "qkn", [128, 2]),
        "sinks": B.din(pfx + "sinks", [128, 32]),
        "masks": B.din(pfx + "masks", [128, 3 * 128]),
    }


def gla_host_inputs(pre, inp, sprev, dprev):
    w_in = np.asarray(inp[pre + "w_in"], np.float32)
    q = w_in[:, 0:1024]; k = w_in[:, 1024:2048]; v = w_in[:, 2048:4096]; r = w_in[:, 4096:6144]; gl = w_in[:, 6144:6160]
    wA = np.concatenate([np.concatenate([q[:, h * 256:(h + 1) * 256], k[:, h * 256:(h + 1) * 256]], axis=1) for h in range(4)], axis=1)
    wgb = np.zeros((64, 1024), np.float32)
    wgb[0:16] = np.asarray(inp[pre + "w_gate_b"], np.float32)
    wgb[32] = np.asarray(inp[pre + "gate_bias"], np.float32)
    s_idx = np.arange(128)[:, None]; t_idx = np.arange(128)[None, :]
    uneg = np.where(s_idx <= t_idx, -1.0 / 16, 0.0).astype(np.float32)
    ugt = np.where(s_idx > t_idx, -1.0 / 16, 0.0).astype(np.float32)
    cz = (s_idx <= t_idx).astype(np.float32)
    masks = np.stack([uneg, ugt, cz], axis=1).reshape(128, 384)
    wo = np.asarray(inp[pre + "wo"], np.float32)
    woh = wo.reshape(4, 4, 128, 2048).transpose(0, 2, 1, 3).reshape(4, 128, 4 * 2048)
    return {
        "g_mix": lay_vec(inp[pre + "norm_mix"]),
        "wA": lay_kslab(wA, 512), "wB": lay_kslab(v, 512), "wC": lay_kslab(r, 512),
        "wgl": lay_kslab(gl, 16)[0], "wgb": wgb,
        "onorm": np.ascontiguousarray(np.broadcast_to(np.asarray(inp[pre + "o_norm"], np.float32)[None, :], (128, 512))),
        "woh": np.ascontiguousarray(woh), "masks": np.ascontiguousarray(masks),
        "sprev": np.ascontiguousarray(sprev, dtype=np.float32), "dprev": np.ascontiguousarray(dprev, dtype=np.float32),
    }


def gla_dram(B, pfx, with_output):
    d = {
        "g_mix": B.din(pfx + "g_mix", [128, KC]),
        "wA": B.din(pfx + "wA", [4, 128, KC * 512]), "wB": B.din(pfx + "wB", [4, 128, KC * 512]),
        "wgl": B.din(pfx + "wgl", [128, KC * 16]), "wgb": B.din(pfx + "wgb", [64, 1024]),
        "masks": B.din(pfx + "masks", [128, 384]),
        "onorm": B.din(pfx + "onorm", [128, 512]),
        "dprev": B.din(pfx + "dprev", [3, 128, 8]),
    }
    if with_output:
        d["wC"] = B.din(pfx + "wC", [4, 128, KC * 512])
        d["woh"] = B.din(pfx + "woh", [4, 128, 4 * 2048])
        d["sprev"] = B.din(pfx + "sprev", [3, 8, 128, 512])
    else:
        d["sloc"] = B.dout(pfx + "sloc", [8, 128, 512])
        d["dtot"] = B.dout(pfx + "dtot", [128, 8])
    return d


def dsa_prep_host(pre, inp):
    w_in = np.asarray(inp[pre + "w_in"], np.float32)
    wkk = np.concatenate([w_in[:, 512:768], w_in[:, 768:832]], axis=1)
    bc = lambda v, n: np.ascontiguousarray(np.broadcast_to(np.asarray(v, np.float32)[None, :], (128, n)))
    return {"g_mix": lay_vec(inp[pre + "norm_mix"]), "wkk": lay_kslab(wkk, 320)[0],
            "kvg": bc(inp[pre + "kv_norm"], 256), "ikg": bc(inp[pre + "idx_k_norm"], 64), "ikb": bc(inp[pre + "idx_k_bias"], 64)}


def dsa_prep_dram(B, pfx):
    return {"g_mix": B.din(pfx + "g_mix", [128, KC]), "wkk": B.din(pfx + "wkk", [128, KC * 320]),
            "kvg": B.din(pfx + "kvg", [128, 256]), "ikg": B.din(pfx + "ikg", [128, 64]), "ikb": B.din(pfx + "ikb", [128, 64]),
            "ckv": B.dout(pfx + "ckv", [T, 256]), "kidx": B.dout(pfx + "kidx", [T, 64])}


def dsa_main_host(pre, inp, j, ckv_seq, kidx_seq):
    nk = (j + 1) * T
    w_in = np.asarray(inp[pre + "w_in"], np.float32)
    ck = np.asarray(ckv_seq[:nk], np.float32)
    ckvT = np.ascontiguousarray(ck.T.reshape(2, 128, nk).transpose(1, 0, 2).reshape(128, 2 * nk))
    ckv1 = np.zeros((nk // 128, 128, 260), np.float32)
    ckv1[:, :, 0:256] = ck.reshape(nk // 128, 128, 256)
    ckv1[:, :, 256] = 1.0
    ckv1 = np.ascontiguousarray(ckv1.transpose(1, 0, 2).reshape(128, (nk // 128) * 260))
    kidxT = np.ascontiguousarray(np.asarray(kidx_seq[:nk], np.float32).T)
    q = np.arange(128)[:, None]
    kcol = np.arange(512)[None, :]
    dm = np.stack([((kcol // 128 < d) | ((kcol // 128 == d) & (kcol % 128 <= q))).astype(np.float32) for d in range(4)], axis=1)
    ngm = np.where(dm > 0, np.float32(3e38), np.float32(-1e30)).astype(np.float32)
    w_uk = np.asarray(inp[pre + "w_uk"], np.float32)
    w_uv = np.asarray(inp[pre + "w_uv"], np.float32)
    w_uq = np.asarray(inp[pre + "w_uq"], np.float32)
    w_iq = np.asarray(inp[pre + "w_iq"], np.float32)
    return {
        "g_mix": lay_vec(inp[pre + "norm_mix"]),
        "wcq": lay_kslab(w_in[:, 0:512], 512)[0], "wwi": lay_kslab(w_in[:, 832:848], 16)[0],
        "gq": np.ascontiguousarray(np.asarray(inp[pre + "q_lat_norm"], np.float32).reshape(4, 128).T),
        "gqa": np.ascontiguousarray(np.asarray(inp[pre + "q_abs_norm"], np.float32).reshape(2, 128).T),
        "wuq": lay_kslab(w_uq, 2048)[0], "wiq": lay_kslab(w_iq, 1024)[0],
        "wuk": np.ascontiguousarray(w_uk.transpose(1, 0, 2).reshape(128, 16 * 256)),
        "wuv": np.ascontiguousarray(w_uv.reshape(16, 2, 128, 128).transpose(2, 0, 1, 3).reshape(128, 16 * 2 * 128)),
        "wo": lay_wo(inp[pre + "wo"]),
        "ckvT": ckvT, "ckv1": ckv1, "kidxT": kidxT,
        "dmask": np.ascontiguousarray(dm.reshape(128, 2048)), "negmask": np.ascontiguousarray(ngm.reshape(128, 2048)),
    }


def dsa_main_dram(B, pfx, j):
    nk = (j + 1) * T
    return {"g_mix": B.din(pfx + "g_mix", [128, KC]), "wcq": B.din(pfx + "wcq", [128, KC * 512]),
            "wwi": B.din(pfx + "wwi", [128, KC * 16]), "gq": B.din(pfx + "gq", [128, 4]), "gqa": B.din(pfx + "gqa", [128, 2]),
            "wuq": B.din(pfx + "wuq", [128, 4 * 2048]), "wiq": B.din(pfx + "wiq", [128, 4 * 1024]),
            "wuk": B.din(pfx + "wuk", [128, 16 * 256]), "wuv": B.din(pfx + "wuv", [128, 16 * 2 * 128]),
            "wo": B.din(pfx + "wo", [4, 128, 4 * KC * 128]),
            "ckvT": B.din(pfx + "ckvT", [128, 2 * nk]), "ckv1": B.din(pfx + "ckv1", [128, (nk // 128) * 260]),
            "kidxT": B.din(pfx + "kidxT", [64, nk]), "dmask": B.din(pfx + "dmask", [128, 2048]),
            "negmask": B.din(pfx + "negmask", [128, 2048])}


def ffn_dram(B, pfx):
    return {"g": B.din(pfx + "g", [128, KC]), "wgu": B.din(pfx + "wgu", [NFC // 2, 128, KC * 2 * 256]),
            "wd": B.din(pfx + "wd", [NFC // 4, 128, 4 * D])}


def ffn_host_inputs(pre, inp):
    return {"g": lay_vec(inp[pre + "norm_ffn"]), "wgu": lay_wgu(inp[pre + "w_gate_up"]), "wd": lay_wd(inp[pre + "w_down"])}


IDENT = np.eye(128, dtype=np.float32)


def _prog(stages_fn, extra_out=None):
    B = Builder()
    xT = B.din("xT", [D, T])
    ident = B.din("ident", [128, 128])
    outT = B.dout("outT", [D, T])
    B.alloc()
    B.load_ident(ident)
    B.load_h(xT)
    stages_fn(B, xT)
    B.S.barrier()
    B.store_h(outT)
    return B.finish()


def _run(nc, in_maps):
    n = len(in_maps)
    res = run_bass_kernel_spmd(nc, in_maps, core_ids=list(range(n)))
    return res.results


def _pfx(p, d):
    return {p + k: v for k, v in d.items()}


def kernel(**inputs):
    inp = {k: np.asarray(v) for k, v in inputs.items()}
    h = np.ascontiguousarray(inp["x"], dtype=np.float32)
    cores = [(c // 4, c % 4) for c in range(NCORES)]

    def own(hh, b, j):
        return np.ascontiguousarray(hh[b, j * T:(j + 1) * T].T)

    def gather(results, key="outT"):
        out = np.empty((2, 4 * T, D), np.float32)
        for c, (b, j) in enumerate(cores):
            out[b, j * T:(j + 1) * T] = results[c][key].T
        return out

    def swa_maps(li, hh):
        pre = f"l{li}_"
        shared = None
        maps = []
        for (b, j) in cores:
            halo = hh[b, j * T - 128:j * T] if j > 0 else np.zeros((128, D), np.float32)
            hi = swa_host_inputs(pre, inp, halo, j == 0)
            if shared is None:
                shared = hi
            else:
                for k in ("wq", "wk", "wv", "wo", "g_mix", "qkn", "sinks"):
                    hi[k] = shared[k]
            maps.append(_pfx("a_", hi))
        return maps

    zs = np.zeros((3, 8, 128, 512), np.float32)
    zd = np.ones((3, 128, 8), np.float32)
    g1 = gla_host_inputs("l1_", inp, zs, zd)
    g1p = {k: v for k, v in g1.items() if k not in ("wC", "woh", "sprev")}
    f0 = ffn_host_inputs("l0_", inp)

    def stA(B, xT):
        P = swa_dram(B, "a_")
        F = ffn_dram(B, "f_")
        G = gla_dram(B, "b_", False)
        B.swa(P)
        B.ffn(F["g"], F["wgu"], F["wd"])
        B.gla(G, False)
    ncA = _prog(stA)
    sm = swa_maps(0, h)
    maps = []
    for c, (b, j) in enumerate(cores):
        im = {"xT": own(h, b, j), "ident": IDENT}
        im.update(sm[c]); im.update(_pfx("f_", f0)); im.update(_pfx("b_", g1p))
        maps.append(im)
    rA = _run(ncA, maps)
    h = gather(rA)
    del f0, sm, maps

    f1 = ffn_host_inputs("l1_", inp)
    dp = dsa_prep_host("l2_", inp)

    def stB(B, xT):
        G = gla_dram(B, "b_", True)
        F = ffn_dram(B, "f_")
        Pp = dsa_prep_dram(B, "c_")
        B.gla(G, True)
        B.ffn(F["g"], F["wgu"], F["wd"])
        B.dsa_prep(Pp)
    ncB = _prog(stB)
    maps = []
    for c, (b, j) in enumerate(cores):
        sp = np.zeros((3, 8, 128, 512), np.float32)
        dpv = np.ones((3, 128, 8), np.float32)
        for s_ in range(3):
            i = j - 3 + s_
            if i >= 0:
                sp[s_] = rA[b * 4 + i]["b_sloc"]
                dpv[s_] = rA[b * 4 + i]["b_dtot"]
        gi = dict(g1)
        gi["sprev"] = sp
        gi["dprev"] = dpv
        im = {"xT": own(h, b, j), "ident": IDENT}
        im.update(_pfx("b_", gi)); im.update(_pfx("f_", f1)); im.update(_pfx("c_", dp))
        maps.append(im)
    rB = _run(ncB, maps)
    h = gather(rB)
    ckv = [np.concatenate([rB[b * 4 + j]["c_ckv"] for j in range(4)], 0) for b in range(2)]
    kidx = [np.concatenate([rB[b * 4 + j]["c_kidx"] for j in range(4)], 0) for b in range(2)]
    del f1, g1, g1p, maps, rA

    f2 = ffn_host_inputs("l2_", inp)
    h2 = np.empty_like(h)
    for j in range(4):
        def stC(B, xT, j=j):
            P = dsa_main_dram(B, "c_", j)
            F = ffn_dram(B, "f_")
            B.dsa_main(P, j, xT)
            B.ffn(F["g"], F["wgu"], F["wd"])
        ncC = _prog(stC)
        maps = []
        for b in range(2):
            hm = dsa_main_host("l2_", inp, j, ckv[b], kidx[b])
            im = {"xT": own(h, b, j), "ident": IDENT}
            im.update(_pfx("c_", hm)); im.update(_pfx("f_", f2))
            maps.append(im)
        rC = _run(ncC, maps)
        for b in range(2):
            h2[b, j * T:(j + 1) * T] = rC[b]["outT"].T
    h = h2
    del f2

    f3 = ffn_host_inputs("l3_", inp)

    def stD(B, xT):
        P = swa_dram(B, "a_")
        F = ffn_dram(B, "f_")
        B.swa(P)
        B.ffn(F["g"], F["wgu"], F["wd"])
    ncD = _prog(stD)
    sm = swa_maps(3, h)
    maps = []
    for c, (b, j) in enumerate(cores):
        im = {"xT": own(h, b, j), "ident": IDENT}
        im.update(sm[c]); im.update(_pfx("f_", f3))
        maps.append(im)
    rD = _run(ncD, maps)
    return gather(rD)
```

```python
import numpy as np
from contextlib import ExitStack
import concourse.bass as bass
import concourse.mybir as mybir
from concourse.bass_utils import run_bass_kernel_spmd

F32 = mybir.dt.float32
BF16 = mybir.dt.bfloat16
AF = mybir.ActivationFunctionType
ALU = mybir.AluOpType
AX = mybir.AxisListType

D = 2048
KC = D // 128
T = 1024
NTH = T // 512
DFF = 5632
NFC = DFF // 128
EPS = 1e-6
NCORES = 8


DBG_WAITS = None


class Op:
    __slots__ = ("eng", "fn", "deps", "ticket", "is_dma", "ndma", "slot")


class Sched:
    ENGS = ("pe", "act", "dve", "pool", "sp")

    def __init__(self, nc, stack):
        self.nc = nc
        self.stack = stack
        self.ops = {e: [] for e in self.ENGS}
        self.last_w = {}
        self.readers = {}
        self.cnt = {e: 0 for e in self.ENGS}
        self.eng_sem = {}
        self.epoch = 0
        self.slot_sem = {}
        self.slot_cnt = {}
        self.nsem = 0
        self.bar_deps = []
        self.all_dma = []
        self.new_epoch()

    def _sem(self, name):
        self.nsem += 1
        return self.stack.enter_context(self.nc.semaphore(name))

    def new_epoch(self):
        self.epoch += 1
        for e in ("pe", "act", "dve", "pool"):
            self.eng_sem[e] = self._sem(f"s_{e}_{self.epoch}")
            self.cnt[e] = 0

    def add(self, eng, fn, reads=(), writes=(), dma_slot=None, ndma=1):
        op = Op()
        op.eng = eng
        op.fn = fn
        op.is_dma = dma_slot is not None
        op.ndma = ndma
        deps = list(self.bar_deps)
        for r in reads:
            w = self.last_w.get(r)
            if w is not None:
                deps.append(w)
        for w_ in writes:
            w = self.last_w.get(w_)
            if w is not None:
                deps.append(w)
            deps.extend(self.readers.get(w_, ()))
        if op.is_dma:
            if dma_slot not in self.slot_sem:
                self.slot_sem[dma_slot] = self._sem(f"d_{len(self.slot_sem)}")
                self.slot_cnt[dma_slot] = 0
            self.slot_cnt[dma_slot] += 16 * ndma
            op.ticket = (self.slot_sem[dma_slot], self.slot_cnt[dma_slot])
        else:
            self.cnt[eng] += 1
            op.ticket = (self.eng_sem[eng], self.cnt[eng])
        seen = set()
        dd = []
        for d_ in deps:
            if d_ is op or id(d_) in seen:
                continue
            seen.add(id(d_))
            if d_.eng == "pe" and eng == "pe" and not d_.is_dma:
                continue
            dd.append(d_)
        op.deps = dd
        for r in reads:
            self.readers.setdefault(r, []).append(op)
        for w_ in writes:
            self.last_w[w_] = op
            self.readers[w_] = []
        self.ops[eng].append(op)
        if op.is_dma:
            self.all_dma.append(op)
        return op

    def barrier(self):
        deps = []
        for e in self.ENGS:
            for op in reversed(self.ops[e]):
                if not op.is_dma:
                    deps.append(op)
                    break
        latest = {}
        for op in self.all_dma:
            latest[id(op.ticket[0])] = op
        deps.extend(latest.values())
        self.bar_deps = deps
        self.last_w = {}
        self.readers = {}

    def emit(self, eng, e):
        waited = {}
        for op in self.ops[eng]:
            need = {}
            for d_ in op.deps:
                s, v = d_.ticket
                k = id(s)
                if waited.get(k, 0) >= v:
                    continue
                if k not in need or need[k][1] < v:
                    need[k] = (s, v)
            for k, (s, v) in need.items():
                e.wait_ge(s, v)
                waited[k] = v
                if DBG_WAITS is not None:
                    DBG_WAITS.append((eng, self.ops[eng].index(op), getattr(s, "name", str(s)), v))
            res = op.fn(e)
            s, v = op.ticket
            if op.is_dma:
                assert isinstance(res, (list, tuple)) and len(res) == op.ndma, (len(res), op.ndma)
                for ins in res:
                    ins.then_inc(s, 16)
            else:
                if isinstance(res, (list, tuple)):
                    res = res[-1]
                res.then_inc(s, 1)

    def final_wait(self, e, ops):
        for op in ops:
            s, v = op.ticket
            e.wait_ge(s, v)


def kr(name, *idx):
    return (name,) + idx


HD_A = 64
NQ_A = 32
NKV_A = 4
TK = T + 128


class Builder:
    GW = 256
    GRP = 4

    def __init__(self):
        self.nc = bass.Bass("TRN2", target_bir_lowering=False)
        self.stack = ExitStack()
        self.S = Sched(self.nc, self.stack)
        self.out_ops = []
        self.cnt = {}

    def din(self, name, shape, dt=F32):
        return self.nc.dram_tensor(name, list(shape), dt, kind="ExternalInput").ap()

    def dout(self, name, shape, dt=F32):
        return self.nc.dram_tensor(name, list(shape), dt, kind="ExternalOutput").ap()

    def sb(self, name, shape, dt):
        return self.stack.enter_context(self.nc.sbuf_tensor(name, list(shape), dt))

    def ps(self, name, shape=(128, 512), dt=F32):
        return self.stack.enter_context(self.nc.psum_tensor(name, list(shape), dt))

    def rot(self, name, n):
        v = self.cnt.get(name, 0)
        self.cnt[name] = v + 1
        return v % n

    def alloc(self):
        self.hT = self.sb("hT", [128, KC, T], F32)
        self.xn = self.sb("xn", [128, KC, T], BF16)
        self.BIG = self.sb("BIG", [128, 16384], BF16)
        self.WA = self.sb("WA", [128, 16384], BF16)
        self.WB = self.sb("WB", [128, 16384], BF16)
        self.MISC = self.sb("MISC", [128, 6144], BF16)
        self.sb_masks = self.sb("masks", [128, 384], BF16)
        self.ones = self.sb("ones", [128, 128], BF16)
        self.bones = self.sb("bones", [128, 128], BF16)
        self.ident = self.sb("ident_sb", [128, 128], BF16)
        self.epsc = self.sb("epsc", [128, 1], F32)
        self.gcol = self.sb("gcolt", [128, KC], F32)
        self.smallf = self.sb("smallf", [128, 64], F32)
        self.psum = [self.ps(f"ps{i}") for i in range(8)]
        S = self.S
        S.add("dve", lambda e: e.memset(self.ones[:], 1.0), writes=["ones"])
        S.add("dve", lambda e: e.memset(self.epsc[:], EPS), writes=["epsc"])
        S.add("dve", lambda e: e.memset(self.bones[:], 0.0), writes=["bones"])
        S.add("dve", lambda e: e.memset(self.bones[0:64, 0:64], 1.0), reads=["bones"], writes=["bones"])
        S.add("dve", lambda e: e.memset(self.bones[64:128, 64:128], 1.0), reads=["bones"], writes=["bones"])

    def load_ident(self, ident_d):
        self.S.add("pool", lambda e: [e.dma_start(out=self.ident[:], in_=ident_d)], writes=["ident"],
                   dma_slot="ident")

    def load_h(self, xT):
        S = self.S
        src = xT.rearrange("(c p) t -> p c t", p=128)
        for q in range(4):
            sl = slice(q * 4, (q + 1) * 4)
            S.add("sp", lambda e, sl=sl: [e.dma_start(out=self.hT[:, sl, :], in_=src[:, sl, :])],
                  writes=[kr("h", c) for c in range(q * 4, q * 4 + 4)], dma_slot=("ldh", q))

    def store_h(self, outT):
        S = self.S
        dst = outT.rearrange("(c p) t -> p c t", p=128)
        for q in range(4):
            sl = slice(q * 4, (q + 1) * 4)
            op = S.add("sp", lambda e, sl=sl: [e.dma_start(out=dst[:, sl, :], in_=self.hT[:, sl, :])],
                       reads=[kr("h", c) for c in range(q * 4, q * 4 + 4)], dma_slot=("sth", q))
            self.out_ops.append(op)

    def rmsnorm(self, src, dst, n, gvec_d, tag, skey, dkey):
        S = self.S
        rstd = self.BIG[:, 0:2 * n].bitcast(F32)
        sq = [self.BIG[:, 2 * n + i * n: 2 * n + (i + 1) * n] for i in range(2)]
        S.add("sp", lambda e: [e.dma_start(out=self.gcol[:], in_=gvec_d)], writes=["gcol"], dma_slot="gcol")
        nh = (n + 511) // 512
        w = n // nh
        pss = [self.psum[6], self.psum[7]]
        for c in range(KC):
            S.add("act", lambda e, c=c: e.activation(out=sq[c % 2], in_=src[:, c, :], func=AF.Square),
                  reads=[kr(skey, c)], writes=[kr(tag + "sq", c % 2)])
            for th in range(nh):
                S.add("pe", lambda e, c=c, th=th: e.matmul(
                    pss[th][:, 0:w], lhsT=self.ones[:], rhs=sq[c % 2][:, th * w:(th + 1) * w],
                    start=(c == 0), stop=(c == KC - 1)),
                    reads=[kr(tag + "sq", c % 2), "ones"], writes=[kr("ps", 6 + th)])
        for th in range(nh):
            sl = slice(th * w, (th + 1) * w)
            S.add("act", lambda e, th=th, sl=sl: e.activation(
                out=rstd[:, sl], in_=pss[th][:, 0:w], func=AF.Sqrt, bias=self.epsc[:], scale=1.0 / D),
                reads=[kr("ps", 6 + th), "epsc"], writes=[kr(tag + "rstd", th)])
            S.add("dve", lambda e, sl=sl: e.reciprocal(out=rstd[:, sl], in_=rstd[:, sl]),
                  reads=[kr(tag + "rstd", th)], writes=[kr(tag + "rstd", th)])
        for c in range(KC):
            S.add("dve", lambda e, c=c: e.scalar_tensor_tensor(
                out=dst[:, c, :], in0=src[:, c, :], scalar=self.gcol[:, c:c + 1], in1=rstd,
                op0=ALU.mult, op1=ALU.mult),
                reads=[kr(skey, c), "gcol"] + [kr(tag + "rstd", th) for th in range(nh)], writes=[kr(dkey, c)])

    def ffn(self, gvec, wgu_d, wd_d):
        S = self.S
        GW, GRP = self.GW, self.GRP
        G = GW // 128
        hT, xn = self.hT, self.xn
        S.barrier()
        self.rmsnorm(hT, xn, T, gvec, "f", "h", "xn")
        S.barrier()
        wgu = [self.WA[:, i * 8192:(i + 1) * 8192].rearrange("p (c f) -> p c f", c=KC) for i in range(2)]
        wd = [self.WB[:, i * 8192:(i + 1) * 8192].rearrange("p (c f) -> p c f", c=GRP) for i in range(2)]
        actb = [self.BIG[:, i * 4096:(i + 1) * 4096].rearrange("p (c f) -> p c f", c=GRP) for i in range(2)]
        sg = [self.BIG[:, 8192 + i * 1024: 8192 + (i + 1) * 1024].bitcast(F32) for i in range(2)]
        nslab = NFC // G
        ngrp = NFC // GRP
        pg = [self.psum[0], self.psum[1]]
        pu = [self.psum[2], self.psum[3]]
        pd = [self.psum[4], self.psum[5]]

        def load_wgu(s):
            b = self.rot("wgu", 2)
            S.add("pool", lambda e, s=s, b=b: [e.dma_start(
                out=self.WA[:, b * 8192:(b + 1) * 8192], in_=wgu_d[s])],
                writes=[kr("wgu", b)], dma_slot=("wgu", b))
            return b

        def load_wd(g):
            b = self.rot("wd", 2)
            S.add("pool", lambda e, g=g, b=b: [e.dma_start(
                out=self.WB[:, b * 8192:(b + 1) * 8192], in_=wd_d[g])],
                writes=[kr("wd", b)], dma_slot=("wd", b))
            return b

        slab_buf = {}
        wd_buf = {}
        slab_buf[0] = load_wgu(0)
        slab_buf[1] = load_wgu(1)
        wd_buf[0] = load_wd(0)
        for g in range(ngrp):
            ab = self.rot("act", 2)
            for j in range(GRP):
                fc = g * GRP + j
                s = fc // G
                jj = fc % G
                wb = slab_buf[s]
                for th in range(NTH):
                    tsl = slice(th * 512, (th + 1) * 512)

                    def mm_g(e, wb=wb, jj=jj, th=th, tsl=tsl):
                        r = None
                        for c in range(KC):
                            r = e.matmul(pg[th][:], lhsT=wgu[wb][:, c, jj * 128:(jj + 1) * 128],
                                         rhs=xn[:, c, tsl], start=(c == 0), stop=(c == KC - 1))
                        return r

                    def mm_u(e, wb=wb, jj=jj, th=th, tsl=tsl):
                        r = None
                        for c in range(KC):
                            r = e.matmul(pu[th][:], lhsT=wgu[wb][:, c, GW + jj * 128:GW + (jj + 1) * 128],
                                         rhs=xn[:, c, tsl], start=(c == 0), stop=(c == KC - 1))
                        return r
                    xr = [kr("xn", c) for c in range(KC)]
                    S.add("pe", mm_g, reads=[kr("wgu", wb)] + xr, writes=[kr("ps", th)])
                    S.add("pe", mm_u, reads=[kr("wgu", wb)] + xr, writes=[kr("ps", 2 + th)])
                    S.add("act", lambda e, th=th: e.activation(out=sg[th], in_=pg[th][:], func=AF.Silu),
                          reads=[kr("ps", th)], writes=[kr("sg", th)])
                    S.add("dve", lambda e, th=th, ab=ab, j=j, tsl=tsl: e.tensor_tensor(
                        out=actb[ab][:, j, tsl], in0=sg[th], in1=pu[th][:], op=ALU.mult),
                        reads=[kr("sg", th), kr("ps", 2 + th)], writes=[kr("act", ab, j, th)])
                if jj == G - 1 and s + 2 < nslab:
                    slab_buf[s + 2] = load_wgu(s + 2)
            if g + 1 < ngrp:
                wd_buf[g + 1] = load_wd(g + 1)
            db = wd_buf[g]
            for dc in range(KC):
                for th in range(NTH):
                    tsl = slice(th * 512, (th + 1) * 512)
                    pb = self.rot("pd", 2)

                    def mm_d(e, db=db, dc=dc, tsl=tsl, pb=pb, ab=ab):
                        r = None
                        for j in range(GRP):
                            r = e.matmul(pd[pb][:], lhsT=wd[db][:, j, dc * 128:(dc + 1) * 128],
                                         rhs=actb[ab][:, j, tsl], start=(j == 0), stop=(j == GRP - 1))
                        return r
                    S.add("pe", mm_d, reads=[kr("wd", db)] + [kr("act", ab, j, th) for j in range(GRP)],
                          writes=[kr("ps", 4 + pb)])
                    S.add("dve", lambda e, dc=dc, tsl=tsl, pb=pb: e.tensor_tensor(
                        out=hT[:, dc, tsl], in0=hT[:, dc, tsl], in1=pd[pb][:], op=ALU.add),
                        reads=[kr("ps", 4 + pb), kr("h", dc)], writes=[kr("h", dc)])

    def proj_headnorm(self, wv, srcs, dsts, gain_col, wkey):
        S = self.S
        for (rhs_fn, n, rkeys), (dst, wkeys) in zip(srcs, dsts):
            pb = self.rot("phn", 2)
            praw = self.psum[pb]
            pssq = self.psum[2 + pb]
            sqs = self.MISC[:, pb * 512: pb * 512 + n]
            rs = self.MISC[:, 1024 + pb * 1024: 1024 + pb * 1024 + 2 * n].bitcast(F32)

            def mm(e, rhs_fn=rhs_fn, n=n, praw=praw):
                r = None
                for c in range(KC):
                    r = e.matmul(praw[:, 0:n], lhsT=wv[:, c, :], rhs=rhs_fn(c), start=(c == 0), stop=(c == KC - 1))
                return r
            S.add("pe", mm, reads=[wkey] + list(rkeys), writes=[kr("ps", pb)])
            S.add("act", lambda e, n=n, praw=praw, sqs=sqs: e.activation(out=sqs, in_=praw[:, 0:n], func=AF.Square),
                  reads=[kr("ps", pb)], writes=[kr("hsq", pb)])
            S.add("pe", lambda e, n=n, pssq=pssq, sqs=sqs: e.matmul(pssq[:, 0:n], lhsT=self.bones[:], rhs=sqs,
                                                                     start=True, stop=True),
                  reads=[kr("hsq", pb), "bones"], writes=[kr("ps", 2 + pb)])
            S.add("act", lambda e, n=n, pssq=pssq, rs=rs: e.activation(
                out=rs, in_=pssq[:, 0:n], func=AF.Sqrt, bias=self.epsc[:], scale=1.0 / HD_A),
                reads=[kr("ps", 2 + pb), "epsc"], writes=[kr("hrs", pb)])
            S.add("dve", lambda e, rs=rs: e.reciprocal(out=rs, in_=rs), reads=[kr("hrs", pb)], writes=[kr("hrs", pb)])
            S.add("dve", lambda e, n=n, praw=praw, rs=rs, dst=dst: e.scalar_tensor_tensor(
                out=dst, in0=praw[:, 0:n], scalar=gain_col, in1=rs, op0=ALU.mult, op1=ALU.mult),
                reads=[kr("ps", pb), kr("hrs", pb), "gains"], writes=list(wkeys))

    def swa(self, p):
        S = self.S
        hT, xn = self.hT, self.xn
        S.barrier()
        halo_h = self.MISC[:, 0:4096].bitcast(F32).rearrange("p (c t) -> p c t", c=KC)
        kT = self.WB[:, 0:8 * TK].rearrange("p (c t) -> p c t", c=8)
        o_v = 8 * TK
        vext = self.WB[:, o_v:o_v + 9 * 4 * 80].rearrange("p (b h d) -> p b h d", b=9, h=4)
        o_x = o_v + 9 * 4 * 80
        xnh = self.WB[:, o_x:o_x + KC * 128].rearrange("p (c t) -> p c t", c=KC)
        assert o_x + KC * 128 <= 16384
        halo_src = p["haloT"].rearrange("(c p) t -> p c t", p=128)
        S.add("sp", lambda e: [e.dma_start(out=halo_h, in_=halo_src)], writes=[kr("hh", c) for c in range(KC)],
              dma_slot="halo")
        gains = self.smallf[:, 0:2]
        S.add("sp", lambda e: [e.dma_start(out=gains, in_=p["qkn"])], writes=["gains"], dma_slot="gains")
        sinks = self.smallf[:, 8:40]
        S.add("sp", lambda e: [e.dma_start(out=sinks, in_=p["sinks"])], writes=["sinks"], dma_slot="sinks")
        S.add("act", lambda e: e.activation(out=sinks, in_=sinks, func=AF.Exp), reads=["sinks"], writes=["sinks"])
        S.add("pool", lambda e: [e.dma_start(out=self.sb_masks[:], in_=p["masks"])], writes=["masks"], dma_slot="masks")
        masks = self.sb_masks[:, :].rearrange("p (m q) -> p m q", m=3)
        self.osb_full = [self.MISC[:, 2048:4096], self.MISC[:, 4096:6144]]
        self.rmsnorm(halo_h, xnh, 128, p["g_mix"], "hn", "hh", "xnh")
        S.barrier()
        self.rmsnorm(hT, xn, T, p["g_mix"], "m", "h", "xn")
        S.barrier()
        qT = self.BIG[:, :].rearrange("p (c t) -> p c t", c=KC)
        wa = [self.WA[:, i * 8192:(i + 1) * 8192].rearrange("p (c f) -> p c f", c=KC) for i in range(2)]

        def load_wa(src, ncol=512):
            b = self.rot("swa_wa", 2)
            S.add("pool", lambda e, b=b: [e.dma_start(out=self.WA[:, b * 8192: b * 8192 + KC * ncol], in_=src)],
                  writes=[kr("wa", b)], dma_slot=("wa", b))
            return b
        xr = [kr("xn", c) for c in range(KC)]
        xhr = [kr("xnh", c) for c in range(KC)]
        slabs = [("q", i) for i in range(4)] + [("k", i) for i in range(2)]
        bufs = {}
        bufs[0] = load_wa(p["wq"][0])
        for si, (kind, i) in enumerate(slabs):
            if si + 1 < len(slabs):
                k2, i2 = slabs[si + 1]
                bufs[si + 1] = load_wa(p["wq"][i2] if k2 == "q" else p["wk"][i2])
            b = bufs[si]
            for j in range(4):
                wv = wa[b][:, :, j * 128:(j + 1) * 128]
                if kind == "q":
                    ch = i * 4 + j
                    srcs = [(lambda c, th=th: xn[:, c, th * 512:(th + 1) * 512], 512, xr) for th in range(NTH)]
                    dsts = [(qT[:, ch, th * 512:(th + 1) * 512], [kr("qT", ch, th)]) for th in range(NTH)]
                    self.proj_headnorm(wv, srcs, dsts, gains[:, 0:1], kr("wa", b))
                else:
                    ch = i * 4 + j
                    srcs = [(lambda c: xnh[:, c, :], 128, xhr)] + \
                           [(lambda c, th=th: xn[:, c, th * 512:(th + 1) * 512], 512, xr) for th in range(NTH)]
                    dsts = [(kT[:, ch, 0:128], [kr("kT", ch, 0)])] + \
                           [(kT[:, ch, 128 + th * 512:128 + (th + 1) * 512], [kr("kT", ch, 1 + th)]) for th in range(NTH)]
                    self.proj_headnorm(wv, srcs, dsts, gains[:, 1:2], kr("wa", b))
        bv = load_wa(p["wv"], 256)
        wvv = self.WA[:, bv * 8192: bv * 8192 + KC * 256].rearrange("p (c f) -> p c f", c=KC)
        S.add("pool", lambda e: e.memset(vext[:, :, :, 64:65], 1.0), writes=["vones"])
        for blk in range(9):
            pb = 4 + self.rot("pv", 2)

            def mmv(e, blk=blk, pb=pb):
                r = None
                for c in range(KC):
                    lhs = xnh[:, c, :] if blk == 0 else xn[:, c, (blk - 1) * 128: blk * 128]
                    r = e.matmul(self.psum[pb][:, 0:256], lhsT=lhs, rhs=wvv[:, c, :], start=(c == 0), stop=(c == KC - 1))
                return r
            S.add("pe", mmv, reads=[kr("wa", bv)] + (xhr if blk == 0 else xr), writes=[kr("ps", pb)])
            S.add("act", lambda e, blk=blk, pb=pb: e.activation(
                out=vext[:, blk, :, 0:64], in_=self.psum[pb][:, 0:256].rearrange("p (h d) -> p h d", h=4),
                func=AF.Copy), reads=[kr("ps", pb)], writes=[kr("v", blk)])
        S.barrier()
        OT = xn
        et = [self.MISC[:, i * 512:(i + 1) * 512] for i in range(4)]
        den = self.smallf[:, 40:48]
        wo_bufs = {}
        wo_v = [self.WA[:, i * 8192:(i + 1) * 8192].rearrange("p (j c f) -> p j c f", j=4, c=KC) for i in range(2)]

        def load_wo(dg):
            b = self.rot("swa_wo", 2)
            S.add("pool", lambda e, b=b, dg=dg: [e.dma_start(out=self.WA[:, b * 8192:(b + 1) * 8192], in_=p["wo"][dg])],
                  writes=[kr("wo", b)], dma_slot=("wob", b))
            return b
        wo_bufs[0] = load_wo(0)
        wo_bufs[1] = load_wo(1)
        tp_bf = [self.psum[6][:].bitcast(BF16), self.psum[7][:].bitcast(BF16)]
        for n in range(8):
            qs = slice(n * 128, (n + 1) * 128)
            for hk in range(4):
                for half in range(2):
                    grp = hk * 2 + half
                    sp_i = self.rot("sps", 2)
                    ps_prev = self.psum[sp_i * 2]
                    ps_cur = self.psum[sp_i * 2 + 1]
                    e_prev = et[sp_i * 2]
                    e_cur = et[sp_i * 2 + 1]
                    for kb, pst in ((0, ps_prev), (1, ps_cur)):
                        ks = slice((n + kb) * 128, (n + kb + 1) * 128)

                        def mms(e, hk=hk, half=half, ks=ks, pst=pst, qs=qs):
                            r = None
                            for j in range(4):
                                h = hk * 8 + half * 4 + j
                                r = e.matmul(pst[:, j * 128:(j + 1) * 128], lhsT=kT[:, hk * 2 + (h % 2), ks],
                                             rhs=qT[:, h // 2, qs], start=True, stop=True)
                            return r
                        S.add("pe", mms, reads=["kTall", "qTall"], writes=[kr("ps", sp_i * 2 + kb)])
                    midx_prev = 0 if n == 0 else 1
                    for kb, pst, ee, mi in ((0, ps_prev, e_prev, midx_prev), (1, ps_cur, e_cur, 2)):
                        S.add("act", lambda e, pst=pst, ee=ee: e.activation(out=ee, in_=pst[:], func=AF.Exp, scale=HD_A ** -0.5),
                              reads=[kr("ps", sp_i * 2 + kb)], writes=[kr("et", sp_i * 2 + kb)])
                        S.add("pool", lambda e, ee=ee, mi=mi: e.tensor_tensor(
                            out=ee.rearrange("p (j q) -> p j q", j=4), in0=ee.rearrange("p (j q) -> p j q", j=4),
                            in1=masks[:, mi:mi + 1, :].to_broadcast([128, 4, 128]), op=ALU.mult),
                            reads=[kr("et", sp_i * 2 + kb), "masks"], writes=[kr("et", sp_i * 2 + kb)])
                    po_i = 4 + self.rot("spo", 2)
                    po = self.psum[po_i]

                    def mmo(e, n=n, hk=hk, po=po, e_prev=e_prev, e_cur=e_cur):
                        r = None
                        for j in range(4):
                            e.matmul(po[:, j * 128:j * 128 + 65], lhsT=e_prev[:, j * 128:(j + 1) * 128],
                                     rhs=vext[:, n, hk, 0:65], start=True, stop=False)
                            r = e.matmul(po[:, j * 128:j * 128 + 65], lhsT=e_cur[:, j * 128:(j + 1) * 128],
                                         rhs=vext[:, n + 1, hk, 0:65], start=False, stop=True)
                        return r
                    S.add("pe", mmo, reads=[kr("et", sp_i * 2), kr("et", sp_i * 2 + 1), "vall"], writes=[kr("ps", po_i)])
                    h0 = hk * 8 + half * 4
                    pov = po[:, :].rearrange("p (j d) -> p j d", j=4)
                    dn = den[:, 0:4] if (grp % 2 == 0) else den[:, 4:8]
                    dk = kr("den", grp % 2)
                    S.add("dve", lambda e, pov=pov, dn=dn, h0=h0: e.tensor_tensor(
                        out=dn.rearrange("p (j o) -> p j o", o=1), in0=pov[:, :, 64:65],
                        in1=sinks[:, h0:h0 + 4].rearrange("p (j o) -> p j o", o=1), op=ALU.add),
                        reads=[kr("ps", po_i), "sinks"], writes=[dk])
                    S.add("dve", lambda e, dn=dn: e.reciprocal(out=dn, in_=dn), reads=[dk], writes=[dk])
                    ob = self.osb_full[n % 2][:, h0 * 64:(h0 + 4) * 64].rearrange("p (j d) -> p j d", j=4)
                    S.add("dve", lambda e, pov=pov, dn=dn, ob=ob: e.tensor_tensor(
                        out=ob, in0=pov[:, :, 0:64], in1=dn.rearrange("p (j o) -> p j o", o=1).to_broadcast([128, 4, 64]),
                        op=ALU.mult),
                        reads=[kr("ps", po_i), dk], writes=[kr("osb", n % 2, grp)])
            for c4 in range(4):
                tp_i = self.rot("tp", 2)
                tpv = tp_bf[tp_i]

                def mmt(e, c4=c4, tpv=tpv, n=n):
                    r = None
                    for j in range(4):
                        c = c4 * 4 + j
                        r = e.transpose(tpv[:, j * 128:(j + 1) * 128], self.osb_full[n % 2][:, c * 128:(c + 1) * 128],
                                        self.ident[:])
                    return r
                S.add("pe", mmt, reads=[kr("osb", n % 2, g_) for g_ in range(8)] + ["ident"], writes=[kr("ps", 6 + tp_i)])
                S.add("act", lambda e, c4=c4, tpv=tpv, qs=qs: e.activation(
                    out=OT[:, c4 * 4:(c4 + 1) * 4, qs], in_=tpv[:, 0:512].rearrange("p (j q) -> p j q", j=4), func=AF.Copy),
                    reads=[kr("ps", 6 + tp_i)], writes=[kr("OT", n, c4)])
        S.barrier()
        for dg in range(4):
            b = wo_bufs[dg]
            for j in range(4):
                dc = dg * 4 + j
                for th in range(NTH):
                    tsl = slice(th * 512, (th + 1) * 512)
                    pb = self.rot("pwo", 2)

                    def mmw(e, b=b, j=j, tsl=tsl, pb=pb):
                        r = None
                        for c in range(KC):
                            r = e.matmul(self.psum[pb][:], lhsT=wo_v[b][:, j, c, :], rhs=OT[:, c, tsl],
                                         start=(c == 0), stop=(c == KC - 1))
                        return r
                    S.add("pe", mmw, reads=[kr("wo", b)], writes=[kr("ps", pb)])
                    S.add("dve", lambda e, dc=dc, tsl=tsl, pb=pb: e.tensor_tensor(
                        out=hT[:, dc, tsl], in0=hT[:, dc, tsl], in1=self.psum[pb][:], op=ALU.add),
                        reads=[kr("ps", pb), kr("h", dc)], writes=[kr("h", dc)])
            if dg + 2 < 4:
                wo_bufs[dg + 2] = load_wo(dg + 2)

    def gla(self, p, with_output):
        S = self.S
        hT, xn = self.hT, self.xn
        S.barrier()
        self.rmsnorm(hT, xn, T, p["g_mix"], "g", "h", "xn")
        S.barrier()
        BIG, MISC = self.BIG, self.MISC
        S32 = BIG[:, 13568:15616].bitcast(F32).rearrange("p (c f) -> p c f", c=2)
        Sbf = BIG[:, 8192:9216].rearrange("p (c f) -> p c f", c=2)
        OTh = BIG[:, 3072:7168].rearrange("p (c t) -> p c t", c=4)
        glT = BIG[0:64, 9216:10240]
        wgb = BIG[0:64, 10240:11264]
        wgl = BIG[:, 11264:11520].rearrange("p (c f) -> p c f", c=KC)
        gain = BIG[:, 11520:12544].bitcast(F32)
        Ltmp = BIG[:, 12544:13568].bitcast(F32)
        sp_tok = MISC[:, 0:256]
        eb = MISC[:, 256:768].bitcast(F32)
        enb = MISC[:, 768:1280].bitcast(F32)
        expf = MISC[:, 1280:1792].bitcast(F32)
        qd = MISC[:, 1792:2048].rearrange("p (c t) -> p c t", c=2)
        ki = MISC[:, 2048:2304].rearrange("p (c t) -> p c t", c=2)
        kend = MISC[:, 2304:2560]
        vsb = MISC[:, 2560:3072]
        am = MISC[:, 3072:3200]
        e1 = MISC[:, 3200:3712].bitcast(F32)
        on = MISC[:, 3712:4736].bitcast(F32)
        sr = MISC[:, 4736:5248]
        og = MISC[:, 5248:5760]
        dcols = self.smallf[:, 0:8]
        dprev = self.smallf[:, 8:32].rearrange("p (s c) -> p s c", s=3)
        ssq = self.smallf[:, 32:33]
        rstd = self.smallf[:, 33:34]
        masks = self.sb_masks[:, :].rearrange("p (m q) -> p m q", m=3)
        ps = self.psum
        import os
        dbg = int(os.environ.get("GLA_DBG", "0"))
        if dbg == 10:
            return
        S.add("pool", lambda e: [e.dma_start(out=self.sb_masks[:], in_=p["masks"])], writes=["masks"], dma_slot="masks")
        S.add("pool", lambda e: [e.dma_start(out=wgb, in_=p["wgb"])], writes=["wgb"], dma_slot="wgb")
        S.add("pool", lambda e: [e.dma_start(out=BIG[:, 11264:11520], in_=p["wgl"])], writes=["wgl"], dma_slot="wgl")
        S.add("sp", lambda e: [e.dma_start(out=gain, in_=p["onorm"])], writes=["gain"], dma_slot="gain")
        S.add("sp", lambda e: [e.dma_start(out=self.smallf[:, 8:32], in_=p["dprev"].rearrange("s p c -> p s c"))],
              writes=["dprev"], dma_slot="dprev") if False else None
        for s_ in range(3):
            S.add("sp", lambda e, s_=s_: [e.dma_start(out=dprev[:, s_, :], in_=p["dprev"][s_])], writes=[kr("dprev", s_)],
                  dma_slot=("dprev", s_))
        if dbg == 11:
            S.barrier()
            S.add("dve", lambda e: e.memset(dcols, 1.0), writes=["dcols"])
            return
        S.add("dve", lambda e: e.memset(glT, 0.0), writes=["glT"])
        S.add("dve", lambda e: e.memset(glT[32:64, :], 1.0), reads=["glT"], writes=["glT"])
        xr = [kr("xn", c) for c in range(KC)]
        for th in range(NTH):
            tsl = slice(th * 512, (th + 1) * 512)

            def mmgl(e, tsl=tsl, th=th):
                r = None
                for c in range(KC):
                    r = e.matmul(ps[th][0:16, :], lhsT=wgl[:, c, :], rhs=xn[:, c, tsl], start=(c == 0), stop=(c == KC - 1))
                return r
            S.add("pe", mmgl, reads=["wgl"] + xr, writes=[kr("ps", th)])
            S.add("act", lambda e, tsl=tsl, th=th: e.activation(out=glT[0:16, tsl], in_=ps[th][0:16, :], func=AF.Copy),
                  reads=[kr("ps", th), "glT"], writes=["glT"])
        S.add("dve", lambda e: e.memset(dcols, 1.0), writes=["dcols"])
        S.barrier()
        import os
        dbg = int(os.environ.get("GLA_DBG", "0"))
        if dbg == 1:
            return
        slabv = [self.WA[:, 0:8192], self.WA[:, 8192:16384], self.WB[:, 0:8192], self.WB[:, 8192:16384]]
        nbuf = 4

        def load_slab(src):
            b = self.rot("gla_slab", nbuf)
            S.add("pool", lambda e, b=b: [e.dma_start(out=slabv[b], in_=src)], writes=[kr("slab", b)], dma_slot=("gslab", b))
            return b
        for hd in range(4):
            bA = load_slab(p["wA"][hd])
            bB = load_slab(p["wB"][hd])
            if with_output:
                bC = load_slab(p["wC"][hd])
                bO = load_slab(p["woh"][hd])
            wA = slabv[bA].rearrange("p (c f) -> p c f", c=KC)
            wBv = slabv[bB].rearrange("p (c f) -> p c f", c=KC)
            if with_output:
                wCv = slabv[bC].rearrange("p (c f) -> p c f", c=KC)
                wOv = slabv[bO].rearrange("p (c f) -> p c f", c=4)
            for dkc in range(2):
                idx = hd * 2 + dkc
                if with_output:
                    S.add("sp", lambda e, idx=idx, dkc=dkc: [e.dma_start(out=S32[:, dkc, :], in_=p["sprev"][0, idx])],
                          writes=[kr("S32", dkc)], dma_slot=("s32", dkc))
                    for s_ in (1, 2):
                        S.add("sp", lambda e, idx=idx, s_=s_: [e.dma_start(out=Ltmp, in_=p["sprev"][s_, idx])],
                              writes=["Ltmp"], dma_slot="ltmp")
                        S.add("dve", lambda e, idx=idx, s_=s_, dkc=dkc: e.scalar_tensor_tensor(
                            out=S32[:, dkc, :], in0=S32[:, dkc, :], scalar=dprev[:, s_, idx:idx + 1], in1=Ltmp,
                            op0=ALU.mult, op1=ALU.add),
                            reads=[kr("S32", dkc), "Ltmp", kr("dprev", s_)], writes=[kr("S32", dkc)])
                elif dbg == 36:
                    altm = BIG[:, 0:2048].bitcast(F32).rearrange("p (c f) -> p c f", c=2)
                    S.add("dve", lambda e, dkc=dkc, altm=altm: e.memset(altm[:, dkc, :], 0.0), writes=[kr("alt", dkc)])
                    S.add("dve", lambda e, dkc=dkc: e.memset(S32[:, dkc, :], 0.0), writes=[kr("S32", dkc)])
                elif dbg == 35:
                    S.add("dve", lambda e, dkc=dkc: e.tensor_copy(out=S32[:, dkc, :], in_=gain), reads=["gain"], writes=[kr("S32", dkc)])
                else:
                    S.add("dve", lambda e, dkc=dkc: e.memset(S32[:, dkc, :], 0.0), writes=[kr("S32", dkc)])
                if with_output:
                    S.add("act", lambda e, dkc=dkc: e.activation(out=Sbf[:, dkc, :], in_=S32[:, dkc, :], func=AF.Copy),
                          reads=[kr("S32", dkc)], writes=[kr("Sbf", dkc)])
            for blk in range(8):
                if dbg == 2 and (hd, blk) != (0, 0):
                    continue
                bs = slice(blk * 128, (blk + 1) * 128)
                if with_output:
                    def mmqk(e, bs=bs, wA=wA):
                        r = None
                        for j in range(4):
                            for c in range(KC):
                                r = e.matmul(ps[0][:, j * 128:(j + 1) * 128], lhsT=wA[:, c, j * 128:(j + 1) * 128],
                                             rhs=xn[:, c, bs], start=(c == 0), stop=(c == KC - 1))
                        return r
                    S.add("pe", mmqk, reads=[kr("slab", bA)] + xr, writes=[kr("ps", 0)])

                def mmk(e, bs=bs, wA=wA):
                    r = None
                    for c in range(KC):
                        r = e.matmul(ps[1][:, 0:256], lhsT=xn[:, c, bs], rhs=wA[:, c, 256:512], start=(c == 0), stop=(c == KC - 1))
                    return r
                S.add("pe", mmk, reads=[kr("slab", bA)] + xr, writes=[kr("ps", 1)])
                S.add("pe", lambda e, bs=bs, hd=hd: e.matmul(ps[1][:, 256:512], lhsT=glT[0:33, bs],
                                                             rhs=wgb[0:33, hd * 256:(hd + 1) * 256], start=True, stop=True),
                      reads=["glT", "wgb"], writes=[kr("ps", 1)])

                def mmv(e, bs=bs, wBv=wBv):
                    r = None
                    for c in range(KC):
                        r = e.matmul(ps[2][:], lhsT=xn[:, c, bs], rhs=wBv[:, c, :], start=(c == 0), stop=(c == KC - 1))
                    return r
                S.add("pe", mmv, reads=[kr("slab", bB)] + xr, writes=[kr("ps", 2)])
                S.add("act", lambda e: e.activation(out=vsb, in_=ps[2][:], func=AF.Copy), reads=[kr("ps", 2)], writes=["vsb"])
                if dbg == 20:
                    break
                S.add("act", lambda e: e.activation(out=e1, in_=ps[1][:, 256:512], func=AF.Exp, scale=-1.0),
                      reads=[kr("ps", 1)], writes=["e1"])
                S.add("act", lambda e: e.activation(out=sp_tok, in_=e1, func=AF.Ln, bias=1.0), reads=["e1"], writes=["sp"])
                if dbg == 21:
                    break

                def mmb(e):
                    r = None
                    for dkc in range(2):
                        r = e.matmul(ps[4][:, dkc * 128:(dkc + 1) * 128], lhsT=sp_tok[:, dkc * 128:(dkc + 1) * 128],
                                     rhs=masks[:, 0, :], start=True, stop=True)
                    return r
                S.add("pe", mmb, reads=["sp", "masks"], writes=[kr("ps", 4)])
                S.add("pe", lambda e: e.matmul(ps[4][:, 256:512], lhsT=masks[:, 1, :], rhs=sp_tok, start=True, stop=True),
                      reads=["sp", "masks"], writes=[kr("ps", 4)])
                S.add("act", lambda e: e.activation(out=eb, in_=ps[4][:, 0:256], func=AF.Exp), reads=[kr("ps", 4)], writes=["eb"])
                S.add("act", lambda e: e.activation(out=expf, in_=ps[4][:, 256:512], func=AF.Exp), reads=[kr("ps", 4)],
                      writes=["expf"])
                S.add("dve", lambda e: e.tensor_tensor(out=kend, in0=ps[1][:, 0:256], in1=expf, op=ALU.mult),
                      reads=[kr("ps", 1), "expf"], writes=["kend"])
                if with_output:
                    S.add("act", lambda e: e.activation(out=enb, in_=ps[4][:, 0:256], func=AF.Exp, scale=-1.0),
                          reads=[kr("ps", 4)], writes=["enb"])
                    S.add("dve", lambda e: e.scalar_tensor_tensor(
                        out=qd, in0=ps[0][:, 0:256].rearrange("p (c t) -> p c t", c=2), scalar=256.0 ** -0.5,
                        in1=eb.rearrange("p (c t) -> p c t", c=2), op0=ALU.mult, op1=ALU.mult),
                        reads=[kr("ps", 0), "eb"], writes=["qd"])
                    S.add("dve", lambda e: e.tensor_tensor(
                        out=ki, in0=ps[0][:, 256:512].rearrange("p (c t) -> p c t", c=2),
                        in1=enb.rearrange("p (c t) -> p c t", c=2), op=ALU.mult),
                        reads=[kr("ps", 0), "enb"], writes=["ki"])

                    def mma(e):
                        r = None
                        for dkc in range(2):
                            r = e.matmul(ps[5][:, 0:128], lhsT=ki[:, dkc, :], rhs=qd[:, dkc, :], start=(dkc == 0), stop=(dkc == 1))
                        return r
                    S.add("pe", mma, reads=["ki", "qd"], writes=[kr("ps", 5)])
                    S.add("dve", lambda e: e.tensor_tensor(out=am, in0=ps[5][:, 0:128], in1=masks[:, 2, :], op=ALU.mult),
                          reads=[kr("ps", 5), "masks"], writes=["am"])

                    def mmo(e):
                        e.matmul(ps[6][:], lhsT=am, rhs=vsb, start=True, stop=False)
                        e.matmul(ps[6][:], lhsT=qd[:, 0, :], rhs=Sbf[:, 0, :], start=False, stop=False)
                        return e.matmul(ps[6][:], lhsT=qd[:, 1, :], rhs=Sbf[:, 1, :], start=False, stop=True)
                    S.add("pe", mmo, reads=["am", "vsb", "qd", kr("Sbf", 0), kr("Sbf", 1)], writes=[kr("ps", 6)])

                    def mmr(e, bs=bs, wCv=wCv):
                        r = None
                        for c in range(KC):
                            r = e.matmul(ps[3][:], lhsT=xn[:, c, bs], rhs=wCv[:, c, :], start=(c == 0), stop=(c == KC - 1))
                        return r
                    S.add("pe", mmr, reads=[kr("slab", bC)] + xr, writes=[kr("ps", 3)])
                if dbg == 22:
                    break
                for dkc in range(2):
                    S.add("pe", lambda e, dkc=dkc: e.matmul(ps[7][:], lhsT=kend[:, dkc * 128:(dkc + 1) * 128], rhs=vsb,
                                                            start=True, stop=True),
                          reads=["kend", "vsb"], writes=[kr("ps", 7)])
                    if dbg == 24:
                        continue
                    dec = self.smallf[:, 34 + dkc:35 + dkc]
                    S.add("dve", lambda e, dkc=dkc, dec=dec: e.tensor_copy(out=dec, in_=eb[:, dkc * 128 + 127: dkc * 128 + 128]),
                          reads=["eb"], writes=[kr("dec", dkc)])
                    if dbg == 25:
                        continue
                    if dbg == 26:
                        S.add("dve", lambda e, dkc=dkc, dec=dec: e.tensor_tensor(
                            out=S32[:, dkc, :], in0=S32[:, dkc, :], in1=ps[7][:], op=ALU.add),
                            reads=[kr("S32", dkc), kr("ps", 7)], writes=[kr("S32", dkc)])
                        continue
                    if dbg == 28:
                        S.add("dve", lambda e, dkc=dkc, dec=dec: e.tensor_copy(out=on, in_=ps[7][:]),
                            reads=[kr("ps", 7)], writes=["on"])
                        continue
                    if dbg == 32:
                        S.add("act", lambda e, dkc=dkc, dec=dec: e.activation(out=S32[:, dkc, :], in_=gain, func=AF.Copy),
                            reads=[kr("S32", dkc), "gain"], writes=[kr("S32", dkc)])
                        continue
                    if dbg in (33, 36):
                        alt = BIG[:, 0:2048].bitcast(F32).rearrange("p (c f) -> p c f", c=2)
                        S.add("dve", lambda e, dkc=dkc, alt=alt: e.tensor_copy(out=alt[:, dkc, :], in_=gain),
                            reads=[kr("alt", dkc), "gain"], writes=[kr("alt", dkc)])
                        continue
                    if dbg == 34:
                        S.add("dve", lambda e, dkc=dkc: e.tensor_copy(out=S32[:, dkc, :], in_=gain),
                            reads=["gain"], writes=[kr("S32x", dkc)])
                        continue
                    if dbg in (31, 35):
                        S.add("dve", lambda e, dkc=dkc, dec=dec: e.tensor_copy(out=S32[:, dkc, :], in_=gain),
                            reads=[kr("S32", dkc), "gain"], writes=[kr("S32", dkc)])
                        continue
                    if dbg in (29, 30):
                        S.add("dve", lambda e, dkc=dkc, dec=dec: e.tensor_copy(out=S32[:, dkc, :], in_=on),
                            reads=[kr("S32", dkc), "on"], writes=[kr("S32", dkc)])
                        continue
                    if dbg == 27:
                        S.add("dve", lambda e, dkc=dkc, dec=dec: e.tensor_copy(out=S32[:, dkc, :], in_=ps[7][:]),
                            reads=[kr("S32", dkc), kr("ps", 7)], writes=[kr("S32", dkc)])
                        continue
                    S.add("dve", lambda e, dkc=dkc, dec=dec: e.scalar_tensor_tensor(
                        out=S32[:, dkc, :], in0=S32[:, dkc, :], scalar=dec, in1=ps[7][:],
                        op0=ALU.mult, op1=ALU.add),
                        reads=[kr("S32", dkc), kr("dec", dkc), kr("ps", 7)], writes=[kr("S32", dkc)])
                    if with_output:
                        S.add("act", lambda e, dkc=dkc: e.activation(out=Sbf[:, dkc, :], in_=S32[:, dkc, :], func=AF.Copy),
                              reads=[kr("S32", dkc)], writes=[kr("Sbf", dkc)])
                    elif dbg != 23:
                        idx = hd * 2 + dkc
                        S.add("dve", lambda e, dkc=dkc, idx=idx: e.tensor_tensor(
                            out=dcols[:, idx:idx + 1], in0=dcols[:, idx:idx + 1],
                            in1=self.smallf[:, 34 + dkc:35 + dkc], op=ALU.mult),
                            reads=[kr("dec", dkc), "dcols"], writes=["dcols"])
                if with_output:
                    S.add("act", lambda e: e.activation(out=on, in_=ps[6][:], func=AF.Square, accum_out=ssq),
                          reads=[kr("ps", 6)], writes=["on", "ssq"])
                    S.add("act", lambda e: e.activation(out=rstd, in_=ssq, func=AF.Sqrt, bias=self.epsc[:], scale=1.0 / 512),
                          reads=["ssq"], writes=["rstd"])
                    S.add("dve", lambda e: e.reciprocal(out=rstd, in_=rstd), reads=["rstd"], writes=["rstd"])
                    S.add("dve", lambda e: e.scalar_tensor_tensor(out=on, in0=ps[6][:], scalar=rstd, in1=gain,
                                                                  op0=ALU.mult, op1=ALU.mult),
                          reads=[kr("ps", 6), "rstd", "gain", "on"], writes=["on"])
                    S.add("act", lambda e: e.activation(out=sr, in_=ps[3][:], func=AF.Silu), reads=[kr("ps", 3)], writes=["sr"])
                    S.add("dve", lambda e: e.tensor_tensor(out=og, in0=on, in1=sr, op=ALU.mult), reads=["on", "sr"], writes=["og"])
                    tpv = ps[5][:].bitcast(BF16)[:, 512:1024]

                    def mmt(e, tpv=tpv):
                        r = None
                        for j in range(4):
                            r = e.transpose(tpv[:, j * 128:(j + 1) * 128], og[:, j * 128:(j + 1) * 128], self.ident[:])
                        return r
                    S.add("pe", mmt, reads=["og", "ident"], writes=[kr("ps", 5)])
                    S.add("act", lambda e, bs=bs, tpv=tpv: e.activation(
                        out=OTh[:, :, bs], in_=tpv.rearrange("p (j q) -> p j q", j=4), func=AF.Copy),
                        reads=[kr("ps", 5)], writes=[kr("OTh", blk)])
            if with_output:
                for dc in range(KC):
                    for th in range(NTH):
                        tsl = slice(th * 512, (th + 1) * 512)
                        pb = 2 + self.rot("gwo", 2)

                        def mmw(e, dc=dc, tsl=tsl, pb=pb, wOv=wOv):
                            r = None
                            for c in range(4):
                                r = e.matmul(ps[pb][:], lhsT=wOv[:, c, dc * 128:(dc + 1) * 128], rhs=OTh[:, c, tsl],
                                             start=(c == 0), stop=(c == 3))
                            return r
                        S.add("pe", mmw, reads=[kr("slab", bO)] + [kr("OTh", b_) for b_ in range(8)], writes=[kr("ps", pb)])
                        S.add("dve", lambda e, dc=dc, tsl=tsl, pb=pb: e.tensor_tensor(
                            out=hT[:, dc, tsl], in0=hT[:, dc, tsl], in1=ps[pb][:], op=ALU.add),
                            reads=[kr("ps", pb), kr("h", dc)], writes=[kr("h", dc)])
            elif dbg != 30:
                for dkc in range(2):
                    idx = hd * 2 + dkc
                    op = S.add("sp", lambda e, idx=idx, dkc=dkc: [e.dma_start(out=p["sloc"][idx], in_=S32[:, dkc, :])],
                               reads=[kr("S32", dkc)], dma_slot=("sloc", dkc))
                    self.out_ops.append(op)
        if not with_output:
            op = S.add("sp", lambda e: [e.dma_start(out=p["dtot"], in_=dcols)], reads=["dcols"], dma_slot="dtot")
            self.out_ops.append(op)

    def wo_proj(self, wo_d):
        S = self.S
        hT, OT = self.hT, self.xn
        wo_v = [self.WA[:, i * 8192:(i + 1) * 8192].rearrange("p (j c f) -> p j c f", j=4, c=KC) for i in range(2)]
        bufs = {}

        def load_wo(dg):
            b = self.rot("wo_g", 2)
            S.add("pool", lambda e, b=b, dg=dg: [e.dma_start(out=self.WA[:, b * 8192:(b + 1) * 8192], in_=wo_d[dg])],
                  writes=[kr("wog", b)], dma_slot=("wog", b))
            return b
        bufs[0] = load_wo(0)
        bufs[1] = load_wo(1)
        for dg in range(4):
            b = bufs[dg]
            for j in range(4):
                dc = dg * 4 + j
                for th in range(NTH):
                    tsl = slice(th * 512, (th + 1) * 512)
                    pb = self.rot("pwog", 2)

                    def mmw(e, b=b, j=j, tsl=tsl, pb=pb):
                        r = None
                        for c in range(KC):
                            r = e.matmul(self.psum[pb][:], lhsT=wo_v[b][:, j, c, :], rhs=OT[:, c, tsl],
                                         start=(c == 0), stop=(c == KC - 1))
                        return r
                    S.add("pe", mmw, reads=[kr("wog", b)], writes=[kr("ps", pb)])
                    S.add("dve", lambda e, dc=dc, tsl=tsl, pb=pb: e.tensor_tensor(
                        out=hT[:, dc, tsl], in0=hT[:, dc, tsl], in1=self.psum[pb][:], op=ALU.add),
                        reads=[kr("ps", pb), kr("h", dc)], writes=[kr("h", dc)])
            if dg + 2 < 4:
                bufs[dg + 2] = load_wo(dg + 2)

    def dsa_prep(self, p):
        S = self.S
        S.barrier()
        self.rmsnorm(self.hT, self.xn, T, p["g_mix"], "dp", "h", "xn")
        S.barrier()
        xn, ps = self.xn, self.psum
        wkk = self.WA[:, 0:KC * 320].rearrange("p (c f) -> p c f", c=KC)
        S.add("pool", lambda e: [e.dma_start(out=self.WA[:, 0:KC * 320], in_=p["wkk"])], writes=["wkk"], dma_slot="wkk")
        gb_ = self.MISC[:, 0:768].bitcast(F32)
        S.add("sp", lambda e: [e.dma_start(out=gb_[:, 0:256], in_=p["kvg"])], writes=["kvg"], dma_slot="kvg")
        S.add("sp", lambda e: [e.dma_start(out=gb_[:, 256:320], in_=p["ikg"])], writes=["ikg"], dma_slot="ikg")
        S.add("sp", lambda e: [e.dma_start(out=gb_[:, 320:384], in_=p["ikb"])], writes=["ikb"], dma_slot="ikb")
        junk = self.MISC[:, 768:1280].bitcast(F32)
        st = self.smallf
        for blk in range(8):
            bs = slice(blk * 128, (blk + 1) * 128)
            pb = self.rot("dpp", 2)
            ob = self.rot("dpo", 2)
            oc = self.MISC[:, 1280 + ob * 640: 1280 + ob * 640 + 512].bitcast(F32)
            ok = self.MISC[:, 1280 + ob * 640 + 512: 1280 + (ob + 1) * 640].bitcast(F32)
            xc = self.MISC[:, 2560:2688].bitcast(F32)

            def mm(e, bs=bs, pb=pb):
                r = None
                for c in range(KC):
                    r = e.matmul(ps[pb][:, 0:320], lhsT=xn[:, c, bs], rhs=wkk[:, c, :], start=(c == 0), stop=(c == KC - 1))
                return r
            S.add("pe", mm, reads=["wkk"] + [kr("xn", c) for c in range(KC)], writes=[kr("ps", pb)])
            S.add("act", lambda e, pb=pb: e.activation(out=junk, in_=ps[pb][:, 0:256], func=AF.Square, accum_out=st[:, 0:1]),
                  reads=[kr("ps", pb)], writes=["junk", "st0"])
            S.add("act", lambda e: e.activation(out=st[:, 1:2], in_=st[:, 0:1], func=AF.Sqrt, bias=self.epsc[:], scale=1.0 / 256),
                  reads=["st0"], writes=["st1"])
            S.add("dve", lambda e: e.reciprocal(out=st[:, 1:2], in_=st[:, 1:2]), reads=["st1"], writes=["st1"])
            S.add("dve", lambda e, pb=pb, oc=oc: e.scalar_tensor_tensor(out=oc, in0=ps[pb][:, 0:256], scalar=st[:, 1:2],
                                                                        in1=gb_[:, 0:256], op0=ALU.mult, op1=ALU.mult),
                  reads=[kr("ps", pb), "st1", "kvg"], writes=[kr("oc", ob)])
            op = S.add("sp", lambda e, bs=bs, oc=oc: [e.dma_start(out=p["ckv"][bs, :], in_=oc)], reads=[kr("oc", ob)],
                       dma_slot=("oc", ob))
            self.out_ops.append(op)
            S.add("dve", lambda e, pb=pb: e.tensor_reduce(out=st[:, 2:3], in_=ps[pb][:, 256:320], axis=AX.X, op=ALU.add),
                  reads=[kr("ps", pb)], writes=["st2"])
            S.add("dve", lambda e: e.tensor_scalar(out=st[:, 2:3], in0=st[:, 2:3], scalar1=1.0 / 64, scalar2=None, op0=ALU.mult),
                  reads=["st2"], writes=["st2"])
            S.add("dve", lambda e, pb=pb, xc=xc: e.tensor_scalar(out=xc, in0=ps[pb][:, 256:320], scalar1=st[:, 2:3], scalar2=None,
                                                                 op0=ALU.subtract),
                  reads=[kr("ps", pb), "st2"], writes=["xc"])
            S.add("act", lambda e, xc=xc: e.activation(out=junk[:, 0:64], in_=xc, func=AF.Square, accum_out=st[:, 3:4]),
                  reads=["xc"], writes=["junk", "st3"])
            S.add("act", lambda e: e.activation(out=st[:, 4:5], in_=st[:, 3:4], func=AF.Sqrt, bias=self.epsc[:], scale=1.0 / 64),
                  reads=["st3"], writes=["st4"])
            S.add("dve", lambda e: e.reciprocal(out=st[:, 4:5], in_=st[:, 4:5]), reads=["st4"], writes=["st4"])
            S.add("dve", lambda e, xc=xc, ok=ok: e.scalar_tensor_tensor(out=ok, in0=xc, scalar=st[:, 4:5], in1=gb_[:, 256:320],
                                                                        op0=ALU.mult, op1=ALU.mult),
                  reads=["xc", "st4", "ikg"], writes=[kr("ok", ob)])
            S.add("dve", lambda e, ok=ok: e.tensor_tensor(out=ok, in0=ok, in1=gb_[:, 320:384], op=ALU.add),
                  reads=[kr("ok", ob), "ikb"], writes=[kr("ok", ob)])
            op = S.add("sp", lambda e, bs=bs, ok=ok: [e.dma_start(out=p["kidx"][bs, :], in_=ok)], reads=[kr("ok", ob)],
                       dma_slot=("ok", ob))
            self.out_ops.append(op)

    def dsa_main(self, p, xT):
        S = self.S
        hT, xn, ps = self.hT, self.xn, self.psum
        BIG, MISC, WA, WB = self.BIG, self.MISC, self.WA, self.WB
        nk = 4 * T
        S.barrier()
        self.rmsnorm(hT, xn, T, p["g_mix"], "dm", "h", "xn")
        S.barrier()
        Hraw = hT[:, :, :].rearrange("p c t -> p (c t)")
        rawq = Hraw[:, 0:4096].rearrange("p (c t) -> p c t", c=4)
        rstdq = Hraw[:, 4096:5120]
        cqT = BIG[:, 0:4096].rearrange("p (c t) -> p c t", c=4)
        selT = BIG[:, 4096:8192].rearrange("p (k q) -> p k q", q=128)
        qabsT = BIG[:, 8192:12288].rearrange("p (h c q) -> p h c q", h=16, c=2)
        qnT = BIG[:, 12288:14336].rearrange("p (h q) -> p h q", h=16)
        qidxT = BIG[:, 14336:16384].rearrange("p (h q) -> p h q", h=16)
        wcq = WA[:, 0:8192].rearrange("p (c f) -> p c f", c=KC)
        wwi = WA[:, 8192:8448].rearrange("p (c f) -> p c f", c=KC)
        S.add("pool", lambda e: [e.dma_start(out=WA[:, 0:8192], in_=p["wcq"])], writes=["wcq"], dma_slot="wcq")
        S.add("pool", lambda e: [e.dma_start(out=WA[:, 8192:8448], in_=p["wwi"])], writes=["wwi"], dma_slot="wwi")
        gq = self.smallf[:, 0:4]
        gqa = self.smallf[:, 4:6]
        S.add("sp", lambda e: [e.dma_start(out=gq, in_=p["gq"])], writes=["gq"], dma_slot="gq")
        S.add("sp", lambda e: [e.dma_start(out=gqa, in_=p["gqa"])], writes=["gqa"], dma_slot="gqa")
        widx = MISC[:, 5888:6144].bitcast(F32)
        sqt = [MISC[:, 0:512], MISC[:, 512:1024]]
        xr = [kr("xn", c) for c in range(KC)]
        for th in range(NTH):
            tsl = slice(th * 512, (th + 1) * 512)
            for jc in range(4):
                pb = self.rot("dq", 2)

                def mm(e, jc=jc, tsl=tsl, pb=pb):
                    r = None
                    for c in range(KC):
                        r = e.matmul(ps[pb][:], lhsT=wcq[:, c, jc * 128:(jc + 1) * 128], rhs=xn[:, c, tsl],
                                     start=(c == 0), stop=(c == KC - 1))
                    return r
                S.add("pe", mm, reads=["wcq"] + xr, writes=[kr("ps", pb)])
                S.add("act", lambda e, jc=jc, tsl=tsl, pb=pb: e.activation(out=rawq[:, jc, tsl], in_=ps[pb][:], func=AF.Copy),
                      reads=[kr("ps", pb)], writes=[kr("rawq", jc, th)])
                S.add("act", lambda e, jc=jc, pb=pb: e.activation(out=sqt[jc % 2], in_=ps[pb][:], func=AF.Square),
                      reads=[kr("ps", pb)], writes=[kr("sqt", jc % 2)])
                S.add("pe", lambda e, jc=jc: e.matmul(ps[2][:], lhsT=self.ones[:], rhs=sqt[jc % 2], start=(jc == 0), stop=(jc == 3)),
                      reads=[kr("sqt", jc % 2), "ones"], writes=[kr("ps", 2)])
            S.add("act", lambda e, tsl=tsl: e.activation(out=rstdq[:, tsl], in_=ps[2][:], func=AF.Sqrt, bias=self.epsc[:],
                                                         scale=1.0 / 512), reads=[kr("ps", 2)], writes=[kr("rstdq", th)])
            S.add("dve", lambda e, tsl=tsl: e.reciprocal(out=rstdq[:, tsl], in_=rstdq[:, tsl]), reads=[kr("rstdq", th)],
                  writes=[kr("rstdq", th)])
            for jc in range(4):
                S.add("dve", lambda e, jc=jc, tsl=tsl: e.scalar_tensor_tensor(
                    out=cqT[:, jc, tsl], in0=rawq[:, jc, tsl], scalar=gq[:, jc:jc + 1], in1=rstdq[:, tsl],
                    op0=ALU.mult, op1=ALU.mult), reads=[kr("rawq", jc, th), kr("rstdq", th), "gq"], writes=[kr("cqT", jc, th)])
        for blk in range(8):
            bs = slice(blk * 128, (blk + 1) * 128)
            pb = 3

            def mmwi(e, bs=bs):
                r = None
                for c in range(KC):
                    r = e.matmul(ps[3][:, 0:16], lhsT=xn[:, c, bs], rhs=wwi[:, c, :], start=(c == 0), stop=(c == KC - 1))
                return r
            S.add("pe", mmwi, reads=["wwi"] + xr, writes=[kr("ps", 3)])
            S.add("act", lambda e, blk=blk: e.activation(out=widx[:, blk * 16:(blk + 1) * 16], in_=ps[3][:, 0:16], func=AF.Copy,
                                                         scale=16 ** -0.5 * 64 ** -0.5),
                  reads=[kr("ps", 3)], writes=[kr("widx", blk)])
        S.barrier()
        H16 = Hraw.bitcast(BF16)
        Wsc = Hraw[:, 0:4096]
        ckvT = H16[:, 8192:8192 + 2 * nk].rearrange("p (c k) -> p c k", c=2)
        ckv1 = H16[:, 16384:16384 + (nk // 128) * 260].rearrange("p (k f) -> p k f", f=260)
        kidx = H16[0:64, 24832:24832 + nk]
        dmt = [H16[:, 29184:29696], H16[:, 30208:30720]]
        nmt = [WB[:, 8192:8704], WB[:, 9216:9728]]
        S.add("pool", lambda e: [e.dma_start(out=H16[:, 8192:8192 + 2 * nk], in_=p["ckvT"])], writes=["ckvT"], dma_slot="ckvT")
        S.add("pool", lambda e: [e.dma_start(out=H16[:, 16384:16384 + (nk // 128) * 260], in_=p["ckv1"])], writes=["ckv1"],
              dma_slot="ckv1")
        S.add("pool", lambda e: [e.dma_start(out=kidx, in_=p["kidxT"])], writes=["kidx"], dma_slot="kidxT")
        wuq = WA[:, 0:8192].rearrange("p (c f) -> p c f", c=4)
        wuk = WA[:, 8192:12288].rearrange("p (h c) -> p h c", h=16)
        wiq = WA[:, 12288:16384].rearrange("p (c f) -> p c f", c=4)
        wuv = WB[:, 0:4096].rearrange("p (h c v) -> p h c v", h=16, c=2)
        S.add("pool", lambda e: [e.dma_start(out=WA[:, 0:8192], in_=p["wuq"])], writes=["wuq"], dma_slot="wuq")
        S.add("pool", lambda e: [e.dma_start(out=WA[:, 8192:12288], in_=p["wuk"])], writes=["wuk"], dma_slot="wuk")
        S.add("pool", lambda e: [e.dma_start(out=WA[:, 12288:16384], in_=p["wiq"])], writes=["wiq"], dma_slot="wiq")
        S.add("pool", lambda e: [e.dma_start(out=WB[:, 0:4096], in_=p["wuv"])], writes=["wuv"], dma_slot="wuv")
        S.barrier()
        OT = xn
        rt = [MISC[:, 0:1024].bitcast(F32), MISC[:, 1024:2048].bitcast(F32)]
        et = [MISC[:, 2048:2560], MISC[:, 2560:3072]]
        selt = MISC[:, 3072:3584]
        olat = MISC[:, 3584:4608].rearrange("p (h c) -> p h c", h=4)
        olT = MISC[:, 4608:5632].rearrange("p (h c q) -> p h c q", h=4, c=2)
        sqa = MISC[:, 5632:5888]
        mx8 = self.smallf[:, 8:16]
        den = self.smallf[:, 16:20]
        rstda = MISC[:, 0:256].bitcast(F32)
        for blk in range(8):
            bs = slice(blk * 128, (blk + 1) * 128)
            nkt = 8
            S.barrier()
            for h4 in range(4):
                pb = self.rot("qi", 2)

                def mmqi(e, h4=h4, pb=pb, bs=bs):
                    r = None
                    for jh in range(4):
                        h = h4 * 4 + jh
                        for c in range(4):
                            r = e.matmul(ps[pb][0:64, jh * 128:(jh + 1) * 128], lhsT=wiq[:, c, h * 64:(h + 1) * 64],
                                         rhs=cqT[:, c, bs], start=(c == 0), stop=(c == 3))
                    return r
                S.add("pe", mmqi, reads=["wiq"], writes=[kr("ps", pb)])
                S.add("act", lambda e, h4=h4, pb=pb: e.activation(
                    out=qidxT[0:64, h4 * 4:(h4 + 1) * 4, :], in_=ps[pb][0:64, :].rearrange("p (h q) -> p h q", h=4), func=AF.Copy),
                    reads=[kr("ps", pb)], writes=[kr("qidxT", h4)])
            for h4 in range(4):
                pb = self.rot("qi", 2)

                def mmqn(e, h4=h4, pb=pb, bs=bs):
                    r = None
                    for jh in range(4):
                        h = h4 * 4 + jh
                        for c in range(4):
                            r = e.matmul(ps[pb][:, jh * 128:(jh + 1) * 128], lhsT=wuq[:, c, h * 128:(h + 1) * 128],
                                         rhs=cqT[:, c, bs], start=(c == 0), stop=(c == 3))
                    return r
                S.add("pe", mmqn, reads=["wuq"], writes=[kr("ps", pb)])
                S.add("act", lambda e, h4=h4, pb=pb: e.activation(
                    out=qnT[:, h4 * 4:(h4 + 1) * 4, :], in_=ps[pb][:, :].rearrange("p (h q) -> p h q", h=4), func=AF.Copy),
                    reads=[kr("ps", pb)], writes=[kr("qnT", h4)])
            for h in range(16):
                pb = 2 + self.rot("qa", 2)
                pq = 4 + self.rot("qas", 2)

                def mmqa(e, h=h, pb=pb):
                    r = None
                    for cc in range(2):
                        r = e.matmul(ps[pb][:, cc * 128:(cc + 1) * 128], lhsT=wuk[:, h, cc * 128:(cc + 1) * 128], rhs=qnT[:, h, :],
                                     start=True, stop=True)
                    return r
                S.add("pe", mmqa, reads=["wuk", kr("qnT", h // 4)], writes=[kr("ps", pb)])
                S.add("act", lambda e, pb=pb: e.activation(out=sqa, in_=ps[pb][:, 0:256], func=AF.Square), reads=[kr("ps", pb)],
                      writes=["sqa"])

                def mmss(e, pq=pq):
                    e.matmul(ps[pq][:, 0:128], lhsT=self.ones[:], rhs=sqa[:, 0:128], start=True, stop=False)
                    return e.matmul(ps[pq][:, 0:128], lhsT=self.ones[:], rhs=sqa[:, 128:256], start=False, stop=True)
                S.add("pe", mmss, reads=["sqa", "ones"], writes=[kr("ps", pq)])
                S.add("act", lambda e, pq=pq: e.activation(out=rstda[:, 0:128], in_=ps[pq][:, 0:128], func=AF.Sqrt, bias=self.epsc[:],
                                                           scale=1.0 / 256), reads=[kr("ps", pq)], writes=["rstda"])
                S.add("dve", lambda e: e.reciprocal(out=rstda[:, 0:128], in_=rstda[:, 0:128]), reads=["rstda"], writes=["rstda"])
                for cc in range(2):
                    S.add("dve", lambda e, h=h, cc=cc, pb=pb: e.scalar_tensor_tensor(
                        out=qabsT[:, h, cc, :], in0=ps[pb][:, cc * 128:(cc + 1) * 128], scalar=gqa[:, cc:cc + 1],
                        in1=rstda[:, 0:128], op0=ALU.mult, op1=ALU.mult),
                        reads=[kr("ps", pb), "rstda", "gqa"], writes=[kr("qabsT", h)])
            S.barrier()
            for kt in range(nkt):
                ksl = slice(kt * 512, (kt + 1) * 512)
                for h in range(16):
                    pb = self.rot("sc", 2)
                    S.add("pe", lambda e, h=h, ksl=ksl, pb=pb: e.matmul(ps[pb][:], lhsT=qidxT[0:64, h, :], rhs=kidx[:, ksl],
                                                                        start=True, stop=True),
                          reads=["kidx", kr("qidxT", h // 4)], writes=[kr("ps", pb)])
                    S.add("act", lambda e, pb=pb: e.activation(out=rt[pb], in_=ps[pb][:], func=AF.Relu), reads=[kr("ps", pb)],
                          writes=[kr("rt", pb)])
                    wcol = widx[:, blk * 16 + h: blk * 16 + h + 1]
                    if h == 0:
                        S.add("dve", lambda e, pb=pb, ksl=ksl, wcol=wcol: e.tensor_scalar(
                            out=Wsc[:, ksl], in0=rt[pb], scalar1=wcol, scalar2=None, op0=ALU.mult),
                            reads=[kr("rt", pb), kr("widx", blk)], writes=[kr("W", kt)])
                    else:
                        S.add("dve", lambda e, pb=pb, ksl=ksl, wcol=wcol: e.scalar_tensor_tensor(
                            out=Wsc[:, ksl], in0=rt[pb], scalar=wcol, in1=Wsc[:, ksl], op0=ALU.mult, op1=ALU.add),
                            reads=[kr("rt", pb), kr("widx", blk), kr("W", kt)], writes=[kr("W", kt)])
                mi = self.rot("nmt", 2)
                S.add("pool", lambda e, mi=mi, blk=blk, kt=kt: [e.dma_start(out=nmt[mi], in_=p["negm"][blk, kt])],
                      writes=[kr("nmt", mi)], dma_slot=("nmt", mi))
                S.add("dve", lambda e, ksl=ksl, mi=mi: e.tensor_tensor(out=Wsc[:, ksl], in0=Wsc[:, ksl], in1=nmt[mi], op=ALU.min),
                      reads=[kr("W", kt), kr("nmt", mi)], writes=[kr("W", kt)])
            wkeys = [kr("W", kt) for kt in range(nkt)]
            Wv = Wsc[:, 0:nkt * 512]
            for it in range(32):
                S.add("dve", lambda e, Wv=Wv: e.max(out=mx8, in_=Wv), reads=wkeys, writes=["mx8"])
                S.add("dve", lambda e, Wv=Wv: e.match_replace(out=Wv, in_to_replace=mx8, in_values=Wv, imm_value=-3.0e38),
                      reads=wkeys + ["mx8"], writes=wkeys)
            for kt in range(nkt):
                ksl = slice(kt * 512, (kt + 1) * 512)
                S.add("dve", lambda e, ksl=ksl: e.tensor_scalar(out=selt, in0=Wsc[:, ksl], scalar1=-2.0e38, scalar2=None, op0=ALU.is_le),
                      reads=[kr("W", kt)], writes=["selt"])
                di = self.rot("dmt", 2)
                S.add("pool", lambda e, di=di, blk=blk, kt=kt: [e.dma_start(out=dmt[di], in_=p["dmk"][blk, kt])],
                      writes=[kr("dmt", di)], dma_slot=("dmt", di))
                S.add("dve", lambda e, di=di: e.tensor_tensor(out=selt, in0=selt, in1=dmt[di], op=ALU.mult),
                      reads=["selt", kr("dmt", di)], writes=["selt"])
                nb = 4
                tp_i = 2 + self.rot("st", 2)
                tpv = ps[tp_i][:].bitcast(BF16)

                def mmt(e, nb=nb, tpv=tpv):
                    r = None
                    for i in range(nb):
                        r = e.transpose(tpv[:, i * 128:(i + 1) * 128], selt[:, i * 128:(i + 1) * 128], self.ident[:])
                    return r
                S.add("pe", mmt, reads=["selt", "ident"], writes=[kr("ps", tp_i)])
                S.add("act", lambda e, kt=kt, nb=nb, tpv=tpv: e.activation(
                    out=selT[:, kt * 4:kt * 4 + nb, :], in_=tpv[:, 0:nb * 128].rearrange("p (k q) -> p k q", q=128), func=AF.Copy),
                    reads=[kr("ps", tp_i)], writes=[kr("selT", kt)])
            S.barrier()
            for hg in range(4):
                for kb in range(32):
                    kbs = slice(kb * 128, (kb + 1) * 128)
                    pb = self.rot("lg", 2)

                    def mml(e, hg=hg, kbs=kbs, pb=pb):
                        r = None
                        for jh in range(4):
                            for cc in range(2):
                                r = e.matmul(ps[pb][:, jh * 128:(jh + 1) * 128], lhsT=ckvT[:, cc, kbs], rhs=qabsT[:, hg * 4 + jh, cc, :],
                                             start=(cc == 0), stop=(cc == 1))
                        return r
                    S.add("pe", mml, reads=["ckvT"], writes=[kr("ps", pb)])
                    S.add("act", lambda e, pb=pb: e.activation(out=et[pb], in_=ps[pb][:], func=AF.Exp, scale=256.0 ** -0.5),
                          reads=[kr("ps", pb)], writes=[kr("et", pb)])
                    S.add("pool", lambda e, pb=pb, kb=kb: e.tensor_tensor(
                        out=et[pb].rearrange("p (j q) -> p j q", j=4), in0=et[pb].rearrange("p (j q) -> p j q", j=4),
                        in1=selT[:, kb:kb + 1, :].to_broadcast([128, 4, 128]), op=ALU.mult),
                        reads=[kr("et", pb)], writes=[kr("et", pb)])

                    def mmo(e, pb=pb, kb=kb, gb=31):
                        r = None
                        for jh in range(4):
                            r = e.matmul(ps[4 + jh][:, 0:257], lhsT=et[pb][:, jh * 128:(jh + 1) * 128], rhs=ckv1[:, kb, 0:257],
                                         start=(kb == 0), stop=(kb == gb))
                        return r
                    S.add("pe", mmo, reads=[kr("et", pb), "ckv1"], writes=[kr("ps", 4 + jh_) for jh_ in range(4)])
                for jh in range(4):
                    S.add("dve", lambda e, jh=jh: e.reciprocal(out=den[:, jh:jh + 1], in_=ps[4 + jh][:, 256:257]),
                          reads=[kr("ps", 4 + jh)], writes=[kr("den", jh)])
                    S.add("dve", lambda e, jh=jh: e.tensor_scalar(out=olat[:, jh, :], in0=ps[4 + jh][:, 0:256], scalar1=den[:, jh:jh + 1],
                                                                 scalar2=None, op0=ALU.mult),
                          reads=[kr("ps", 4 + jh), kr("den", jh)], writes=[kr("olat", jh)])
                tpv = ps[2][:].bitcast(BF16)

                def mmt2(e, tpv=tpv):
                    r = None
                    for jh in range(4):
                        for cc in range(2):
                            r = e.transpose(tpv[:, (jh * 2 + cc) * 128:(jh * 2 + cc + 1) * 128], olat[:, jh, cc * 128:(cc + 1) * 128],
                                            self.ident[:])
                    return r
                S.add("pe", mmt2, reads=[kr("olat", jh_) for jh_ in range(4)] + ["ident"], writes=[kr("ps", 2)])
                S.add("act", lambda e, tpv=tpv: e.activation(out=olT, in_=tpv.rearrange("p (h c q) -> p h c q", h=4, c=2), func=AF.Copy),
                      reads=[kr("ps", 2)], writes=["olT"])

                def mmv(e, hg=hg):
                    r = None
                    for jh in range(4):
                        for cc in range(2):
                            r = e.matmul(ps[3][:, jh * 128:(jh + 1) * 128], lhsT=wuv[:, hg * 4 + jh, cc, :], rhs=olT[:, jh, cc, :],
                                         start=(cc == 0), stop=(cc == 1))
                    return r
                S.add("pe", mmv, reads=["olT", "wuv"], writes=[kr("ps", 3)])
                S.add("act", lambda e, hg=hg, bs=bs: e.activation(
                    out=OT[:, hg * 4:(hg + 1) * 4, bs], in_=ps[3][:, :].rearrange("p (h q) -> p h q", h=4), func=AF.Copy),
                    reads=[kr("ps", 3)], writes=[kr("OTd", blk, hg)])
        S.barrier()
        self.load_h(xT)
        S.barrier()
        self.wo_proj(p["wo"])

    def finish(self):
        S = self.S
        nc = self.nc
        with nc.Block() as block:
            @block.tensor
            def _(e):
                S.emit("pe", e)

            @block.scalar
            def _(e):
                S.emit("act", e)

            @block.vector
            def _(e):
                S.emit("dve", e)

            @block.gpsimd
            def _(e):
                S.emit("pool", e)

            @block.sync
            def _(e):
                S.emit("sp", e)
                S.final_wait(e, self.out_ops)
                import os
                if os.environ.get("GLA_FW"):
                    for en in os.environ["GLA_FW"].split(","):
                        cands = [o for o in S.ops[en] if not o.is_dma]
                        if cands:
                            S.final_wait(e, [cands[-1]])
        self.stack.close()
        return nc


def lay_vec(g):
    return np.ascontiguousarray(np.asarray(g, np.float32).reshape(KC, 128).T)


def lay_kslab(w, ncol):
    w = np.asarray(w, np.float32)
    K_, N_ = w.shape
    ns = N_ // ncol
    out = w.reshape(K_ // 128, 128, ns, ncol).transpose(2, 1, 0, 3).reshape(ns, 128, (K_ // 128) * ncol)
    return np.ascontiguousarray(out)


def lay_wgu(w, GW=256):
    w = np.asarray(w, np.float32)
    nslab = DFF // GW
    g = w[:, :DFF].reshape(KC, 128, nslab, GW)
    u = w[:, DFF:].reshape(KC, 128, nslab, GW)
    gu = np.stack([g, u], axis=3)
    out = gu.transpose(2, 1, 0, 3, 4).reshape(nslab, 128, KC * 2 * GW)
    return np.ascontiguousarray(out)


def lay_wd(w, GRP=4):
    w = np.asarray(w, np.float32)
    ngrp = NFC // GRP
    out = w.reshape(ngrp, GRP, 128, D).transpose(0, 2, 1, 3).reshape(ngrp, 128, GRP * D)
    return np.ascontiguousarray(out)


def lay_wo(w):
    w = np.asarray(w, np.float32)
    out = w.reshape(KC, 128, 4, 4, 128).transpose(2, 1, 3, 0, 4).reshape(4, 128, 4 * KC * 128)
    return np.ascontiguousarray(out)


def swa_host_inputs(pre, inp, halo_rows, first):
    w_in = np.asarray(inp[pre + "w_in"], np.float32)
    wq = w_in[:, :2048]
    wk = w_in[:, 2048:2304]
    wv = w_in[:, 2304:2560]
    wkp = np.zeros((2048, 8, 128), np.float32)
    for hk in range(4):
        wkp[:, hk * 2 + 0, 0:64] = wk[:, hk * 64:(hk + 1) * 64]
        wkp[:, hk * 2 + 1, 64:128] = wk[:, hk * 64:(hk + 1) * 64]
    s_idx = np.arange(128)[:, None]
    q_idx = np.arange(128)[None, :]
    mprev = (s_idx > q_idx).astype(np.float32)
    mcur = (s_idx <= q_idx).astype(np.float32)
    m0 = np.zeros_like(mprev) if first else mprev
    masks = np.stack([m0, mprev, mcur], axis=1).reshape(128, 3 * 128)
    qkn = np.stack([np.tile(np.asarray(inp[pre + "q_norm"], np.float32), 2),
                    np.tile(np.asarray(inp[pre + "k_norm"], np.float32), 2)], axis=1)
    return {
        "g_mix": lay_vec(inp[pre + "norm_mix"]),
        "haloT": np.ascontiguousarray(halo_rows.T),
        "wq": lay_kslab(wq, 512),
        "wk": lay_kslab(wkp.reshape(2048, 1024), 512),
        "wv": lay_kslab(wv, 256)[0],
        "wo": lay_wo(inp[pre + "wo"]),
        "qkn": np.ascontiguousarray(qkn),
        "sinks": np.ascontiguousarray(np.broadcast_to(np.asarray(inp[pre + "sinks"], np.float32)[None, :], (128, 32))),
        "masks": np.ascontiguousarray(masks),
    }


def swa_dram(B, pfx):
    return {
        "g_mix": B.din(pfx + "g_mix", [128, KC]),
        "haloT": B.din(pfx + "haloT", [D, 128]),
        "wq": B.din(pfx + "wq", [4, 128, KC * 512]),
        "wk": B.din(pfx + "wk", [2, 128, KC * 512]),
        "wv": B.din(pfx + "wv", [128, KC * 256]),
        "wo": B.din(pfx + "wo", [4, 128, 4 * KC * 128]),
        "qkn": B.din(pfx + "qkn", [128, 2]),
        "sinks": B.din(pfx + "sinks", [128, 32]),
        "masks": B.din(pfx + "masks", [128, 3 * 128]),
    }


def gla_host_inputs(pre, inp, sprev, dprev):
    w_in = np.asarray(inp[pre + "w_in"], np.float32)
    q = w_in[:, 0:1024]; k = w_in[:, 1024:2048]; v = w_in[:, 2048:4096]; r = w_in[:, 4096:6144]; gl = w_in[:, 6144:6160]
    wA = np.concatenate([np.concatenate([q[:, h * 256:(h + 1) * 256], k[:, h * 256:(h + 1) * 256]], axis=1) for h in range(4)], axis=1)
    wgb = np.zeros((64, 1024), np.float32)
    wgb[0:16] = np.asarray(inp[pre + "w_gate_b"], np.float32)
    wgb[32] = np.asarray(inp[pre + "gate_bias"], np.float32)
    s_idx = np.arange(128)[:, None]; t_idx = np.arange(128)[None, :]
    uneg = np.where(s_idx <= t_idx, -1.0 / 16, 0.0).astype(np.float32)
    ugt = np.where(s_idx > t_idx, -1.0 / 16, 0.0).astype(np.float32)
    cz = (s_idx <= t_idx).astype(np.float32)
    masks = np.stack([uneg, ugt, cz], axis=1).reshape(128, 384)
    wo = np.asarray(inp[pre + "wo"], np.float32)
    woh = wo.reshape(4, 4, 128, 2048).transpose(0, 2, 1, 3).reshape(4, 128, 4 * 2048)
    return {
        "g_mix": lay_vec(inp[pre + "norm_mix"]),
        "wA": lay_kslab(wA, 512), "wB": lay_kslab(v, 512), "wC": lay_kslab(r, 512),
        "wgl": lay_kslab(gl, 16)[0], "wgb": wgb,
        "onorm": np.ascontiguousarray(np.broadcast_to(np.asarray(inp[pre + "o_norm"], np.float32)[None, :], (128, 512))),
        "woh": np.ascontiguousarray(woh), "masks": np.ascontiguousarray(masks),
        "sprev": np.ascontiguousarray(sprev, dtype=np.float32), "dprev": np.ascontiguousarray(dprev, dtype=np.float32),
    }


def gla_dram(B, pfx, with_output):
    d = {
        "g_mix": B.din(pfx + "g_mix", [128, KC]),
        "wA": B.din(pfx + "wA", [4, 128, KC * 512]), "wB": B.din(pfx + "wB", [4, 128, KC * 512]),
        "wgl": B.din(pfx + "wgl", [128, KC * 16]), "wgb": B.din(pfx + "wgb", [64, 1024]),
        "masks": B.din(pfx + "masks", [128, 384]),
        "onorm": B.din(pfx + "onorm", [128, 512]),
        "dprev": B.din(pfx + "dprev", [3, 128, 8]),
    }
    if with_output:
        d["wC"] = B.din(pfx + "wC", [4, 128, KC * 512])
        d["woh"] = B.din(pfx + "woh", [4, 128, 4 * 2048])
        d["sprev"] = B.din(pfx + "sprev", [3, 8, 128, 512])
    else:
        d["sloc"] = B.dout(pfx + "sloc", [8, 128, 512])
        d["dtot"] = B.dout(pfx + "dtot", [128, 8])
    return d


def dsa_prep_host(pre, inp):
    w_in = np.asarray(inp[pre + "w_in"], np.float32)
    wkk = np.concatenate([w_in[:, 512:768], w_in[:, 768:832]], axis=1)
    bc = lambda v, n: np.ascontiguousarray(np.broadcast_to(np.asarray(v, np.float32)[None, :], (128, n)))
    return {"g_mix": lay_vec(inp[pre + "norm_mix"]), "wkk": lay_kslab(wkk, 320)[0],
            "kvg": bc(inp[pre + "kv_norm"], 256), "ikg": bc(inp[pre + "idx_k_norm"], 64), "ikb": bc(inp[pre + "idx_k_bias"], 64)}


def dsa_prep_dram(B, pfx):
    return {"g_mix": B.din(pfx + "g_mix", [128, KC]), "wkk": B.din(pfx + "wkk", [128, KC * 320]),
            "kvg": B.din(pfx + "kvg", [128, 256]), "ikg": B.din(pfx + "ikg", [128, 64]), "ikb": B.din(pfx + "ikb", [128, 64]),
            "ckv": B.dout(pfx + "ckv", [T, 256]), "kidx": B.dout(pfx + "kidx", [T, 64])}


def dsa_main_host(pre, inp, ckv_seq, kidx_seq):
    nk = 4 * T
    w_in = np.asarray(inp[pre + "w_in"], np.float32)
    ck = np.asarray(ckv_seq[:nk], np.float32)
    ckvT = np.ascontiguousarray(ck.T.reshape(2, 128, nk).transpose(1, 0, 2).reshape(128, 2 * nk))
    ckv1 = np.zeros((nk // 128, 128, 260), np.float32)
    ckv1[:, :, 0:256] = ck.reshape(nk // 128, 128, 256)
    ckv1[:, :, 256] = 1.0
    ckv1 = np.ascontiguousarray(ckv1.transpose(1, 0, 2).reshape(128, (nk // 128) * 260))
    kidxT = np.ascontiguousarray(np.asarray(kidx_seq[:nk], np.float32).T)
    w_uk = np.asarray(inp[pre + "w_uk"], np.float32)
    w_uv = np.asarray(inp[pre + "w_uv"], np.float32)
    w_uq = np.asarray(inp[pre + "w_uq"], np.float32)
    w_iq = np.asarray(inp[pre + "w_iq"], np.float32)
    return {
        "g_mix": lay_vec(inp[pre + "norm_mix"]),
        "wcq": lay_kslab(w_in[:, 0:512], 512)[0], "wwi": lay_kslab(w_in[:, 832:848], 16)[0],
        "gq": np.ascontiguousarray(np.asarray(inp[pre + "q_lat_norm"], np.float32).reshape(4, 128).T),
        "gqa": np.ascontiguousarray(np.asarray(inp[pre + "q_abs_norm"], np.float32).reshape(2, 128).T),
        "wuq": lay_kslab(w_uq, 2048)[0], "wiq": lay_kslab(w_iq, 1024)[0],
        "wuk": np.ascontiguousarray(w_uk.transpose(1, 0, 2).reshape(128, 16 * 256)),
        "wuv": np.ascontiguousarray(w_uv.reshape(16, 2, 128, 128).transpose(2, 0, 1, 3).reshape(128, 16 * 2 * 128)),
        "wo": lay_wo(inp[pre + "wo"]),
        "ckvT": ckvT, "ckv1": ckv1, "kidxT": kidxT,
    }


def dsa_masks_host(j):
    q = np.arange(128)[:, None]
    kcol = np.arange(512)[None, :]
    diag = [((kcol // 128 < d) | ((kcol // 128 == d) & (kcol % 128 <= q))).astype(np.float32) for d in range(4)]
    ones = np.ones((128, 512), np.float32)
    zeros = np.zeros((128, 512), np.float32)
    dmk = np.empty((8, 8, 128, 512), np.float32)
    for blk in range(8):
        gb = j * 8 + blk
        for kt in range(8):
            dmk[blk, kt] = ones if kt < gb // 4 else (diag[gb % 4] if kt == gb // 4 else zeros)
    negm = np.where(dmk > 0, np.float32(3e38), np.float32(-1e30)).astype(np.float32)
    return {"dmk": dmk, "negm": negm}


def dsa_main_dram(B, pfx):
    nk = 4 * T
    return {"g_mix": B.din(pfx + "g_mix", [128, KC]), "wcq": B.din(pfx + "wcq", [128, KC * 512]),
            "wwi": B.din(pfx + "wwi", [128, KC * 16]), "gq": B.din(pfx + "gq", [128, 4]), "gqa": B.din(pfx + "gqa", [128, 2]),
            "wuq": B.din(pfx + "wuq", [128, 4 * 2048]), "wiq": B.din(pfx + "wiq", [128, 4 * 1024]),
            "wuk": B.din(pfx + "wuk", [128, 16 * 256]), "wuv": B.din(pfx + "wuv", [128, 16 * 2 * 128]),
            "wo": B.din(pfx + "wo", [4, 128, 4 * KC * 128]),
            "ckvT": B.din(pfx + "ckvT", [128, 2 * nk]), "ckv1": B.din(pfx + "ckv1", [128, (nk // 128) * 260]),
            "kidxT": B.din(pfx + "kidxT", [64, nk]), "dmk": B.din(pfx + "dmk", [8, 8, 128, 512]),
            "negm": B.din(pfx + "negm", [8, 8, 128, 512])}


def ffn_dram(B, pfx):
    return {"g": B.din(pfx + "g", [128, KC]), "wgu": B.din(pfx + "wgu", [NFC // 2, 128, KC * 2 * 256]),
            "wd": B.din(pfx + "wd", [NFC // 4, 128, 4 * D])}


def ffn_host_inputs(pre, inp):
    return {"g": lay_vec(inp[pre + "norm_ffn"]), "wgu": lay_wgu(inp[pre + "w_gate_up"]), "wd": lay_wd(inp[pre + "w_down"])}


IDENT = np.eye(128, dtype=np.float32)


def _prog(stages_fn, extra_out=None):
    B = Builder()
    xT = B.din("xT", [D, T])
    ident = B.din("ident", [128, 128])
    outT = B.dout("outT", [D, T])
    B.alloc()
    B.load_ident(ident)
    B.load_h(xT)
    stages_fn(B, xT)
    B.S.barrier()
    B.store_h(outT)
    return B.finish()


def _run(nc, in_maps):
    n = len(in_maps)
    res = run_bass_kernel_spmd(nc, in_maps, core_ids=list(range(n)))
    return res.results


def _pfx(p, d):
    return {p + k: v for k, v in d.items()}


def kernel(**inputs):
    inp = {k: np.asarray(v) for k, v in inputs.items()}
    h = np.ascontiguousarray(inp["x"], dtype=np.float32)
    cores = [(c // 4, c % 4) for c in range(NCORES)]

    def own(hh, b, j):
        return np.ascontiguousarray(hh[b, j * T:(j + 1) * T].T)

    def gather(results, key="outT"):
        out = np.empty((2, 4 * T, D), np.float32)
        for c, (b, j) in enumerate(cores):
            out[b, j * T:(j + 1) * T] = results[c][key].T
        return out

    def swa_maps(li, hh):
        pre = f"l{li}_"
        shared = None
        maps = []
        for (b, j) in cores:
            halo = hh[b, j * T - 128:j * T] if j > 0 else np.zeros((128, D), np.float32)
            hi = swa_host_inputs(pre, inp, halo, j == 0)
            if shared is None:
                shared = hi
            else:
                for k in ("wq", "wk", "wv", "wo", "g_mix", "qkn", "sinks"):
                    hi[k] = shared[k]
            maps.append(_pfx("a_", hi))
        return maps

    zs = np.zeros((3, 8, 128, 512), np.float32)
    zd = np.ones((3, 128, 8), np.float32)
    g1 = gla_host_inputs("l1_", inp, zs, zd)
    g1p = {k: v for k, v in g1.items() if k not in ("wC", "woh", "sprev")}
    f0 = ffn_host_inputs("l0_", inp)

    def stA(B, xT):
        P = swa_dram(B, "a_")
        F = ffn_dram(B, "f_")
        G = gla_dram(B, "b_", False)
        B.swa(P)
        B.ffn(F["g"], F["wgu"], F["wd"])
        B.gla(G, False)
    ncA = _prog(stA)
    sm = swa_maps(0, h)
    maps = []
    for c, (b, j) in enumerate(cores):
        im = {"xT": own(h, b, j), "ident": IDENT}
        im.update(sm[c]); im.update(_pfx("f_", f0)); im.update(_pfx("b_", g1p))
        maps.append(im)
    rA = _run(ncA, maps)
    h = gather(rA)
    del f0, sm, maps

    f1 = ffn_host_inputs("l1_", inp)
    dp = dsa_prep_host("l2_", inp)

    def stB(B, xT):
        G = gla_dram(B, "b_", True)
        F = ffn_dram(B, "f_")
        Pp = dsa_prep_dram(B, "c_")
        B.gla(G, True)
        B.ffn(F["g"], F["wgu"], F["wd"])
        B.dsa_prep(Pp)
    ncB = _prog(stB)
    maps = []
    for c, (b, j) in enumerate(cores):
        sp = np.zeros((3, 8, 128, 512), np.float32)
        dpv = np.ones((3, 128, 8), np.float32)
        for s_ in range(3):
            i = j - 3 + s_
            if i >= 0:
                sp[s_] = rA[b * 4 + i]["b_sloc"]
                dpv[s_] = rA[b * 4 + i]["b_dtot"]
        gi = dict(g1)
        gi["sprev"] = sp
        gi["dprev"] = dpv
        im = {"xT": own(h, b, j), "ident": IDENT}
        im.update(_pfx("b_", gi)); im.update(_pfx("f_", f1)); im.update(_pfx("c_", dp))
        maps.append(im)
    rB = _run(ncB, maps)
    h = gather(rB)
    ckv = [np.concatenate([rB[b * 4 + j]["c_ckv"] for j in range(4)], 0) for b in range(2)]
    kidx = [np.concatenate([rB[b * 4 + j]["c_kidx"] for j in range(4)], 0) for b in range(2)]
    del f1, g1, g1p, maps, rA

    f2 = ffn_host_inputs("l2_", inp)

    def stC(B, xT):
        P = dsa_main_dram(B, "c_")
        F = ffn_dram(B, "f_")
        B.dsa_main(P, xT)
        B.ffn(F["g"], F["wgu"], F["wd"])
    ncC = _prog(stC)
    hm = [dsa_main_host("l2_", inp, ckv[b], kidx[b]) for b in range(2)]
    for k in ("g_mix", "wcq", "wwi", "gq", "gqa", "wuq", "wiq", "wuk", "wuv", "wo"):
        hm[1][k] = hm[0][k]
    mk = [dsa_masks_host(j) for j in range(4)]
    maps = []
    for c, (b, j) in enumerate(cores):
        im = {"xT": own(h, b, j), "ident": IDENT}
        im.update(_pfx("c_", hm[b])); im.update(_pfx("c_", mk[j])); im.update(_pfx("f_", f2))
        maps.append(im)
    rC = _run(ncC, maps)
    h = gather(rC)
    del f2, hm, mk, maps

    f3 = ffn_host_inputs("l3_", inp)

    def stD(B, xT):
        P = swa_dram(B, "a_")
        F = ffn_dram(B, "f_")
        B.swa(P)
        B.ffn(F["g"], F["wgu"], F["wd"])
    ncD = _prog(stD)
    sm = swa_maps(3, h)
    maps = []
    for c, (b, j) in enumerate(cores):
        im = {"xT": own(h, b, j), "ident": IDENT}
        im.update(sm[c]); im.update(_pfx("f_", f3))
        maps.append(im)
    rD = _run(ncD, maps)
    return gather(rD)
```

```python
import numpy as np
from contextlib import ExitStack
import concourse.bass as bass
import concourse.mybir as mybir
from concourse.bass_utils import run_bass_kernel_spmd

F32 = mybir.dt.float32
BF16 = mybir.dt.bfloat16
AF = mybir.ActivationFunctionType
ALU = mybir.AluOpType
AX = mybir.AxisListType

D = 2048
KC = D // 128
T = 1024
NTH = T // 512
DFF = 5632
NFC = DFF // 128
EPS = 1e-6
NCORES = 8


DBG_WAITS = None


class Op:
    __slots__ = ("eng", "fn", "deps", "ticket", "is_dma", "ndma", "slot")


class Sched:
    ENGS = ("pe", "act", "dve", "pool", "sp")

    def __init__(self, nc, stack):
        self.nc = nc
        self.stack = stack
        self.ops = {e: [] for e in self.ENGS}
        self.last_w = {}
        self.readers = {}
        self.cnt = {e: 0 for e in self.ENGS}
        self.eng_sem = {}
        self.epoch = 0
        self.slot_sem = {}
        self.slot_cnt = {}
        self.nsem = 0
        self.bar_deps = []
        self.all_dma = []
        self.new_epoch()

    def _sem(self, name):
        self.nsem += 1
        return self.stack.enter_context(self.nc.semaphore(name))

    def new_epoch(self):
        self.epoch += 1
        for e in ("pe", "act", "dve", "pool"):
            self.eng_sem[e] = self._sem(f"s_{e}_{self.epoch}")
            self.cnt[e] = 0

    def add(self, eng, fn, reads=(), writes=(), dma_slot=None, ndma=1):
        op = Op()
        op.eng = eng
        op.fn = fn
        op.is_dma = dma_slot is not None
        op.ndma = ndma
        deps = list(self.bar_deps)
        for r in reads:
            w = self.last_w.get(r)
            if w is not None:
                deps.append(w)
        for w_ in writes:
            w = self.last_w.get(w_)
            if w is not None:
                deps.append(w)
            deps.extend(self.readers.get(w_, ()))
        if op.is_dma:
            if dma_slot not in self.slot_sem:
                self.slot_sem[dma_slot] = self._sem(f"d_{len(self.slot_sem)}")
                self.slot_cnt[dma_slot] = 0
            self.slot_cnt[dma_slot] += 16 * ndma
            op.ticket = (self.slot_sem[dma_slot], self.slot_cnt[dma_slot])
        else:
            self.cnt[eng] += 1
            op.ticket = (self.eng_sem[eng], self.cnt[eng])
        seen = set()
        dd = []
        for d_ in deps:
            if d_ is op or id(d_) in seen:
                continue
            seen.add(id(d_))
            if d_.eng == "pe" and eng == "pe" and not d_.is_dma:
                continue
            dd.append(d_)
        op.deps = dd
        for r in reads:
            self.readers.setdefault(r, []).append(op)
        for w_ in writes:
            self.last_w[w_] = op
            self.readers[w_] = []
        self.ops[eng].append(op)
        if op.is_dma:
            self.all_dma.append(op)
        return op

    def barrier(self):
        deps = []
        for e in self.ENGS:
            for op in reversed(self.ops[e]):
                if not op.is_dma:
                    deps.append(op)
                    break
        latest = {}
        for op in self.all_dma:
            latest[id(op.ticket[0])] = op
        deps.extend(latest.values())
        self.bar_deps = deps
        self.last_w = {}
        self.readers = {}

    def emit(self, eng, e):
        waited = {}
        for op in self.ops[eng]:
            need = {}
            for d_ in op.deps:
                s, v = d_.ticket
                k = id(s)
                if waited.get(k, 0) >= v:
                    continue
                if k not in need or need[k][1] < v:
                    need[k] = (s, v)
            for k, (s, v) in need.items():
                e.wait_ge(s, v)
                waited[k] = v
                if DBG_WAITS is not None:
                    DBG_WAITS.append((eng, self.ops[eng].index(op), getattr(s, "name", str(s)), v))
            res = op.fn(e)
            s, v = op.ticket
            if op.is_dma:
                assert isinstance(res, (list, tuple)) and len(res) == op.ndma, (len(res), op.ndma)
                for ins in res:
                    ins.then_inc(s, 16)
            else:
                if isinstance(res, (list, tuple)):
                    res = res[-1]
                res.then_inc(s, 1)

    def final_wait(self, e, ops):
        for op in ops:
            s, v = op.ticket
            e.wait_ge(s, v)


def kr(name, *idx):
    return (name,) + idx


HD_A = 64
NQ_A = 32
NKV_A = 4
TK = T + 128


class Builder:
    GW = 256
    GRP = 4

    def __init__(self):
        self.nc = bass.Bass("TRN2", target_bir_lowering=False)
        self.stack = ExitStack()
        self.S = Sched(self.nc, self.stack)
        self.out_ops = []
        self.cnt = {}

    def din(self, name, shape, dt=F32):
        return self.nc.dram_tensor(name, list(shape), dt, kind="ExternalInput").ap()

    def dout(self, name, shape, dt=F32):
        return self.nc.dram_tensor(name, list(shape), dt, kind="ExternalOutput").ap()

    def sb(self, name, shape, dt):
        return self.stack.enter_context(self.nc.sbuf_tensor(name, list(shape), dt))

    def ps(self, name, shape=(128, 512), dt=F32):
        return self.stack.enter_context(self.nc.psum_tensor(name, list(shape), dt))

    def rot(self, name, n):
        v = self.cnt.get(name, 0)
        self.cnt[name] = v + 1
        return v % n

    def alloc(self):
        self.hT = self.sb("hT", [128, KC, T], F32)
        self.xn = self.sb("xn", [128, KC, T], BF16)
        self.BIG = self.sb("BIG", [128, 16384], BF16)
        self.WA = self.sb("WA", [128, 16384], BF16)
        self.WB = self.sb("WB", [128, 16384], BF16)
        self.MISC = self.sb("MISC", [128, 6144], BF16)
        self.sb_masks = self.sb("masks", [128, 384], BF16)
        self.ones = self.sb("ones", [128, 128], BF16)
        self.bones = self.sb("bones", [128, 128], BF16)
        self.ident = self.sb("ident_sb", [128, 128], BF16)
        self.epsc = self.sb("epsc", [128, 1], F32)
        self.gcol = self.sb("gcolt", [128, KC], F32)
        self.smallf = self.sb("smallf", [128, 64], F32)
        self.smallf2 = self.sb("smallf2", [128, 128], F32)
        self.psum = [self.ps(f"ps{i}") for i in range(8)]
        S = self.S
        S.add("dve", lambda e: e.memset(self.ones[:], 1.0), writes=["ones"])
        S.add("dve", lambda e: e.memset(self.epsc[:], EPS), writes=["epsc"])
        S.add("dve", lambda e: e.memset(self.bones[:], 0.0), writes=["bones"])
        S.add("dve", lambda e: e.memset(self.bones[0:64, 0:64], 1.0), reads=["bones"], writes=["bones"])
        S.add("dve", lambda e: e.memset(self.bones[64:128, 64:128], 1.0), reads=["bones"], writes=["bones"])

    def load_ident(self, ident_d):
        self.S.add("pool", lambda e: [e.dma_start(out=self.ident[:], in_=ident_d)], writes=["ident"],
                   dma_slot="ident")

    def load_h(self, xT):
        S = self.S
        src = xT.rearrange("(c p) t -> p c t", p=128)
        for q in range(4):
            sl = slice(q * 4, (q + 1) * 4)
            S.add("sp", lambda e, sl=sl: [e.dma_start(out=self.hT[:, sl, :], in_=src[:, sl, :])],
                  writes=[kr("h", c) for c in range(q * 4, q * 4 + 4)], dma_slot=("ldh", q))

    def store_h(self, outT):
        S = self.S
        dst = outT.rearrange("(c p) t -> p c t", p=128)
        for q in range(4):
            sl = slice(q * 4, (q + 1) * 4)
            op = S.add("sp", lambda e, sl=sl: [e.dma_start(out=dst[:, sl, :], in_=self.hT[:, sl, :])],
                       reads=[kr("h", c) for c in range(q * 4, q * 4 + 4)], dma_slot=("sth", q))
            self.out_ops.append(op)

    def rmsnorm(self, src, dst, n, gvec_d, tag, skey, dkey):
        S = self.S
        rstd = self.BIG[:, 0:2 * n].bitcast(F32)
        sq = [self.BIG[:, 2 * n + i * n: 2 * n + (i + 1) * n] for i in range(2)]
        S.add("sp", lambda e: [e.dma_start(out=self.gcol[:], in_=gvec_d)], writes=["gcol"], dma_slot="gcol")
        nh = (n + 511) // 512
        w = n // nh
        pss = [self.psum[6], self.psum[7]]
        for c in range(KC):
            S.add("act", lambda e, c=c: e.activation(out=sq[c % 2], in_=src[:, c, :], func=AF.Square),
                  reads=[kr(skey, c)], writes=[kr(tag + "sq", c % 2)])
            for th in range(nh):
                S.add("pe", lambda e, c=c, th=th: e.matmul(
                    pss[th][:, 0:w], lhsT=self.ones[:], rhs=sq[c % 2][:, th * w:(th + 1) * w],
                    start=(c == 0), stop=(c == KC - 1)),
                    reads=[kr(tag + "sq", c % 2), "ones"], writes=[kr("ps", 6 + th)])
        for th in range(nh):
            sl = slice(th * w, (th + 1) * w)
            S.add("act", lambda e, th=th, sl=sl: e.activation(
                out=rstd[:, sl], in_=pss[th][:, 0:w], func=AF.Sqrt, bias=self.epsc[:], scale=1.0 / D),
                reads=[kr("ps", 6 + th), "epsc"], writes=[kr(tag + "rstd", th)])
            S.add("dve", lambda e, sl=sl: e.reciprocal(out=rstd[:, sl], in_=rstd[:, sl]),
                  reads=[kr(tag + "rstd", th)], writes=[kr(tag + "rstd", th)])
        for c in range(KC):
            S.add("dve", lambda e, c=c: e.scalar_tensor_tensor(
                out=dst[:, c, :], in0=src[:, c, :], scalar=self.gcol[:, c:c + 1], in1=rstd,
                op0=ALU.mult, op1=ALU.mult),
                reads=[kr(skey, c), "gcol"] + [kr(tag + "rstd", th) for th in range(nh)], writes=[kr(dkey, c)])

    def ffn(self, gvec, wgu_d, wd_d):
        S = self.S
        GW, GRP = self.GW, self.GRP
        G = GW // 128
        hT, xn = self.hT, self.xn
        S.barrier()
        self.rmsnorm(hT, xn, T, gvec, "f", "h", "xn")
        S.barrier()
        wgu = [self.WA[:, i * 8192:(i + 1) * 8192].rearrange("p (c f) -> p c f", c=KC) for i in range(2)]
        wd = [self.WB[:, i * 8192:(i + 1) * 8192].rearrange("p (c f) -> p c f", c=GRP) for i in range(2)]
        actb = [self.BIG[:, i * 4096:(i + 1) * 4096].rearrange("p (c f) -> p c f", c=GRP) for i in range(2)]
        sg = [self.BIG[:, 8192 + i * 1024: 8192 + (i + 1) * 1024].bitcast(F32) for i in range(2)]
        nslab = NFC // G
        ngrp = NFC // GRP
        pg = [self.psum[0], self.psum[1]]
        pu = [self.psum[2], self.psum[3]]
        pd = [self.psum[4], self.psum[5]]

        def load_wgu(s):
            b = self.rot("wgu", 2)
            S.add("pool", lambda e, s=s, b=b: [e.dma_start(
                out=self.WA[:, b * 8192:(b + 1) * 8192], in_=wgu_d[s])],
                writes=[kr("wgu", b)], dma_slot=("wgu", b))
            return b

        def load_wd(g):
            b = self.rot("wd", 2)
            S.add("pool", lambda e, g=g, b=b: [e.dma_start(
                out=self.WB[:, b * 8192:(b + 1) * 8192], in_=wd_d[g])],
                writes=[kr("wd", b)], dma_slot=("wd", b))
            return b

        slab_buf = {}
        wd_buf = {}
        slab_buf[0] = load_wgu(0)
        slab_buf[1] = load_wgu(1)
        wd_buf[0] = load_wd(0)
        for g in range(ngrp):
            ab = self.rot("act", 2)
            for j in range(GRP):
                fc = g * GRP + j
                s = fc // G
                jj = fc % G
                wb = slab_buf[s]
                for th in range(NTH):
                    tsl = slice(th * 512, (th + 1) * 512)

                    def mm_g(e, wb=wb, jj=jj, th=th, tsl=tsl):
                        r = None
                        for c in range(KC):
                            r = e.matmul(pg[th][:], lhsT=wgu[wb][:, c, jj * 128:(jj + 1) * 128],
                                         rhs=xn[:, c, tsl], start=(c == 0), stop=(c == KC - 1))
                        return r

                    def mm_u(e, wb=wb, jj=jj, th=th, tsl=tsl):
                        r = None
                        for c in range(KC):
                            r = e.matmul(pu[th][:], lhsT=wgu[wb][:, c, GW + jj * 128:GW + (jj + 1) * 128],
                                         rhs=xn[:, c, tsl], start=(c == 0), stop=(c == KC - 1))
                        return r
                    xr = [kr("xn", c) for c in range(KC)]
                    S.add("pe", mm_g, reads=[kr("wgu", wb)] + xr, writes=[kr("ps", th)])
                    S.add("pe", mm_u, reads=[kr("wgu", wb)] + xr, writes=[kr("ps", 2 + th)])
                    S.add("act", lambda e, th=th: e.activation(out=sg[th], in_=pg[th][:], func=AF.Silu),
                          reads=[kr("ps", th)], writes=[kr("sg", th)])
                    S.add("dve", lambda e, th=th, ab=ab, j=j, tsl=tsl: e.tensor_tensor(
                        out=actb[ab][:, j, tsl], in0=sg[th], in1=pu[th][:], op=ALU.mult),
                        reads=[kr("sg", th), kr("ps", 2 + th)], writes=[kr("act", ab, j, th)])
                if jj == G - 1 and s + 2 < nslab:
                    slab_buf[s + 2] = load_wgu(s + 2)
            if g + 1 < ngrp:
                wd_buf[g + 1] = load_wd(g + 1)
            db = wd_buf[g]
            for dc in range(KC):
                for th in range(NTH):
                    tsl = slice(th * 512, (th + 1) * 512)
                    pb = self.rot("pd", 2)

                    def mm_d(e, db=db, dc=dc, tsl=tsl, pb=pb, ab=ab):
                        r = None
                        for j in range(GRP):
                            r = e.matmul(pd[pb][:], lhsT=wd[db][:, j, dc * 128:(dc + 1) * 128],
                                         rhs=actb[ab][:, j, tsl], start=(j == 0), stop=(j == GRP - 1))
                        return r
                    S.add("pe", mm_d, reads=[kr("wd", db)] + [kr("act", ab, j, th) for j in range(GRP)],
                          writes=[kr("ps", 4 + pb)])
                    S.add("dve", lambda e, dc=dc, tsl=tsl, pb=pb: e.tensor_tensor(
                        out=hT[:, dc, tsl], in0=hT[:, dc, tsl], in1=pd[pb][:], op=ALU.add),
                        reads=[kr("ps", 4 + pb), kr("h", dc)], writes=[kr("h", dc)])

    def proj_headnorm(self, wv, srcs, dsts, gain_col, wkey):
        S = self.S
        for (rhs_fn, n, rkeys), (dst, wkeys) in zip(srcs, dsts):
            pb = self.rot("phn", 2)
            praw = self.psum[pb]
            pssq = self.psum[2 + pb]
            sqs = self.MISC[:, pb * 512: pb * 512 + n]
            rs = self.MISC[:, 1024 + pb * 1024: 1024 + pb * 1024 + 2 * n].bitcast(F32)

            def mm(e, rhs_fn=rhs_fn, n=n, praw=praw):
                r = None
                for c in range(KC):
                    r = e.matmul(praw[:, 0:n], lhsT=wv[:, c, :], rhs=rhs_fn(c), start=(c == 0), stop=(c == KC - 1))
                return r
            S.add("pe", mm, reads=[wkey] + list(rkeys), writes=[kr("ps", pb)])
            S.add("act", lambda e, n=n, praw=praw, sqs=sqs: e.activation(out=sqs, in_=praw[:, 0:n], func=AF.Square),
                  reads=[kr("ps", pb)], writes=[kr("hsq", pb)])
            S.add("pe", lambda e, n=n, pssq=pssq, sqs=sqs: e.matmul(pssq[:, 0:n], lhsT=self.bones[:], rhs=sqs,
                                                                     start=True, stop=True),
                  reads=[kr("hsq", pb), "bones"], writes=[kr("ps", 2 + pb)])
            S.add("act", lambda e, n=n, pssq=pssq, rs=rs: e.activation(
                out=rs, in_=pssq[:, 0:n], func=AF.Sqrt, bias=self.epsc[:], scale=1.0 / HD_A),
                reads=[kr("ps", 2 + pb), "epsc"], writes=[kr("hrs", pb)])
            S.add("dve", lambda e, rs=rs: e.reciprocal(out=rs, in_=rs), reads=[kr("hrs", pb)], writes=[kr("hrs", pb)])
            S.add("dve", lambda e, n=n, praw=praw, rs=rs, dst=dst: e.scalar_tensor_tensor(
                out=dst, in0=praw[:, 0:n], scalar=gain_col, in1=rs, op0=ALU.mult, op1=ALU.mult),
                reads=[kr("ps", pb), kr("hrs", pb), "gains"], writes=list(wkeys))

    def swa(self, p):
        S = self.S
        hT, xn = self.hT, self.xn
        S.barrier()
        halo_h = self.MISC[:, 0:4096].bitcast(F32).rearrange("p (c t) -> p c t", c=KC)
        kT = self.WB[:, 0:8 * TK].rearrange("p (c t) -> p c t", c=8)
        o_v = 8 * TK
        vext = self.WB[:, o_v:o_v + 9 * 4 * 80].rearrange("p (b h d) -> p b h d", b=9, h=4)
        o_x = o_v + 9 * 4 * 80
        xnh = self.WB[:, o_x:o_x + KC * 128].rearrange("p (c t) -> p c t", c=KC)
        assert o_x + KC * 128 <= 16384
        halo_src = p["haloT"].rearrange("(c p) t -> p c t", p=128)
        S.add("sp", lambda e: [e.dma_start(out=halo_h, in_=halo_src)], writes=[kr("hh", c) for c in range(KC)],
              dma_slot="halo")
        gains = self.smallf[:, 0:2]
        S.add("sp", lambda e: [e.dma_start(out=gains, in_=p["qkn"])], writes=["gains"], dma_slot="gains")
        sinks = self.smallf[:, 8:40]
        S.add("sp", lambda e: [e.dma_start(out=sinks, in_=p["sinks"])], writes=["sinks"], dma_slot="sinks")
        S.add("act", lambda e: e.activation(out=sinks, in_=sinks, func=AF.Exp), reads=["sinks"], writes=["sinks"])
        S.add("pool", lambda e: [e.dma_start(out=self.sb_masks[:], in_=p["masks"])], writes=["masks"], dma_slot="masks")
        masks = self.sb_masks[:, :].rearrange("p (m q) -> p m q", m=3)
        self.osb_full = [self.MISC[:, 2048:4096], self.MISC[:, 4096:6144]]
        self.rmsnorm(halo_h, xnh, 128, p["g_mix"], "hn", "hh", "xnh")
        S.barrier()
        self.rmsnorm(hT, xn, T, p["g_mix"], "m", "h", "xn")
        S.barrier()
        qT = self.BIG[:, :].rearrange("p (c t) -> p c t", c=KC)
        wa = [self.WA[:, i * 8192:(i + 1) * 8192].rearrange("p (c f) -> p c f", c=KC) for i in range(2)]

        def load_wa(src, ncol=512):
            b = self.rot("swa_wa", 2)
            S.add("pool", lambda e, b=b: [e.dma_start(out=self.WA[:, b * 8192: b * 8192 + KC * ncol], in_=src)],
                  writes=[kr("wa", b)], dma_slot=("wa", b))
            return b
        xr = [kr("xn", c) for c in range(KC)]
        xhr = [kr("xnh", c) for c in range(KC)]
        slabs = [("q", i) for i in range(4)] + [("k", i) for i in range(2)]
        bufs = {}
        bufs[0] = load_wa(p["wq"][0])
        for si, (kind, i) in enumerate(slabs):
            if si + 1 < len(slabs):
                k2, i2 = slabs[si + 1]
                bufs[si + 1] = load_wa(p["wq"][i2] if k2 == "q" else p["wk"][i2])
            b = bufs[si]
            for j in range(4):
                wv = wa[b][:, :, j * 128:(j + 1) * 128]
                if kind == "q":
                    ch = i * 4 + j
                    srcs = [(lambda c, th=th: xn[:, c, th * 512:(th + 1) * 512], 512, xr) for th in range(NTH)]
                    dsts = [(qT[:, ch, th * 512:(th + 1) * 512], [kr("qT", ch, th)]) for th in range(NTH)]
                    self.proj_headnorm(wv, srcs, dsts, gains[:, 0:1], kr("wa", b))
                else:
                    ch = i * 4 + j
                    srcs = [(lambda c: xnh[:, c, :], 128, xhr)] + \
                           [(lambda c, th=th: xn[:, c, th * 512:(th + 1) * 512], 512, xr) for th in range(NTH)]
                    dsts = [(kT[:, ch, 0:128], [kr("kT", ch, 0)])] + \
                           [(kT[:, ch, 128 + th * 512:128 + (th + 1) * 512], [kr("kT", ch, 1 + th)]) for th in range(NTH)]
                    self.proj_headnorm(wv, srcs, dsts, gains[:, 1:2], kr("wa", b))
        bv = load_wa(p["wv"], 256)
        wvv = self.WA[:, bv * 8192: bv * 8192 + KC * 256].rearrange("p (c f) -> p c f", c=KC)
        S.add("pool", lambda e: e.memset(vext[:, :, :, 64:65], 1.0), writes=["vones"])
        for blk in range(9):
            pb = 4 + self.rot("pv", 2)

            def mmv(e, blk=blk, pb=pb):
                r = None
                for c in range(KC):
                    lhs = xnh[:, c, :] if blk == 0 else xn[:, c, (blk - 1) * 128: blk * 128]
                    r = e.matmul(self.psum[pb][:, 0:256], lhsT=lhs, rhs=wvv[:, c, :], start=(c == 0), stop=(c == KC - 1))
                return r
            S.add("pe", mmv, reads=[kr("wa", bv)] + (xhr if blk == 0 else xr), writes=[kr("ps", pb)])
            S.add("act", lambda e, blk=blk, pb=pb: e.activation(
                out=vext[:, blk, :, 0:64], in_=self.psum[pb][:, 0:256].rearrange("p (h d) -> p h d", h=4),
                func=AF.Copy), reads=[kr("ps", pb)], writes=[kr("v", blk)])
        S.barrier()
        OT = xn
        et = [self.MISC[:, i * 512:(i + 1) * 512] for i in range(4)]
        den = self.smallf[:, 40:48]
        wo_bufs = {}
        wo_v = [self.WA[:, i * 8192:(i + 1) * 8192].rearrange("p (j c f) -> p j c f", j=4, c=KC) for i in range(2)]

        def load_wo(dg):
            b = self.rot("swa_wo", 2)
            S.add("pool", lambda e, b=b, dg=dg: [e.dma_start(out=self.WA[:, b * 8192:(b + 1) * 8192], in_=p["wo"][dg])],
                  writes=[kr("wo", b)], dma_slot=("wob", b))
            return b
        wo_bufs[0] = load_wo(0)
        wo_bufs[1] = load_wo(1)
        tp_bf = [self.psum[6][:].bitcast(BF16), self.psum[7][:].bitcast(BF16)]
        for n in range(8):
            qs = slice(n * 128, (n + 1) * 128)
            for hk in range(4):
                for half in range(2):
                    grp = hk * 2 + half
                    sp_i = self.rot("sps", 2)
                    ps_prev = self.psum[sp_i * 2]
                    ps_cur = self.psum[sp_i * 2 + 1]
                    e_prev = et[sp_i * 2]
                    e_cur = et[sp_i * 2 + 1]
                    for kb, pst in ((0, ps_prev), (1, ps_cur)):
                        ks = slice((n + kb) * 128, (n + kb + 1) * 128)

                        def mms(e, hk=hk, half=half, ks=ks, pst=pst, qs=qs):
                            r = None
                            for j in range(4):
                                h = hk * 8 + half * 4 + j
                                r = e.matmul(pst[:, j * 128:(j + 1) * 128], lhsT=kT[:, hk * 2 + (h % 2), ks],
                                             rhs=qT[:, h // 2, qs], start=True, stop=True)
                            return r
                        S.add("pe", mms, reads=["kTall", "qTall"], writes=[kr("ps", sp_i * 2 + kb)])
                    midx_prev = 0 if n == 0 else 1
                    for kb, pst, ee, mi in ((0, ps_prev, e_prev, midx_prev), (1, ps_cur, e_cur, 2)):
                        S.add("act", lambda e, pst=pst, ee=ee: e.activation(out=ee, in_=pst[:], func=AF.Exp, scale=HD_A ** -0.5),
                              reads=[kr("ps", sp_i * 2 + kb)], writes=[kr("et", sp_i * 2 + kb)])
                        S.add("pool", lambda e, ee=ee, mi=mi: e.tensor_tensor(
                            out=ee.rearrange("p (j q) -> p j q", j=4), in0=ee.rearrange("p (j q) -> p j q", j=4),
                            in1=masks[:, mi:mi + 1, :].to_broadcast([128, 4, 128]), op=ALU.mult),
                            reads=[kr("et", sp_i * 2 + kb), "masks"], writes=[kr("et", sp_i * 2 + kb)])
                    po_i = 4 + self.rot("spo", 2)
                    po = self.psum[po_i]

                    def mmo(e, n=n, hk=hk, po=po, e_prev=e_prev, e_cur=e_cur):
                        r = None
                        for j in range(4):
                            e.matmul(po[:, j * 128:j * 128 + 65], lhsT=e_prev[:, j * 128:(j + 1) * 128],
                                     rhs=vext[:, n, hk, 0:65], start=True, stop=False)
                            r = e.matmul(po[:, j * 128:j * 128 + 65], lhsT=e_cur[:, j * 128:(j + 1) * 128],
                                         rhs=vext[:, n + 1, hk, 0:65], start=False, stop=True)
                        return r
                    S.add("pe", mmo, reads=[kr("et", sp_i * 2), kr("et", sp_i * 2 + 1), "vall"], writes=[kr("ps", po_i)])
                    h0 = hk * 8 + half * 4
                    pov = po[:, :].rearrange("p (j d) -> p j d", j=4)
                    dn = den[:, 0:4] if (grp % 2 == 0) else den[:, 4:8]
                    dk = kr("den", grp % 2)
                    S.add("dve", lambda e, pov=pov, dn=dn, h0=h0: e.tensor_tensor(
                        out=dn.rearrange("p (j o) -> p j o", o=1), in0=pov[:, :, 64:65],
                        in1=sinks[:, h0:h0 + 4].rearrange("p (j o) -> p j o", o=1), op=ALU.add),
                        reads=[kr("ps", po_i), "sinks"], writes=[dk])
                    S.add("dve", lambda e, dn=dn: e.reciprocal(out=dn, in_=dn), reads=[dk], writes=[dk])
                    ob = self.osb_full[n % 2][:, h0 * 64:(h0 + 4) * 64].rearrange("p (j d) -> p j d", j=4)
                    S.add("dve", lambda e, pov=pov, dn=dn, ob=ob: e.tensor_tensor(
                        out=ob, in0=pov[:, :, 0:64], in1=dn.rearrange("p (j o) -> p j o", o=1).to_broadcast([128, 4, 64]),
                        op=ALU.mult),
                        reads=[kr("ps", po_i), dk], writes=[kr("osb", n % 2, grp)])
            for c4 in range(4):
                tp_i = self.rot("tp", 2)
                tpv = tp_bf[tp_i]

                def mmt(e, c4=c4, tpv=tpv, n=n):
                    r = None
                    for j in range(4):
                        c = c4 * 4 + j
                        r = e.transpose(tpv[:, j * 128:(j + 1) * 128], self.osb_full[n % 2][:, c * 128:(c + 1) * 128],
                                        self.ident[:])
                    return r
                S.add("pe", mmt, reads=[kr("osb", n % 2, g_) for g_ in range(8)] + ["ident"], writes=[kr("ps", 6 + tp_i)])
                S.add("act", lambda e, c4=c4, tpv=tpv, qs=qs: e.activation(
                    out=OT[:, c4 * 4:(c4 + 1) * 4, qs], in_=tpv[:, 0:512].rearrange("p (j q) -> p j q", j=4), func=AF.Copy),
                    reads=[kr("ps", 6 + tp_i)], writes=[kr("OT", n, c4)])
        S.barrier()
        for dg in range(4):
            b = wo_bufs[dg]
            for j in range(4):
                dc = dg * 4 + j
                for th in range(NTH):
                    tsl = slice(th * 512, (th + 1) * 512)
                    pb = self.rot("pwo", 2)

                    def mmw(e, b=b, j=j, tsl=tsl, pb=pb):
                        r = None
                        for c in range(KC):
                            r = e.matmul(self.psum[pb][:], lhsT=wo_v[b][:, j, c, :], rhs=OT[:, c, tsl],
                                         start=(c == 0), stop=(c == KC - 1))
                        return r
                    S.add("pe", mmw, reads=[kr("wo", b)], writes=[kr("ps", pb)])
                    S.add("dve", lambda e, dc=dc, tsl=tsl, pb=pb: e.tensor_tensor(
                        out=hT[:, dc, tsl], in0=hT[:, dc, tsl], in1=self.psum[pb][:], op=ALU.add),
                        reads=[kr("ps", pb), kr("h", dc)], writes=[kr("h", dc)])
            if dg + 2 < 4:
                wo_bufs[dg + 2] = load_wo(dg + 2)

    def gla(self, p, with_output):
        S = self.S
        hT, xn = self.hT, self.xn
        S.barrier()
        self.rmsnorm(hT, xn, T, p["g_mix"], "g", "h", "xn")
        S.barrier()
        BIG, MISC = self.BIG, self.MISC
        S32 = BIG[:, 13568:15616].bitcast(F32).rearrange("p (c f) -> p c f", c=2)
        Sbf = BIG[:, 8192:9216].rearrange("p (c f) -> p c f", c=2)
        OTh = BIG[:, 3072:7168].rearrange("p (c t) -> p c t", c=4)
        glT = BIG[0:64, 9216:10240]
        wgb = BIG[0:64, 10240:11264]
        wgl = BIG[:, 11264:11520].rearrange("p (c f) -> p c f", c=KC)
        gain = BIG[:, 11520:12544].bitcast(F32)
        Ltmp = BIG[:, 12544:13568].bitcast(F32)
        sp_tok = MISC[:, 0:256]
        eb = MISC[:, 256:768].bitcast(F32)
        enb = MISC[:, 768:1280].bitcast(F32)
        expf = MISC[:, 1280:1792].bitcast(F32)
        qd = MISC[:, 1792:2048].rearrange("p (c t) -> p c t", c=2)
        ki = MISC[:, 2048:2304].rearrange("p (c t) -> p c t", c=2)
        kend = MISC[:, 2304:2560]
        vsb = MISC[:, 2560:3072]
        am = MISC[:, 3072:3200]
        e1 = MISC[:, 3200:3712].bitcast(F32)
        on = MISC[:, 3712:4736].bitcast(F32)
        sr = MISC[:, 4736:5248]
        og = MISC[:, 5248:5760]
        dcols = self.smallf[:, 0:8]
        dprev = self.smallf[:, 8:32].rearrange("p (s c) -> p s c", s=3)
        ssq = self.smallf[:, 32:33]
        rstd = self.smallf[:, 33:34]
        masks = self.sb_masks[:, :].rearrange("p (m q) -> p m q", m=3)
        ps = self.psum
        import os
        dbg = int(os.environ.get("GLA_DBG", "0"))
        if dbg == 10:
            return
        S.add("pool", lambda e: [e.dma_start(out=self.sb_masks[:], in_=p["masks"])], writes=["masks"], dma_slot="masks")
        S.add("pool", lambda e: [e.dma_start(out=wgb, in_=p["wgb"])], writes=["wgb"], dma_slot="wgb")
        S.add("pool", lambda e: [e.dma_start(out=BIG[:, 11264:11520], in_=p["wgl"])], writes=["wgl"], dma_slot="wgl")
        S.add("sp", lambda e: [e.dma_start(out=gain, in_=p["onorm"])], writes=["gain"], dma_slot="gain")
        S.add("sp", lambda e: [e.dma_start(out=self.smallf[:, 8:32], in_=p["dprev"].rearrange("s p c -> p s c"))],
              writes=["dprev"], dma_slot="dprev") if False else None
        for s_ in range(3):
            S.add("sp", lambda e, s_=s_: [e.dma_start(out=dprev[:, s_, :], in_=p["dprev"][s_])], writes=[kr("dprev", s_)],
                  dma_slot=("dprev", s_))
        if dbg == 11:
            S.barrier()
            S.add("dve", lambda e: e.memset(dcols, 1.0), writes=["dcols"])
            return
        S.add("dve", lambda e: e.memset(glT, 0.0), writes=["glT"])
        S.add("dve", lambda e: e.memset(glT[32:64, :], 1.0), reads=["glT"], writes=["glT"])
        xr = [kr("xn", c) for c in range(KC)]
        for th in range(NTH):
            tsl = slice(th * 512, (th + 1) * 512)

            def mmgl(e, tsl=tsl, th=th):
                r = None
                for c in range(KC):
                    r = e.matmul(ps[th][0:16, :], lhsT=wgl[:, c, :], rhs=xn[:, c, tsl], start=(c == 0), stop=(c == KC - 1))
                return r
            S.add("pe", mmgl, reads=["wgl"] + xr, writes=[kr("ps", th)])
            S.add("act", lambda e, tsl=tsl, th=th: e.activation(out=glT[0:16, tsl], in_=ps[th][0:16, :], func=AF.Copy),
                  reads=[kr("ps", th), "glT"], writes=["glT"])
        S.add("dve", lambda e: e.memset(dcols, 1.0), writes=["dcols"])
        S.barrier()
        import os
        dbg = int(os.environ.get("GLA_DBG", "0"))
        if dbg == 1:
            return
        slabv = [self.WA[:, 0:8192], self.WA[:, 8192:16384], self.WB[:, 0:8192], self.WB[:, 8192:16384]]
        nbuf = 4

        def load_slab(src):
            b = self.rot("gla_slab", nbuf)
            S.add("pool", lambda e, b=b: [e.dma_start(out=slabv[b], in_=src)], writes=[kr("slab", b)], dma_slot=("gslab", b))
            return b
        for hd in range(4):
            bA = load_slab(p["wA"][hd])
            bB = load_slab(p["wB"][hd])
            if with_output:
                bC = load_slab(p["wC"][hd])
                bO = load_slab(p["woh"][hd])
            wA = slabv[bA].rearrange("p (c f) -> p c f", c=KC)
            wBv = slabv[bB].rearrange("p (c f) -> p c f", c=KC)
            if with_output:
                wCv = slabv[bC].rearrange("p (c f) -> p c f", c=KC)
                wOv = slabv[bO].rearrange("p (c f) -> p c f", c=4)
            for dkc in range(2):
                idx = hd * 2 + dkc
                if with_output:
                    S.add("sp", lambda e, idx=idx, dkc=dkc: [e.dma_start(out=S32[:, dkc, :], in_=p["sprev"][0, idx])],
                          writes=[kr("S32", dkc)], dma_slot=("s32", dkc))
                    for s_ in (1, 2):
                        S.add("sp", lambda e, idx=idx, s_=s_: [e.dma_start(out=Ltmp, in_=p["sprev"][s_, idx])],
                              writes=["Ltmp"], dma_slot="ltmp")
                        S.add("dve", lambda e, idx=idx, s_=s_, dkc=dkc: e.scalar_tensor_tensor(
                            out=S32[:, dkc, :], in0=S32[:, dkc, :], scalar=dprev[:, s_, idx:idx + 1], in1=Ltmp,
                            op0=ALU.mult, op1=ALU.add),
                            reads=[kr("S32", dkc), "Ltmp", kr("dprev", s_)], writes=[kr("S32", dkc)])
                elif dbg == 36:
                    altm = BIG[:, 0:2048].bitcast(F32).rearrange("p (c f) -> p c f", c=2)
                    S.add("dve", lambda e, dkc=dkc, altm=altm: e.memset(altm[:, dkc, :], 0.0), writes=[kr("alt", dkc)])
                    S.add("dve", lambda e, dkc=dkc: e.memset(S32[:, dkc, :], 0.0), writes=[kr("S32", dkc)])
                elif dbg == 35:
                    S.add("dve", lambda e, dkc=dkc: e.tensor_copy(out=S32[:, dkc, :], in_=gain), reads=["gain"], writes=[kr("S32", dkc)])
                else:
                    S.add("dve", lambda e, dkc=dkc: e.memset(S32[:, dkc, :], 0.0), writes=[kr("S32", dkc)])
                if with_output:
                    S.add("act", lambda e, dkc=dkc: e.activation(out=Sbf[:, dkc, :], in_=S32[:, dkc, :], func=AF.Copy),
                          reads=[kr("S32", dkc)], writes=[kr("Sbf", dkc)])
            for blk in range(8):
                if dbg == 2 and (hd, blk) != (0, 0):
                    continue
                bs = slice(blk * 128, (blk + 1) * 128)
                if with_output:
                    def mmqk(e, bs=bs, wA=wA):
                        r = None
                        for j in range(4):
                            for c in range(KC):
                                r = e.matmul(ps[0][:, j * 128:(j + 1) * 128], lhsT=wA[:, c, j * 128:(j + 1) * 128],
                                             rhs=xn[:, c, bs], start=(c == 0), stop=(c == KC - 1))
                        return r
                    S.add("pe", mmqk, reads=[kr("slab", bA)] + xr, writes=[kr("ps", 0)])

                def mmk(e, bs=bs, wA=wA):
                    r = None
                    for c in range(KC):
                        r = e.matmul(ps[1][:, 0:256], lhsT=xn[:, c, bs], rhs=wA[:, c, 256:512], start=(c == 0), stop=(c == KC - 1))
                    return r
                S.add("pe", mmk, reads=[kr("slab", bA)] + xr, writes=[kr("ps", 1)])
                S.add("pe", lambda e, bs=bs, hd=hd: e.matmul(ps[1][:, 256:512], lhsT=glT[0:33, bs],
                                                             rhs=wgb[0:33, hd * 256:(hd + 1) * 256], start=True, stop=True),
                      reads=["glT", "wgb"], writes=[kr("ps", 1)])

                def mmv(e, bs=bs, wBv=wBv):
                    r = None
                    for c in range(KC):
                        r = e.matmul(ps[2][:], lhsT=xn[:, c, bs], rhs=wBv[:, c, :], start=(c == 0), stop=(c == KC - 1))
                    return r
                S.add("pe", mmv, reads=[kr("slab", bB)] + xr, writes=[kr("ps", 2)])
                S.add("act", lambda e: e.activation(out=vsb, in_=ps[2][:], func=AF.Copy), reads=[kr("ps", 2)], writes=["vsb"])
                if dbg == 20:
                    break
                S.add("act", lambda e: e.activation(out=e1, in_=ps[1][:, 256:512], func=AF.Exp, scale=-1.0),
                      reads=[kr("ps", 1)], writes=["e1"])
                S.add("act", lambda e: e.activation(out=sp_tok, in_=e1, func=AF.Ln, bias=1.0), reads=["e1"], writes=["sp"])
                if dbg == 21:
                    break

                def mmb(e):
                    r = None
                    for dkc in range(2):
                        r = e.matmul(ps[4][:, dkc * 128:(dkc + 1) * 128], lhsT=sp_tok[:, dkc * 128:(dkc + 1) * 128],
                                     rhs=masks[:, 0, :], start=True, stop=True)
                    return r
                S.add("pe", mmb, reads=["sp", "masks"], writes=[kr("ps", 4)])
                S.add("pe", lambda e: e.matmul(ps[4][:, 256:512], lhsT=masks[:, 1, :], rhs=sp_tok, start=True, stop=True),
                      reads=["sp", "masks"], writes=[kr("ps", 4)])
                S.add("act", lambda e: e.activation(out=eb, in_=ps[4][:, 0:256], func=AF.Exp), reads=[kr("ps", 4)], writes=["eb"])
                S.add("act", lambda e: e.activation(out=expf, in_=ps[4][:, 256:512], func=AF.Exp), reads=[kr("ps", 4)],
                      writes=["expf"])
                S.add("dve", lambda e: e.tensor_tensor(out=kend, in0=ps[1][:, 0:256], in1=expf, op=ALU.mult),
                      reads=[kr("ps", 1), "expf"], writes=["kend"])
                if with_output:
                    S.add("act", lambda e: e.activation(out=enb, in_=ps[4][:, 0:256], func=AF.Exp, scale=-1.0),
                          reads=[kr("ps", 4)], writes=["enb"])
                    S.add("dve", lambda e: e.scalar_tensor_tensor(
                        out=qd, in0=ps[0][:, 0:256].rearrange("p (c t) -> p c t", c=2), scalar=256.0 ** -0.5,
                        in1=eb.rearrange("p (c t) -> p c t", c=2), op0=ALU.mult, op1=ALU.mult),
                        reads=[kr("ps", 0), "eb"], writes=["qd"])
                    S.add("dve", lambda e: e.tensor_tensor(
                        out=ki, in0=ps[0][:, 256:512].rearrange("p (c t) -> p c t", c=2),
                        in1=enb.rearrange("p (c t) -> p c t", c=2), op=ALU.mult),
                        reads=[kr("ps", 0), "enb"], writes=["ki"])

                    def mma(e):
                        r = None
                        for dkc in range(2):
                            r = e.matmul(ps[5][:, 0:128], lhsT=ki[:, dkc, :], rhs=qd[:, dkc, :], start=(dkc == 0), stop=(dkc == 1))
                        return r
                    S.add("pe", mma, reads=["ki", "qd"], writes=[kr("ps", 5)])
                    S.add("dve", lambda e: e.tensor_tensor(out=am, in0=ps[5][:, 0:128], in1=masks[:, 2, :], op=ALU.mult),
                          reads=[kr("ps", 5), "masks"], writes=["am"])

                    def mmo(e):
                        e.matmul(ps[6][:], lhsT=am, rhs=vsb, start=True, stop=False)
                        e.matmul(ps[6][:], lhsT=qd[:, 0, :], rhs=Sbf[:, 0, :], start=False, stop=False)
                        return e.matmul(ps[6][:], lhsT=qd[:, 1, :], rhs=Sbf[:, 1, :], start=False, stop=True)
                    S.add("pe", mmo, reads=["am", "vsb", "qd", kr("Sbf", 0), kr("Sbf", 1)], writes=[kr("ps", 6)])

                    def mmr(e, bs=bs, wCv=wCv):
                        r = None
                        for c in range(KC):
                            r = e.matmul(ps[3][:], lhsT=xn[:, c, bs], rhs=wCv[:, c, :], start=(c == 0), stop=(c == KC - 1))
                        return r
                    S.add("pe", mmr, reads=[kr("slab", bC)] + xr, writes=[kr("ps", 3)])
                if dbg == 22:
                    break
                for dkc in range(2):
                    S.add("pe", lambda e, dkc=dkc: e.matmul(ps[7][:], lhsT=kend[:, dkc * 128:(dkc + 1) * 128], rhs=vsb,
                                                            start=True, stop=True),
                          reads=["kend", "vsb"], writes=[kr("ps", 7)])
                    if dbg == 24:
                        continue
                    dec = self.smallf[:, 34 + dkc:35 + dkc]
                    S.add("dve", lambda e, dkc=dkc, dec=dec: e.tensor_copy(out=dec, in_=eb[:, dkc * 128 + 127: dkc * 128 + 128]),
                          reads=["eb"], writes=[kr("dec", dkc)])
                    if dbg == 25:
                        continue
                    if dbg == 26:
                        S.add("dve", lambda e, dkc=dkc, dec=dec: e.tensor_tensor(
                            out=S32[:, dkc, :], in0=S32[:, dkc, :], in1=ps[7][:], op=ALU.add),
                            reads=[kr("S32", dkc), kr("ps", 7)], writes=[kr("S32", dkc)])
                        continue
                    if dbg == 28:
                        S.add("dve", lambda e, dkc=dkc, dec=dec: e.tensor_copy(out=on, in_=ps[7][:]),
                            reads=[kr("ps", 7)], writes=["on"])
                        continue
                    if dbg == 32:
                        S.add("act", lambda e, dkc=dkc, dec=dec: e.activation(out=S32[:, dkc, :], in_=gain, func=AF.Copy),
                            reads=[kr("S32", dkc), "gain"], writes=[kr("S32", dkc)])
                        continue
                    if dbg in (33, 36):
                        alt = BIG[:, 0:2048].bitcast(F32).rearrange("p (c f) -> p c f", c=2)
                        S.add("dve", lambda e, dkc=dkc, alt=alt: e.tensor_copy(out=alt[:, dkc, :], in_=gain),
                            reads=[kr("alt", dkc), "gain"], writes=[kr("alt", dkc)])
                        continue
                    if dbg == 34:
                        S.add("dve", lambda e, dkc=dkc: e.tensor_copy(out=S32[:, dkc, :], in_=gain),
                            reads=["gain"], writes=[kr("S32x", dkc)])
                        continue
                    if dbg in (31, 35):
                        S.add("dve", lambda e, dkc=dkc, dec=dec: e.tensor_copy(out=S32[:, dkc, :], in_=gain),
                            reads=[kr("S32", dkc), "gain"], writes=[kr("S32", dkc)])
                        continue
                    if dbg in (29, 30):
                        S.add("dve", lambda e, dkc=dkc, dec=dec: e.tensor_copy(out=S32[:, dkc, :], in_=on),
                            reads=[kr("S32", dkc), "on"], writes=[kr("S32", dkc)])
                        continue
                    if dbg == 27:
                        S.add("dve", lambda e, dkc=dkc, dec=dec: e.tensor_copy(out=S32[:, dkc, :], in_=ps[7][:]),
                            reads=[kr("S32", dkc), kr("ps", 7)], writes=[kr("S32", dkc)])
                        continue
                    S.add("dve", lambda e, dkc=dkc, dec=dec: e.scalar_tensor_tensor(
                        out=S32[:, dkc, :], in0=S32[:, dkc, :], scalar=dec, in1=ps[7][:],
                        op0=ALU.mult, op1=ALU.add),
                        reads=[kr("S32", dkc), kr("dec", dkc), kr("ps", 7)], writes=[kr("S32", dkc)])
                    if with_output:
                        S.add("act", lambda e, dkc=dkc: e.activation(out=Sbf[:, dkc, :], in_=S32[:, dkc, :], func=AF.Copy),
                              reads=[kr("S32", dkc)], writes=[kr("Sbf", dkc)])
                    elif dbg != 23:
                        idx = hd * 2 + dkc
                        S.add("dve", lambda e, dkc=dkc, idx=idx: e.tensor_tensor(
                            out=dcols[:, idx:idx + 1], in0=dcols[:, idx:idx + 1],
                            in1=self.smallf[:, 34 + dkc:35 + dkc], op=ALU.mult),
                            reads=[kr("dec", dkc), "dcols"], writes=["dcols"])
                if with_output:
                    S.add("act", lambda e: e.activation(out=on, in_=ps[6][:], func=AF.Square, accum_out=ssq),
                          reads=[kr("ps", 6)], writes=["on", "ssq"])
                    S.add("act", lambda e: e.activation(out=rstd, in_=ssq, func=AF.Sqrt, bias=self.epsc[:], scale=1.0 / 512),
                          reads=["ssq"], writes=["rstd"])
                    S.add("dve", lambda e: e.reciprocal(out=rstd, in_=rstd), reads=["rstd"], writes=["rstd"])
                    S.add("dve", lambda e: e.scalar_tensor_tensor(out=on, in0=ps[6][:], scalar=rstd, in1=gain,
                                                                  op0=ALU.mult, op1=ALU.mult),
                          reads=[kr("ps", 6), "rstd", "gain", "on"], writes=["on"])
                    S.add("act", lambda e: e.activation(out=sr, in_=ps[3][:], func=AF.Silu), reads=[kr("ps", 3)], writes=["sr"])
                    S.add("dve", lambda e: e.tensor_tensor(out=og, in0=on, in1=sr, op=ALU.mult), reads=["on", "sr"], writes=["og"])
                    tpv = ps[5][:].bitcast(BF16)[:, 512:1024]

                    def mmt(e, tpv=tpv):
                        r = None
                        for j in range(4):
                            r = e.transpose(tpv[:, j * 128:(j + 1) * 128], og[:, j * 128:(j + 1) * 128], self.ident[:])
                        return r
                    S.add("pe", mmt, reads=["og", "ident"], writes=[kr("ps", 5)])
                    S.add("act", lambda e, bs=bs, tpv=tpv: e.activation(
                        out=OTh[:, :, bs], in_=tpv.rearrange("p (j q) -> p j q", j=4), func=AF.Copy),
                        reads=[kr("ps", 5)], writes=[kr("OTh", blk)])
            if with_output:
                for dc in range(KC):
                    for th in range(NTH):
                        tsl = slice(th * 512, (th + 1) * 512)
                        pb = 2 + self.rot("gwo", 2)

                        def mmw(e, dc=dc, tsl=tsl, pb=pb, wOv=wOv):
                            r = None
                            for c in range(4):
                                r = e.matmul(ps[pb][:], lhsT=wOv[:, c, dc * 128:(dc + 1) * 128], rhs=OTh[:, c, tsl],
                                             start=(c == 0), stop=(c == 3))
                            return r
                        S.add("pe", mmw, reads=[kr("slab", bO)] + [kr("OTh", b_) for b_ in range(8)], writes=[kr("ps", pb)])
                        S.add("dve", lambda e, dc=dc, tsl=tsl, pb=pb: e.tensor_tensor(
                            out=hT[:, dc, tsl], in0=hT[:, dc, tsl], in1=ps[pb][:], op=ALU.add),
                            reads=[kr("ps", pb), kr("h", dc)], writes=[kr("h", dc)])
            elif dbg != 30:
                for dkc in range(2):
                    idx = hd * 2 + dkc
                    op = S.add("sp", lambda e, idx=idx, dkc=dkc: [e.dma_start(out=p["sloc"][idx], in_=S32[:, dkc, :])],
                               reads=[kr("S32", dkc)], dma_slot=("sloc", dkc))
                    self.out_ops.append(op)
        if not with_output:
            op = S.add("sp", lambda e: [e.dma_start(out=p["dtot"], in_=dcols)], reads=["dcols"], dma_slot="dtot")
            self.out_ops.append(op)

    def wo_proj(self, wo_d):
        S = self.S
        hT, OT = self.hT, self.xn
        wo_v = [self.WA[:, i * 8192:(i + 1) * 8192].rearrange("p (j c f) -> p j c f", j=4, c=KC) for i in range(2)]
        bufs = {}

        def load_wo(dg):
            b = self.rot("wo_g", 2)
            S.add("pool", lambda e, b=b, dg=dg: [e.dma_start(out=self.WA[:, b * 8192:(b + 1) * 8192], in_=wo_d[dg])],
                  writes=[kr("wog", b)], dma_slot=("wog", b))
            return b
        bufs[0] = load_wo(0)
        bufs[1] = load_wo(1)
        for dg in range(4):
            b = bufs[dg]
            for j in range(4):
                dc = dg * 4 + j
                for th in range(NTH):
                    tsl = slice(th * 512, (th + 1) * 512)
                    pb = self.rot("pwog", 2)

                    def mmw(e, b=b, j=j, tsl=tsl, pb=pb):
                        r = None
                        for c in range(KC):
                            r = e.matmul(self.psum[pb][:], lhsT=wo_v[b][:, j, c, :], rhs=OT[:, c, tsl],
                                         start=(c == 0), stop=(c == KC - 1))
                        return r
                    S.add("pe", mmw, reads=[kr("wog", b)], writes=[kr("ps", pb)])
                    S.add("dve", lambda e, dc=dc, tsl=tsl, pb=pb: e.tensor_tensor(
                        out=hT[:, dc, tsl], in0=hT[:, dc, tsl], in1=self.psum[pb][:], op=ALU.add),
                        reads=[kr("ps", pb), kr("h", dc)], writes=[kr("h", dc)])
            if dg + 2 < 4:
                bufs[dg + 2] = load_wo(dg + 2)

    def dsa_prep(self, p):
        S = self.S
        S.barrier()
        self.rmsnorm(self.hT, self.xn, T, p["g_mix"], "dp", "h", "xn")
        S.barrier()
        xn, ps = self.xn, self.psum
        wkk = self.WA[:, 0:KC * 320].rearrange("p (c f) -> p c f", c=KC)
        S.add("pool", lambda e: [e.dma_start(out=self.WA[:, 0:KC * 320], in_=p["wkk"])], writes=["wkk"], dma_slot="wkk")
        gb_ = self.MISC[:, 0:768].bitcast(F32)
        S.add("sp", lambda e: [e.dma_start(out=gb_[:, 0:256], in_=p["kvg"])], writes=["kvg"], dma_slot="kvg")
        S.add("sp", lambda e: [e.dma_start(out=gb_[:, 256:320], in_=p["ikg"])], writes=["ikg"], dma_slot="ikg")
        S.add("sp", lambda e: [e.dma_start(out=gb_[:, 320:384], in_=p["ikb"])], writes=["ikb"], dma_slot="ikb")
        junk = self.MISC[:, 768:1280].bitcast(F32)
        st = self.smallf
        for blk in range(8):
            bs = slice(blk * 128, (blk + 1) * 128)
            pb = self.rot("dpp", 2)
            ob = self.rot("dpo", 2)
            oc = self.MISC[:, 1280 + ob * 640: 1280 + ob * 640 + 512].bitcast(F32)
            ok = self.MISC[:, 1280 + ob * 640 + 512: 1280 + (ob + 1) * 640].bitcast(F32)
            xc = self.MISC[:, 2560:2688].bitcast(F32)

            def mm(e, bs=bs, pb=pb):
                r = None
                for c in range(KC):
                    r = e.matmul(ps[pb][:, 0:320], lhsT=xn[:, c, bs], rhs=wkk[:, c, :], start=(c == 0), stop=(c == KC - 1))
                return r
            S.add("pe", mm, reads=["wkk"] + [kr("xn", c) for c in range(KC)], writes=[kr("ps", pb)])
            S.add("act", lambda e, pb=pb: e.activation(out=junk, in_=ps[pb][:, 0:256], func=AF.Square, accum_out=st[:, 0:1]),
                  reads=[kr("ps", pb)], writes=["junk", "st0"])
            S.add("act", lambda e: e.activation(out=st[:, 1:2], in_=st[:, 0:1], func=AF.Sqrt, bias=self.epsc[:], scale=1.0 / 256),
                  reads=["st0"], writes=["st1"])
            S.add("dve", lambda e: e.reciprocal(out=st[:, 1:2], in_=st[:, 1:2]), reads=["st1"], writes=["st1"])
            S.add("dve", lambda e, pb=pb, oc=oc: e.scalar_tensor_tensor(out=oc, in0=ps[pb][:, 0:256], scalar=st[:, 1:2],
                                                                        in1=gb_[:, 0:256], op0=ALU.mult, op1=ALU.mult),
                  reads=[kr("ps", pb), "st1", "kvg"], writes=[kr("oc", ob)])
            op = S.add("sp", lambda e, bs=bs, oc=oc: [e.dma_start(out=p["ckv"][bs, :], in_=oc)], reads=[kr("oc", ob)],
                       dma_slot=("oc", ob))
            self.out_ops.append(op)
            S.add("dve", lambda e, pb=pb: e.tensor_reduce(out=st[:, 2:3], in_=ps[pb][:, 256:320], axis=AX.X, op=ALU.add),
                  reads=[kr("ps", pb)], writes=["st2"])
            S.add("dve", lambda e: e.tensor_scalar(out=st[:, 2:3], in0=st[:, 2:3], scalar1=1.0 / 64, scalar2=None, op0=ALU.mult),
                  reads=["st2"], writes=["st2"])
            S.add("dve", lambda e, pb=pb, xc=xc: e.tensor_scalar(out=xc, in0=ps[pb][:, 256:320], scalar1=st[:, 2:3], scalar2=None,
                                                                 op0=ALU.subtract),
                  reads=[kr("ps", pb), "st2"], writes=["xc"])
            S.add("act", lambda e, xc=xc: e.activation(out=junk[:, 0:64], in_=xc, func=AF.Square, accum_out=st[:, 3:4]),
                  reads=["xc"], writes=["junk", "st3"])
            S.add("act", lambda e: e.activation(out=st[:, 4:5], in_=st[:, 3:4], func=AF.Sqrt, bias=self.epsc[:], scale=1.0 / 64),
                  reads=["st3"], writes=["st4"])
            S.add("dve", lambda e: e.reciprocal(out=st[:, 4:5], in_=st[:, 4:5]), reads=["st4"], writes=["st4"])
            S.add("dve", lambda e, xc=xc, ok=ok: e.scalar_tensor_tensor(out=ok, in0=xc, scalar=st[:, 4:5], in1=gb_[:, 256:320],
                                                                        op0=ALU.mult, op1=ALU.mult),
                  reads=["xc", "st4", "ikg"], writes=[kr("ok", ob)])
            S.add("dve", lambda e, ok=ok: e.tensor_tensor(out=ok, in0=ok, in1=gb_[:, 320:384], op=ALU.add),
                  reads=[kr("ok", ob), "ikb"], writes=[kr("ok", ob)])
            op = S.add("sp", lambda e, bs=bs, ok=ok: [e.dma_start(out=p["kidx"][bs, :], in_=ok)], reads=[kr("ok", ob)],
                       dma_slot=("ok", ob))
            self.out_ops.append(op)

    def dsa_main(self, p, xT):
        S = self.S
        hT, xn, ps = self.hT, self.xn, self.psum
        BIG, MISC, WA, WB = self.BIG, self.MISC, self.WA, self.WB
        nk = 4 * T
        S.barrier()
        self.rmsnorm(hT, xn, T, p["g_mix"], "dm", "h", "xn")
        S.barrier()
        Hraw = hT[:, :, :].rearrange("p c t -> p (c t)")
        rawq = Hraw[:, 0:4096].rearrange("p (c t) -> p c t", c=4)
        rstdq = Hraw[:, 4096:5120]
        cqT = BIG[:, 0:4096].rearrange("p (c t) -> p c t", c=4)
        selT = BIG[:, 4096:8192].rearrange("p (k q) -> p k q", q=128)
        qabs_bufs = [BIG[:, 8192:12288].rearrange("p (h c q) -> p h c q", h=16, c=2),
                     WB[:, 4096:8192].rearrange("p (h c q) -> p h c q", h=16, c=2)]
        qnT = BIG[:, 12288:14336].rearrange("p (h q) -> p h q", h=16)
        qidxT = BIG[:, 14336:16384].rearrange("p (h q) -> p h q", h=16)
        wcq = WA[:, 0:8192].rearrange("p (c f) -> p c f", c=KC)
        wwi = WA[:, 8192:8448].rearrange("p (c f) -> p c f", c=KC)
        S.add("pool", lambda e: [e.dma_start(out=WA[:, 0:8192], in_=p["wcq"])], writes=["wcq"], dma_slot="wcq")
        S.add("pool", lambda e: [e.dma_start(out=WA[:, 8192:8448], in_=p["wwi"])], writes=["wwi"], dma_slot="wwi")
        gq = self.smallf[:, 0:4]
        gqa = self.smallf[:, 4:6]
        S.add("sp", lambda e: [e.dma_start(out=gq, in_=p["gq"])], writes=["gq"], dma_slot="gq")
        S.add("sp", lambda e: [e.dma_start(out=gqa, in_=p["gqa"])], writes=["gqa"], dma_slot="gqa")
        widx = MISC[:, 5888:6144].bitcast(F32)
        sqt = [MISC[:, 0:512], MISC[:, 512:1024]]
        xr = [kr("xn", c) for c in range(KC)]
        for th in range(NTH):
            tsl = slice(th * 512, (th + 1) * 512)
            for jc in range(4):
                pb = self.rot("dq", 2)

                def mm(e, jc=jc, tsl=tsl, pb=pb):
                    r = None
                    for c in range(KC):
                        r = e.matmul(ps[pb][:], lhsT=wcq[:, c, jc * 128:(jc + 1) * 128], rhs=xn[:, c, tsl],
                                     start=(c == 0), stop=(c == KC - 1))
                    return r
                S.add("pe", mm, reads=["wcq"] + xr, writes=[kr("ps", pb)])
                S.add("act", lambda e, jc=jc, tsl=tsl, pb=pb: e.activation(out=rawq[:, jc, tsl], in_=ps[pb][:], func=AF.Copy),
                      reads=[kr("ps", pb)], writes=[kr("rawq", jc, th)])
                S.add("act", lambda e, jc=jc, pb=pb: e.activation(out=sqt[jc % 2], in_=ps[pb][:], func=AF.Square),
                      reads=[kr("ps", pb)], writes=[kr("sqt", jc % 2)])
                S.add("pe", lambda e, jc=jc: e.matmul(ps[2][:], lhsT=self.ones[:], rhs=sqt[jc % 2], start=(jc == 0), stop=(jc == 3)),
                      reads=[kr("sqt", jc % 2), "ones"], writes=[kr("ps", 2)])
            S.add("act", lambda e, tsl=tsl: e.activation(out=rstdq[:, tsl], in_=ps[2][:], func=AF.Sqrt, bias=self.epsc[:],
                                                         scale=1.0 / 512), reads=[kr("ps", 2)], writes=[kr("rstdq", th)])
            S.add("dve", lambda e, tsl=tsl: e.reciprocal(out=rstdq[:, tsl], in_=rstdq[:, tsl]), reads=[kr("rstdq", th)],
                  writes=[kr("rstdq", th)])
            for jc in range(4):
                S.add("dve", lambda e, jc=jc, tsl=tsl: e.scalar_tensor_tensor(
                    out=cqT[:, jc, tsl], in0=rawq[:, jc, tsl], scalar=gq[:, jc:jc + 1], in1=rstdq[:, tsl],
                    op0=ALU.mult, op1=ALU.mult), reads=[kr("rawq", jc, th), kr("rstdq", th), "gq"], writes=[kr("cqT", jc, th)])
        for blk in range(8):
            bs = slice(blk * 128, (blk + 1) * 128)
            pb = 3

            def mmwi(e, bs=bs):
                r = None
                for c in range(KC):
                    r = e.matmul(ps[3][:, 0:16], lhsT=xn[:, c, bs], rhs=wwi[:, c, :], start=(c == 0), stop=(c == KC - 1))
                return r
            S.add("pe", mmwi, reads=["wwi"] + xr, writes=[kr("ps", 3)])
            S.add("act", lambda e, blk=blk: e.activation(out=widx[:, blk * 16:(blk + 1) * 16], in_=ps[3][:, 0:16], func=AF.Copy,
                                                         scale=16 ** -0.5 * 64 ** -0.5),
                  reads=[kr("ps", 3)], writes=[kr("widx", blk)])
        S.barrier()
        H16 = Hraw.bitcast(BF16)
        Wsc = Hraw[:, 0:4096]
        ckvT = H16[:, 8192:8192 + 2 * nk].rearrange("p (c k) -> p c k", c=2)
        ckv1 = H16[:, 16384:16384 + (nk // 128) * 260].rearrange("p (k f) -> p k f", f=260)
        kidx = H16[0:64, 24832:24832 + nk]
        dmt = [H16[:, 29184:29696], H16[:, 30208:30720]]
        nmt = [WB[:, 8192:8704], WB[:, 9216:9728]]
        S.add("pool", lambda e: [e.dma_start(out=H16[:, 8192:8192 + 2 * nk], in_=p["ckvT"])], writes=["ckvT"], dma_slot="ckvT")
        S.add("pool", lambda e: [e.dma_start(out=H16[:, 16384:16384 + (nk // 128) * 260], in_=p["ckv1"])], writes=["ckv1"],
              dma_slot="ckv1")
        S.add("pool", lambda e: [e.dma_start(out=kidx, in_=p["kidxT"])], writes=["kidx"], dma_slot="kidxT")
        wuq = WA[:, 0:8192].rearrange("p (c f) -> p c f", c=4)
        wuk = WA[:, 8192:12288].rearrange("p (h c) -> p h c", h=16)
        wiq = WA[:, 12288:16384].rearrange("p (c f) -> p c f", c=4)
        wuv = WB[:, 0:4096].rearrange("p (h c v) -> p h c v", h=16, c=2)
        S.add("pool", lambda e: [e.dma_start(out=WA[:, 0:8192], in_=p["wuq"])], writes=["wuq"], dma_slot="wuq")
        S.add("pool", lambda e: [e.dma_start(out=WA[:, 8192:12288], in_=p["wuk"])], writes=["wuk"], dma_slot="wuk")
        S.add("pool", lambda e: [e.dma_start(out=WA[:, 12288:16384], in_=p["wiq"])], writes=["wiq"], dma_slot="wiq")
        S.add("pool", lambda e: [e.dma_start(out=WB[:, 0:4096], in_=p["wuv"])], writes=["wuv"], dma_slot="wuv")
        S.barrier()
        OT = xn
        rt = [MISC[:, 0:1024].bitcast(F32), MISC[:, 1024:2048].bitcast(F32)]
        et = [MISC[:, 2048:2560], MISC[:, 2560:3072]]
        selt = MISC[:, 3072:3584]
        olat = MISC[:, 3584:4608].rearrange("p (h c) -> p h c", h=4)
        olT = MISC[:, 4608:5632].rearrange("p (h c q) -> p h c q", h=4, c=2)
        sqa = MISC[:, 5632:5888]
        mx8 = self.smallf[:, 8:16]
        den = self.smallf[:, 16:20]
        rstda = self.smallf2[:, :]
        def emit_bcd(blk):
            bs = slice(blk * 128, (blk + 1) * 128)
            nkt = blk + 1
            qabsT = qabs_bufs[blk % 2]
            for h4 in range(4):
                pb = self.rot("qi", 2)

                def mmqi(e, h4=h4, pb=pb, bs=bs):
                    r = None
                    for jh in range(4):
                        h = h4 * 4 + jh
                        for c in range(4):
                            r = e.matmul(ps[pb][0:64, jh * 128:(jh + 1) * 128], lhsT=wiq[:, c, h * 64:(h + 1) * 64],
                                         rhs=cqT[:, c, bs], start=(c == 0), stop=(c == 3))
                    return r
                S.add("pe", mmqi, reads=["wiq"], writes=[kr("ps", pb)])
                S.add("act", lambda e, h4=h4, pb=pb: e.activation(
                    out=qidxT[0:64, h4 * 4:(h4 + 1) * 4, :], in_=ps[pb][0:64, :].rearrange("p (h q) -> p h q", h=4), func=AF.Copy),
                    reads=[kr("ps", pb)], writes=[kr("qidxT", h4)])
            for h4 in range(4):
                pb = self.rot("qi", 2)

                def mmqn(e, h4=h4, pb=pb, bs=bs):
                    r = None
                    for jh in range(4):
                        h = h4 * 4 + jh
                        for c in range(4):
                            r = e.matmul(ps[pb][:, jh * 128:(jh + 1) * 128], lhsT=wuq[:, c, h * 128:(h + 1) * 128],
                                         rhs=cqT[:, c, bs], start=(c == 0), stop=(c == 3))
                    return r
                S.add("pe", mmqn, reads=["wuq"], writes=[kr("ps", pb)])
                S.add("act", lambda e, h4=h4, pb=pb: e.activation(
                    out=qnT[:, h4 * 4:(h4 + 1) * 4, :], in_=ps[pb][:, :].rearrange("p (h q) -> p h q", h=4), func=AF.Copy),
                    reads=[kr("ps", pb)], writes=[kr("qnT", h4)])
            for h in range(16):
                pb = 2 + self.rot("qa", 2)
                pq = 4 + self.rot("qas", 2)

                def mmqa(e, h=h, pb=pb):
                    r = None
                    for cc in range(2):
                        r = e.matmul(ps[pb][:, cc * 128:(cc + 1) * 128], lhsT=wuk[:, h, cc * 128:(cc + 1) * 128], rhs=qnT[:, h, :],
                                     start=True, stop=True)
                    return r
                S.add("pe", mmqa, reads=["wuk", kr("qnT", h // 4)], writes=[kr("ps", pb)])
                S.add("act", lambda e, pb=pb: e.activation(out=sqa, in_=ps[pb][:, 0:256], func=AF.Square), reads=[kr("ps", pb)],
                      writes=["sqa"])

                def mmss(e, pq=pq):
                    e.matmul(ps[pq][:, 0:128], lhsT=self.ones[:], rhs=sqa[:, 0:128], start=True, stop=False)
                    return e.matmul(ps[pq][:, 0:128], lhsT=self.ones[:], rhs=sqa[:, 128:256], start=False, stop=True)
                S.add("pe", mmss, reads=["sqa", "ones"], writes=[kr("ps", pq)])
                S.add("act", lambda e, pq=pq: e.activation(out=rstda[:, 0:128], in_=ps[pq][:, 0:128], func=AF.Sqrt, bias=self.epsc[:],
                                                           scale=1.0 / 256), reads=[kr("ps", pq)], writes=["rstda"])
                S.add("dve", lambda e: e.reciprocal(out=rstda[:, 0:128], in_=rstda[:, 0:128]), reads=["rstda"], writes=["rstda"])
                for cc in range(2):
                    S.add("dve", lambda e, h=h, cc=cc, pb=pb: e.scalar_tensor_tensor(
                        out=qabsT[:, h, cc, :], in0=ps[pb][:, cc * 128:(cc + 1) * 128], scalar=gqa[:, cc:cc + 1],
                        in1=rstda[:, 0:128], op0=ALU.mult, op1=ALU.mult),
                        reads=[kr("ps", pb), "rstda", "gqa"], writes=[kr("qabsT", blk % 2, h)])
            for kt in range(nkt):
                ksl = slice(kt * 512, (kt + 1) * 512)
                for h in range(16):
                    pb = self.rot("sc", 2)
                    S.add("pe", lambda e, h=h, ksl=ksl, pb=pb: e.matmul(ps[pb][:], lhsT=qidxT[0:64, h, :], rhs=kidx[:, ksl],
                                                                        start=True, stop=True),
                          reads=["kidx", kr("qidxT", h // 4)], writes=[kr("ps", pb)])
                    S.add("act", lambda e, pb=pb: e.activation(out=rt[pb], in_=ps[pb][:], func=AF.Relu), reads=[kr("ps", pb)],
                          writes=[kr("rt", pb)])
                    wcol = widx[:, blk * 16 + h: blk * 16 + h + 1]
                    if h == 0:
                        S.add("dve", lambda e, pb=pb, ksl=ksl, wcol=wcol: e.tensor_scalar(
                            out=Wsc[:, ksl], in0=rt[pb], scalar1=wcol, scalar2=None, op0=ALU.mult),
                            reads=[kr("rt", pb), kr("widx", blk)], writes=[kr("W", kt)])
                    else:
                        S.add("dve", lambda e, pb=pb, ksl=ksl, wcol=wcol: e.scalar_tensor_tensor(
                            out=Wsc[:, ksl], in0=rt[pb], scalar=wcol, in1=Wsc[:, ksl], op0=ALU.mult, op1=ALU.add),
                            reads=[kr("rt", pb), kr("widx", blk), kr("W", kt)], writes=[kr("W", kt)])
                mi = self.rot("nmt", 2)
                S.add("pool", lambda e, mi=mi, blk=blk, kt=kt: [e.dma_start(out=nmt[mi], in_=p["negm"][blk, kt])],
                      writes=[kr("nmt", mi)], dma_slot=("nmt", mi))
                S.add("dve", lambda e, ksl=ksl, mi=mi: e.tensor_tensor(out=Wsc[:, ksl], in0=Wsc[:, ksl], in1=nmt[mi], op=ALU.min),
                      reads=[kr("W", kt), kr("nmt", mi)], writes=[kr("W", kt)])

        def emit_topk(blk, its):
            nkt = blk + 1
            wkeys = [kr("W", kt) for kt in range(nkt)]
            Wv = Wsc[:, 0:nkt * 512]
            for it in its:
                S.add("dve", lambda e, Wv=Wv: e.max(out=mx8, in_=Wv), reads=wkeys, writes=["mx8"])
                S.add("dve", lambda e, Wv=Wv: e.match_replace(out=Wv, in_to_replace=mx8, in_values=Wv, imm_value=-3.0e38),
                      reads=wkeys + ["mx8"], writes=wkeys)

        def emit_sel(blk):
            nkt = blk + 1
            for kt in range(nkt):
                ksl = slice(kt * 512, (kt + 1) * 512)
                S.add("dve", lambda e, ksl=ksl: e.tensor_scalar(out=selt, in0=Wsc[:, ksl], scalar1=-2.0e38, scalar2=None, op0=ALU.is_le),
                      reads=[kr("W", kt)], writes=["selt"])
                di = self.rot("dmt", 2)
                S.add("pool", lambda e, di=di, blk=blk, kt=kt: [e.dma_start(out=dmt[di], in_=p["dmk"][blk, kt])],
                      writes=[kr("dmt", di)], dma_slot=("dmt", di))
                S.add("dve", lambda e, di=di: e.tensor_tensor(out=selt, in0=selt, in1=dmt[di], op=ALU.mult),
                      reads=["selt", kr("dmt", di)], writes=["selt"])
                nb = 4
                tp_i = 2 + self.rot("st", 2)
                tpv = ps[tp_i][:].bitcast(BF16)

                def mmt(e, nb=nb, tpv=tpv):
                    r = None
                    for i in range(nb):
                        r = e.transpose(tpv[:, i * 128:(i + 1) * 128], selt[:, i * 128:(i + 1) * 128], self.ident[:])
                    return r
                S.add("pe", mmt, reads=["selt", "ident"], writes=[kr("ps", tp_i)])
                S.add("act", lambda e, kt=kt, nb=nb, tpv=tpv: e.activation(
                    out=selT[:, kt * 4:kt * 4 + nb, :], in_=tpv[:, 0:nb * 128].rearrange("p (k q) -> p k q", q=128), func=AF.Copy),
                    reads=[kr("ps", tp_i)], writes=[kr("selT", kt)])

        def emit_att(blk, hg):
            bs = slice(blk * 128, (blk + 1) * 128)
            qabsT = qabs_bufs[blk % 2]
            nkb = 4 * blk + 4
            for kb in range(nkb):
                kbs = slice(kb * 128, (kb + 1) * 128)
                pb = self.rot("lg", 2)

                def mml(e, hg=hg, kbs=kbs, pb=pb):
                    r = None
                    for jh in range(4):
                        for cc in range(2):
                            r = e.matmul(ps[pb][:, jh * 128:(jh + 1) * 128], lhsT=ckvT[:, cc, kbs], rhs=qabsT[:, hg * 4 + jh, cc, :],
                                         start=(cc == 0), stop=(cc == 1))
                    return r
                S.add("pe", mml, reads=["ckvT"] + [kr("qabsT", blk % 2, hg * 4 + jh_) for jh_ in range(4)], writes=[kr("ps", pb)])
                S.add("act", lambda e, pb=pb: e.activation(out=et[pb], in_=ps[pb][:], func=AF.Exp, scale=256.0 ** -0.5),
                      reads=[kr("ps", pb)], writes=[kr("et", pb)])
                S.add("pool", lambda e, pb=pb, kb=kb: e.tensor_tensor(
                    out=et[pb].rearrange("p (j q) -> p j q", j=4), in0=et[pb].rearrange("p (j q) -> p j q", j=4),
                    in1=selT[:, kb:kb + 1, :].to_broadcast([128, 4, 128]), op=ALU.mult),
                    reads=[kr("et", pb), kr("selT", kb // 4)], writes=[kr("et", pb)])

                def mmo(e, pb=pb, kb=kb, gb=nkb - 1):
                    r = None
                    for jh in range(4):
                        r = e.matmul(ps[4 + jh][:, 0:257], lhsT=et[pb][:, jh * 128:(jh + 1) * 128], rhs=ckv1[:, kb, 0:257],
                                     start=(kb == 0), stop=(kb == gb))
                    return r
                S.add("pe", mmo, reads=[kr("et", pb), "ckv1"], writes=[kr("ps", 4 + jh_) for jh_ in range(4)])
            for jh in range(4):
                S.add("dve", lambda e, jh=jh: e.reciprocal(out=den[:, jh:jh + 1], in_=ps[4 + jh][:, 256:257]),
                      reads=[kr("ps", 4 + jh)], writes=[kr("den", jh)])
                S.add("dve", lambda e, jh=jh: e.tensor_scalar(out=olat[:, jh, :], in0=ps[4 + jh][:, 0:256], scalar1=den[:, jh:jh + 1],
                                                             scalar2=None, op0=ALU.mult),
                      reads=[kr("ps", 4 + jh), kr("den", jh)], writes=[kr("olat", jh)])
            tpv = ps[2][:].bitcast(BF16)

            def mmt2(e, tpv=tpv):
                r = None
                for jh in range(4):
                    for cc in range(2):
                        r = e.transpose(tpv[:, (jh * 2 + cc) * 128:(jh * 2 + cc + 1) * 128], olat[:, jh, cc * 128:(cc + 1) * 128],
                                        self.ident[:])
                return r
            S.add("pe", mmt2, reads=[kr("olat", jh_) for jh_ in range(4)] + ["ident"], writes=[kr("ps", 2)])
            S.add("act", lambda e, tpv=tpv: e.activation(out=olT, in_=tpv.rearrange("p (h c q) -> p h c q", h=4, c=2), func=AF.Copy),
                  reads=[kr("ps", 2)], writes=["olT"])

            def mmv(e, hg=hg):
                r = None
                for jh in range(4):
                    for cc in range(2):
                        r = e.matmul(ps[3][:, jh * 128:(jh + 1) * 128], lhsT=wuv[:, hg * 4 + jh, cc, :], rhs=olT[:, jh, cc, :],
                                     start=(cc == 0), stop=(cc == 1))
                return r
            S.add("pe", mmv, reads=["olT", "wuv"], writes=[kr("ps", 3)])
            S.add("act", lambda e, hg=hg, bs=bs: e.activation(
                out=OT[:, hg * 4:(hg + 1) * 4, bs], in_=ps[3][:, :].rearrange("p (h q) -> p h q", h=4), func=AF.Copy),
                reads=[kr("ps", 3)], writes=[kr("OTd", blk, hg)])

        for blk in range(8):
            emit_bcd(blk)
            for q4 in range(4):
                emit_topk(blk, range(q4 * 8, q4 * 8 + 8))
                if blk > 0:
                    emit_att(blk - 1, q4)
            emit_sel(blk)
        for q4 in range(4):
            emit_att(7, q4)
        S.barrier()
        self.load_h(xT)
        S.barrier()
        self.wo_proj(p["wo"])

    def finish(self):
        S = self.S
        nc = self.nc
        with nc.Block() as block:
            @block.tensor
            def _(e):
                S.emit("pe", e)

            @block.scalar
            def _(e):
                S.emit("act", e)

            @block.vector
            def _(e):
                S.emit("dve", e)

            @block.gpsimd
            def _(e):
                S.emit("pool", e)

            @block.sync
            def _(e):
                S.emit("sp", e)
                S.final_wait(e, self.out_ops)
                import os
                if os.environ.get("GLA_FW"):
                    for en in os.environ["GLA_FW"].split(","):
                        cands = [o for o in S.ops[en] if not o.is_dma]
                        if cands:
                            S.final_wait(e, [cands[-1]])
        self.stack.close()
        return nc


def lay_vec(g):
    return np.ascontiguousarray(np.asarray(g, np.float32).reshape(KC, 128).T)


def lay_kslab(w, ncol):
    w = np.asarray(w, np.float32)
    K_, N_ = w.shape
    ns = N_ // ncol
    out = w.reshape(K_ // 128, 128, ns, ncol).transpose(2, 1, 0, 3).reshape(ns, 128, (K_ // 128) * ncol)
    return np.ascontiguousarray(out)


def lay_wgu(w, GW=256):
    w = np.asarray(w, np.float32)
    nslab = DFF // GW
    g = w[:, :DFF].reshape(KC, 128, nslab, GW)
    u = w[:, DFF:].reshape(KC, 128, nslab, GW)
    gu = np.stack([g, u], axis=3)
    out = gu.transpose(2, 1, 0, 3, 4).reshape(nslab, 128, KC * 2 * GW)
    return np.ascontiguousarray(out)


def lay_wd(w, GRP=4):
    w = np.asarray(w, np.float32)
    ngrp = NFC // GRP
    out = w.reshape(ngrp, GRP, 128, D).transpose(0, 2, 1, 3).reshape(ngrp, 128, GRP * D)
    return np.ascontiguousarray(out)


def lay_wo(w):
    w = np.asarray(w, np.float32)
    out = w.reshape(KC, 128, 4, 4, 128).transpose(2, 1, 3, 0, 4).reshape(4, 128, 4 * KC * 128)
    return np.ascontiguousarray(out)


def swa_host_inputs(pre, inp, halo_rows, first):
    w_in = np.asarray(inp[pre + "w_in"], np.float32)
    wq = w_in[:, :2048]
    wk = w_in[:, 2048:2304]
    wv = w_in[:, 2304:2560]
    wkp = np.zeros((2048, 8, 128), np.float32)
    for hk in range(4):
        wkp[:, hk * 2 + 0, 0:64] = wk[:, hk * 64:(hk + 1) * 64]
        wkp[:, hk * 2 + 1, 64:128] = wk[:, hk * 64:(hk + 1) * 64]
    s_idx = np.arange(128)[:, None]
    q_idx = np.arange(128)[None, :]
    mprev = (s_idx > q_idx).astype(np.float32)
    mcur = (s_idx <= q_idx).astype(np.float32)
    m0 = np.zeros_like(mprev) if first else mprev
    masks = np.stack([m0, mprev, mcur], axis=1).reshape(128, 3 * 128)
    qkn = np.stack([np.tile(np.asarray(inp[pre + "q_norm"], np.float32), 2),
                    np.tile(np.asarray(inp[pre + "k_norm"], np.float32), 2)], axis=1)
    return {
        "g_mix": lay_vec(inp[pre + "norm_mix"]),
        "haloT": np.ascontiguousarray(halo_rows.T),
        "wq": lay_kslab(wq, 512),
        "wk": lay_kslab(wkp.reshape(2048, 1024), 512),
        "wv": lay_kslab(wv, 256)[0],
        "wo": lay_wo(inp[pre + "wo"]),
        "qkn": np.ascontiguousarray(qkn),
        "sinks": np.ascontiguousarray(np.broadcast_to(np.asarray(inp[pre + "sinks"], np.float32)[None, :], (128, 32))),
        "masks": np.ascontiguousarray(masks),
    }


def swa_dram(B, pfx):
    return {
        "g_mix": B.din(pfx + "g_mix", [128, KC]),
        "haloT": B.din(pfx + "haloT", [D, 128]),
        "wq": B.din(pfx + "wq", [4, 128, KC * 512]),
        "wk": B.din(pfx + "wk", [2, 128, KC * 512]),
        "wv": B.din(pfx + "wv", [128, KC * 256]),
        "wo": B.din(pfx + "wo", [4, 128, 4 * KC * 128]),
        "qkn": B.din(pfx + "qkn", [128, 2]),
        "sinks": B.din(pfx + "sinks", [128, 32]),
        "masks": B.din(pfx + "masks", [128, 3 * 128]),
    }


def gla_host_inputs(pre, inp, sprev, dprev):
    w_in = np.asarray(inp[pre + "w_in"], np.float32)
    q = w_in[:, 0:1024]; k = w_in[:, 1024:2048]; v = w_in[:, 2048:4096]; r = w_in[:, 4096:6144]; gl = w_in[:, 6144:6160]
    wA = np.concatenate([np.concatenate([q[:, h * 256:(h + 1) * 256], k[:, h * 256:(h + 1) * 256]], axis=1) for h in range(4)], axis=1)
    wgb = np.zeros((64, 1024), np.float32)
    wgb[0:16] = np.asarray(inp[pre + "w_gate_b"], np.float32)
    wgb[32] = np.asarray(inp[pre + "gate_bias"], np.float32)
    s_idx = np.arange(128)[:, None]; t_idx = np.arange(128)[None, :]
    uneg = np.where(s_idx <= t_idx, -1.0 / 16, 0.0).astype(np.float32)
    ugt = np.where(s_idx > t_idx, -1.0 / 16, 0.0).astype(np.float32)
    cz = (s_idx <= t_idx).astype(np.float32)
    masks = np.stack([uneg, ugt, cz], axis=1).reshape(128, 384)
    wo = np.asarray(inp[pre + "wo"], np.float32)
    woh = wo.reshape(4, 4, 128, 2048).transpose(0, 2, 1, 3).reshape(4, 128, 4 * 2048)
    return {
        "g_mix": lay_vec(inp[pre + "norm_mix"]),
        "wA": lay_kslab(wA, 512), "wB": lay_kslab(v, 512), "wC": lay_kslab(r, 512),
        "wgl": lay_kslab(gl, 16)[0], "wgb": wgb,
        "onorm": np.ascontiguousarray(np.broadcast_to(np.asarray(inp[pre + "o_norm"], np.float32)[None, :], (128, 512))),
        "woh": np.ascontiguousarray(woh), "masks": np.ascontiguousarray(masks),
        "sprev": np.ascontiguousarray(sprev, dtype=np.float32), "dprev": np.ascontiguousarray(dprev, dtype=np.float32),
    }


def gla_dram(B, pfx, with_output):
    d = {
        "g_mix": B.din(pfx + "g_mix", [128, KC]),
        "wA": B.din(pfx + "wA", [4, 128, KC * 512]), "wB": B.din(pfx + "wB", [4, 128, KC * 512]),
        "wgl": B.din(pfx + "wgl", [128, KC * 16]), "wgb": B.din(pfx + "wgb", [64, 1024]),
        "masks": B.din(pfx + "masks", [128, 384]),
        "onorm": B.din(pfx + "onorm", [128, 512]),
        "dprev": B.din(pfx + "dprev", [3, 128, 8]),
    }
    if with_output:
        d["wC"] = B.din(pfx + "wC", [4, 128, KC * 512])
        d["woh"] = B.din(pfx + "woh", [4, 128, 4 * 2048])
        d["sprev"] = B.din(pfx + "sprev", [3, 8, 128, 512])
    else:
        d["sloc"] = B.dout(pfx + "sloc", [8, 128, 512])
        d["dtot"] = B.dout(pfx + "dtot", [128, 8])
    return d


def dsa_prep_host(pre, inp):
    w_in = np.asarray(inp[pre + "w_in"], np.float32)
    wkk = np.concatenate([w_in[:, 512:768], w_in[:, 768:832]], axis=1)
    bc = lambda v, n: np.ascontiguousarray(np.broadcast_to(np.asarray(v, np.float32)[None, :], (128, n)))
    return {"g_mix": lay_vec(inp[pre + "norm_mix"]), "wkk": lay_kslab(wkk, 320)[0],
            "kvg": bc(inp[pre + "kv_norm"], 256), "ikg": bc(inp[pre + "idx_k_norm"], 64), "ikb": bc(inp[pre + "idx_k_bias"], 64)}


def dsa_prep_dram(B, pfx):
    return {"g_mix": B.din(pfx + "g_mix", [128, KC]), "wkk": B.din(pfx + "wkk", [128, KC * 320]),
            "kvg": B.din(pfx + "kvg", [128, 256]), "ikg": B.din(pfx + "ikg", [128, 64]), "ikb": B.din(pfx + "ikb", [128, 64]),
            "ckv": B.dout(pfx + "ckv", [T, 256]), "kidx": B.dout(pfx + "kidx", [T, 64])}


def dsa_main_host(pre, inp, ckv_seq, kidx_seq):
    nk = 4 * T
    w_in = np.asarray(inp[pre + "w_in"], np.float32)
    ck = np.asarray(ckv_seq[:nk], np.float32)
    ckvT = np.ascontiguousarray(ck.T.reshape(2, 128, nk).transpose(1, 0, 2).reshape(128, 2 * nk))
    ckv1 = np.zeros((nk // 128, 128, 260), np.float32)
    ckv1[:, :, 0:256] = ck.reshape(nk // 128, 128, 256)
    ckv1[:, :, 256] = 1.0
    ckv1 = np.ascontiguousarray(ckv1.transpose(1, 0, 2).reshape(128, (nk // 128) * 260))
    kidxT = np.ascontiguousarray(np.asarray(kidx_seq[:nk], np.float32).T)
    w_uk = np.asarray(inp[pre + "w_uk"], np.float32)
    w_uv = np.asarray(inp[pre + "w_uv"], np.float32)
    w_uq = np.asarray(inp[pre + "w_uq"], np.float32)
    w_iq = np.asarray(inp[pre + "w_iq"], np.float32)
    return {
        "g_mix": lay_vec(inp[pre + "norm_mix"]),
        "wcq": lay_kslab(w_in[:, 0:512], 512)[0], "wwi": lay_kslab(w_in[:, 832:848], 16)[0],
        "gq": np.ascontiguousarray(np.asarray(inp[pre + "q_lat_norm"], np.float32).reshape(4, 128).T),
        "gqa": np.ascontiguousarray(np.asarray(inp[pre + "q_abs_norm"], np.float32).reshape(2, 128).T),
        "wuq": lay_kslab(w_uq, 2048)[0], "wiq": lay_kslab(w_iq, 1024)[0],
        "wuk": np.ascontiguousarray(w_uk.transpose(1, 0, 2).reshape(128, 16 * 256)),
        "wuv": np.ascontiguousarray(w_uv.reshape(16, 2, 128, 128).transpose(2, 0, 1, 3).reshape(128, 16 * 2 * 128)),
        "wo": lay_wo(inp[pre + "wo"]),
        "ckvT": ckvT, "ckv1": ckv1, "kidxT": kidxT,
    }


def dsa_masks_host(j):
    q = np.arange(128)[:, None]
    kcol = np.arange(512)[None, :]
    diag = ((kcol // 128 < j) | ((kcol // 128 == j) & (kcol % 128 <= q))).astype(np.float32)
    ones = np.ones((128, 512), np.float32)
    zeros = np.zeros((128, 512), np.float32)
    dmk = np.empty((8, 8, 128, 512), np.float32)
    for blk in range(8):
        for kt in range(8):
            dmk[blk, kt] = ones if kt < blk else (diag if kt == blk else zeros)
    negm = np.where(dmk > 0, np.float32(3e38), np.float32(-1e30)).astype(np.float32)
    return {"dmk": dmk, "negm": negm}


def dsa_main_dram(B, pfx):
    nk = 4 * T
    return {"g_mix": B.din(pfx + "g_mix", [128, KC]), "wcq": B.din(pfx + "wcq", [128, KC * 512]),
            "wwi": B.din(pfx + "wwi", [128, KC * 16]), "gq": B.din(pfx + "gq", [128, 4]), "gqa": B.din(pfx + "gqa", [128, 2]),
            "wuq": B.din(pfx + "wuq", [128, 4 * 2048]), "wiq": B.din(pfx + "wiq", [128, 4 * 1024]),
            "wuk": B.din(pfx + "wuk", [128, 16 * 256]), "wuv": B.din(pfx + "wuv", [128, 16 * 2 * 128]),
            "wo": B.din(pfx + "wo", [4, 128, 4 * KC * 128]),
            "ckvT": B.din(pfx + "ckvT", [128, 2 * nk]), "ckv1": B.din(pfx + "ckv1", [128, (nk // 128) * 260]),
            "kidxT": B.din(pfx + "kidxT", [64, nk]), "dmk": B.din(pfx + "dmk", [8, 8, 128, 512]),
            "negm": B.din(pfx + "negm", [8, 8, 128, 512])}


def ffn_dram(B, pfx):
    return {"g": B.din(pfx + "g", [128, KC]), "wgu": B.din(pfx + "wgu", [NFC // 2, 128, KC * 2 * 256]),
            "wd": B.din(pfx + "wd", [NFC // 4, 128, 4 * D])}


def ffn_host_inputs(pre, inp):
    return {"g": lay_vec(inp[pre + "norm_ffn"]), "wgu": lay_wgu(inp[pre + "w_gate_up"]), "wd": lay_wd(inp[pre + "w_down"])}


IDENT = np.eye(128, dtype=np.float32)


def _prog(stages_fn, extra_out=None):
    B = Builder()
    xT = B.din("xT", [D, T])
    ident = B.din("ident", [128, 128])
    outT = B.dout("outT", [D, T])
    B.alloc()
    B.load_ident(ident)
    B.load_h(xT)
    stages_fn(B, xT)
    B.S.barrier()
    B.store_h(outT)
    return B.finish()


def _run(nc, in_maps):
    n = len(in_maps)
    res = run_bass_kernel_spmd(nc, in_maps, core_ids=list(range(n)))
    return res.results


def _pfx(p, d):
    return {p + k: v for k, v in d.items()}


def kernel(**inputs):
    inp = {k: np.asarray(v) for k, v in inputs.items()}
    h = np.ascontiguousarray(inp["x"], dtype=np.float32)
    cores = [(c // 4, c % 4) for c in range(NCORES)]

    def own(hh, b, j):
        return np.ascontiguousarray(hh[b, j * T:(j + 1) * T].T)

    def gather(results, key="outT"):
        out = np.empty((2, 4 * T, D), np.float32)
        for c, (b, j) in enumerate(cores):
            out[b, j * T:(j + 1) * T] = results[c][key].T
        return out

    def swa_maps(li, hh):
        pre = f"l{li}_"
        shared = None
        maps = []
        for (b, j) in cores:
            halo = hh[b, j * T - 128:j * T] if j > 0 else np.zeros((128, D), np.float32)
            hi = swa_host_inputs(pre, inp, halo, j == 0)
            if shared is None:
                shared = hi
            else:
                for k in ("wq", "wk", "wv", "wo", "g_mix", "qkn", "sinks"):
                    hi[k] = shared[k]
            maps.append(_pfx("a_", hi))
        return maps

    zs = np.zeros((3, 8, 128, 512), np.float32)
    zd = np.ones((3, 128, 8), np.float32)
    g1 = gla_host_inputs("l1_", inp, zs, zd)
    g1p = {k: v for k, v in g1.items() if k not in ("wC", "woh", "sprev")}
    f0 = ffn_host_inputs("l0_", inp)

    def stA(B, xT):
        P = swa_dram(B, "a_")
        F = ffn_dram(B, "f_")
        G = gla_dram(B, "b_", False)
        B.swa(P)
        B.ffn(F["g"], F["wgu"], F["wd"])
        B.gla(G, False)
    ncA = _prog(stA)
    sm = swa_maps(0, h)
    maps = []
    for c, (b, j) in enumerate(cores):
        im = {"xT": own(h, b, j), "ident": IDENT}
        im.update(sm[c]); im.update(_pfx("f_", f0)); im.update(_pfx("b_", g1p))
        maps.append(im)
    rA = _run(ncA, maps)
    h = gather(rA)
    del f0, sm, maps

    f1 = ffn_host_inputs("l1_", inp)
    dp = dsa_prep_host("l2_", inp)

    def stB(B, xT):
        G = gla_dram(B, "b_", True)
        F = ffn_dram(B, "f_")
        Pp = dsa_prep_dram(B, "c_")
        B.gla(G, True)
        B.ffn(F["g"], F["wgu"], F["wd"])
        B.dsa_prep(Pp)
    ncB = _prog(stB)
    maps = []
    for c, (b, j) in enumerate(cores):
        sp = np.zeros((3, 8, 128, 512), np.float32)
        dpv = np.ones((3, 128, 8), np.float32)
        for s_ in range(3):
            i = j - 3 + s_
            if i >= 0:
                sp[s_] = rA[b * 4 + i]["b_sloc"]
                dpv[s_] = rA[b * 4 + i]["b_dtot"]
        gi = dict(g1)
        gi["sprev"] = sp
        gi["dprev"] = dpv
        im = {"xT": own(h, b, j), "ident": IDENT}
        im.update(_pfx("b_", gi)); im.update(_pfx("f_", f1)); im.update(_pfx("c_", dp))
        maps.append(im)
    rB = _run(ncB, maps)
    h = gather(rB)
    ckv = [np.concatenate([rB[b * 4 + j]["c_ckv"] for j in range(4)], 0) for b in range(2)]
    kidx = [np.concatenate([rB[b * 4 + j]["c_kidx"] for j in range(4)], 0) for b in range(2)]
    del f1, g1, g1p, maps, rA

    f2 = ffn_host_inputs("l2_", inp)

    def stC(B, xT):
        P = dsa_main_dram(B, "c_")
        F = ffn_dram(B, "f_")
        B.dsa_main(P, xT)
        B.ffn(F["g"], F["wgu"], F["wd"])
    ncC = _prog(stC)
    hm = [dsa_main_host("l2_", inp, ckv[b], kidx[b]) for b in range(2)]
    for k in ("g_mix", "wcq", "wwi", "gq", "gqa", "wuq", "wiq", "wuk", "wuv", "wo"):
        hm[1][k] = hm[0][k]
    mk = [dsa_masks_host(j) for j in range(4)]
    def own_il(hh, b, j):
        rows = hh[b].reshape(8, 4, 128, D)[:, j].reshape(T, D)
        return np.ascontiguousarray(rows.T)
    maps = []
    for c, (b, j) in enumerate(cores):
        im = {"xT": own_il(h, b, j), "ident": IDENT}
        im.update(_pfx("c_", hm[b])); im.update(_pfx("c_", mk[j])); im.update(_pfx("f_", f2))
        maps.append(im)
    rC = _run(ncC, maps)
    h = np.empty((2, 4 * T, D), np.float32)
    for c, (b, j) in enumerate(cores):
        h[b].reshape(8, 4, 128, D)[:, j] = rC[c]["outT"].T.reshape(8, 128, D)
    del f2, hm, mk, maps

    f3 = ffn_host_inputs("l3_", inp)

    def stD(B, xT):
        P = swa_dram(B, "a_")
        F = ffn_dram(B, "f_")
        B.swa(P)
        B.ffn(F["g"], F["wgu"], F["wd"])
    ncD = _prog(stD)
    sm = swa_maps(3, h)
    maps = []
    for c, (b, j) in enumerate(cores):
        im = {"xT": own(h, b, j), "ident": IDENT}
        im.update(sm[c]); im.update(_pfx("f_", f3))
        maps.append(im)
    rD = _run(ncD, maps)
    return gather(rD)
```

```python
import numpy as np
from contextlib import ExitStack
import concourse.bass as bass
import concourse.mybir as mybir
from concourse.bass_utils import run_bass_kernel_spmd

F32 = mybir.dt.float32
BF16 = mybir.dt.bfloat16
AF = mybir.ActivationFunctionType
ALU = mybir.AluOpType
AX = mybir.AxisListType

D = 2048
KC = D // 128
T = 1024
NTH = T // 512
DFF = 5632
NFC = DFF // 128
EPS = 1e-6
NCORES = 8


DBG_WAITS = None


class Op:
    __slots__ = ("eng", "fn", "deps", "ticket", "is_dma", "ndma", "slot")


class Sched:
    ENGS = ("pe", "act", "dve", "pool", "sp")

    def __init__(self, nc, stack):
        self.nc = nc
        self.stack = stack
        self.ops = {e: [] for e in self.ENGS}
        self.last_w = {}
        self.readers = {}
        self.cnt = {e: 0 for e in self.ENGS}
        self.eng_sem = {}
        self.epoch = 0
        self.slot_sem = {}
        self.slot_cnt = {}
        self.nsem = 0
        self.bar_deps = []
        self.all_dma = []
        self.new_epoch()

    def _sem(self, name):
        self.nsem += 1
        return self.stack.enter_context(self.nc.semaphore(name))

    def new_epoch(self):
        self.epoch += 1
        for e in ("pe", "act", "dve", "pool"):
            self.eng_sem[e] = self._sem(f"s_{e}_{self.epoch}")
            self.cnt[e] = 0

    def add(self, eng, fn, reads=(), writes=(), dma_slot=None, ndma=1):
        op = Op()
        op.eng = eng
        op.fn = fn
        op.is_dma = dma_slot is not None
        op.ndma = ndma
        deps = list(self.bar_deps)
        for r in reads:
            w = self.last_w.get(r)
            if w is not None:
                deps.append(w)
        for w_ in writes:
            w = self.last_w.get(w_)
            if w is not None:
                deps.append(w)
            deps.extend(self.readers.get(w_, ()))
        if op.is_dma:
            if dma_slot not in self.slot_sem:
                self.slot_sem[dma_slot] = self._sem(f"d_{len(self.slot_sem)}")
                self.slot_cnt[dma_slot] = 0
            self.slot_cnt[dma_slot] += 16 * ndma
            op.ticket = (self.slot_sem[dma_slot], self.slot_cnt[dma_slot])
        else:
            self.cnt[eng] += 1
            op.ticket = (self.eng_sem[eng], self.cnt[eng])
        seen = set()
        dd = []
        for d_ in deps:
            if d_ is op or id(d_) in seen:
                continue
            seen.add(id(d_))
            if d_.eng == "pe" and eng == "pe" and not d_.is_dma:
                continue
            dd.append(d_)
        op.deps = dd
        for r in reads:
            self.readers.setdefault(r, []).append(op)
        for w_ in writes:
            self.last_w[w_] = op
            self.readers[w_] = []
        self.ops[eng].append(op)
        if op.is_dma:
            self.all_dma.append(op)
        return op

    def barrier(self):
        deps = []
        for e in self.ENGS:
            for op in reversed(self.ops[e]):
                if not op.is_dma:
                    deps.append(op)
                    break
        latest = {}
        for op in self.all_dma:
            latest[id(op.ticket[0])] = op
        deps.extend(latest.values())
        self.bar_deps = deps
        self.last_w = {}
        self.readers = {}

    def emit(self, eng, e):
        waited = {}
        for op in self.ops[eng]:
            need = {}
            for d_ in op.deps:
                s, v = d_.ticket
                k = id(s)
                if waited.get(k, 0) >= v:
                    continue
                if k not in need or need[k][1] < v:
                    need[k] = (s, v)
            for k, (s, v) in need.items():
                e.wait_ge(s, v)
                waited[k] = v
                if DBG_WAITS is not None:
                    DBG_WAITS.append((eng, self.ops[eng].index(op), getattr(s, "name", str(s)), v))
            res = op.fn(e)
            s, v = op.ticket
            if op.is_dma:
                assert isinstance(res, (list, tuple)) and len(res) == op.ndma, (len(res), op.ndma)
                for ins in res:
                    ins.then_inc(s, 16)
            else:
                if isinstance(res, (list, tuple)):
                    res = res[-1]
                res.then_inc(s, 1)

    def final_wait(self, e, ops):
        for op in ops:
            s, v = op.ticket
            e.wait_ge(s, v)


def kr(name, *idx):
    return (name,) + idx


HD_A = 64
NQ_A = 32
NKV_A = 4
TK = T + 128


class Builder:
    GW = 256
    GRP = 4

    def __init__(self):
        self.nc = bass.Bass("TRN2", target_bir_lowering=False)
        self.stack = ExitStack()
        self.S = Sched(self.nc, self.stack)
        self.out_ops = []
        self.cnt = {}

    def din(self, name, shape, dt=F32):
        return self.nc.dram_tensor(name, list(shape), dt, kind="ExternalInput").ap()

    def dout(self, name, shape, dt=F32):
        return self.nc.dram_tensor(name, list(shape), dt, kind="ExternalOutput").ap()

    def sb(self, name, shape, dt):
        return self.stack.enter_context(self.nc.sbuf_tensor(name, list(shape), dt))

    def ps(self, name, shape=(128, 512), dt=F32):
        return self.stack.enter_context(self.nc.psum_tensor(name, list(shape), dt))

    def rot(self, name, n):
        v = self.cnt.get(name, 0)
        self.cnt[name] = v + 1
        return v % n

    def alloc(self):
        self.hT = self.sb("hT", [128, KC, T], F32)
        self.xn = self.sb("xn", [128, KC, T], BF16)
        self.BIG = self.sb("BIG", [128, 16384], BF16)
        self.WA = self.sb("WA", [128, 16384], BF16)
        self.WB = self.sb("WB", [128, 16384], BF16)
        self.MISC = self.sb("MISC", [128, 6144], BF16)
        self.sb_masks = self.sb("masks", [128, 384], BF16)
        self.ones = self.sb("ones", [128, 128], BF16)
        self.bones = self.sb("bones", [128, 128], BF16)
        self.ident = self.sb("ident_sb", [128, 128], BF16)
        self.epsc = self.sb("epsc", [128, 1], F32)
        self.gcol = self.sb("gcolt", [128, KC], F32)
        self.smallf = self.sb("smallf", [128, 64], F32)
        self.smallf2 = self.sb("smallf2", [128, 128], F32)
        self.psum = [self.ps(f"ps{i}") for i in range(8)]
        S = self.S
        S.add("dve", lambda e: e.memset(self.ones[:], 1.0), writes=["ones"])
        S.add("dve", lambda e: e.memset(self.epsc[:], EPS), writes=["epsc"])
        S.add("dve", lambda e: e.memset(self.bones[:], 0.0), writes=["bones"])
        S.add("dve", lambda e: e.memset(self.bones[0:64, 0:64], 1.0), reads=["bones"], writes=["bones"])
        S.add("dve", lambda e: e.memset(self.bones[64:128, 64:128], 1.0), reads=["bones"], writes=["bones"])

    def load_ident(self, ident_d):
        self.S.add("pool", lambda e: [e.dma_start(out=self.ident[:], in_=ident_d)], writes=["ident"],
                   dma_slot="ident")

    def load_h(self, xT):
        S = self.S
        src = xT.rearrange("(c p) t -> p c t", p=128)
        for q in range(4):
            sl = slice(q * 4, (q + 1) * 4)
            S.add("sp", lambda e, sl=sl: [e.dma_start(out=self.hT[:, sl, :], in_=src[:, sl, :])],
                  writes=[kr("h", c) for c in range(q * 4, q * 4 + 4)], dma_slot=("ldh", q))

    def store_h(self, outT):
        S = self.S
        dst = outT.rearrange("(c p) t -> p c t", p=128)
        for q in range(4):
            sl = slice(q * 4, (q + 1) * 4)
            op = S.add("sp", lambda e, sl=sl: [e.dma_start(out=dst[:, sl, :], in_=self.hT[:, sl, :])],
                       reads=[kr("h", c) for c in range(q * 4, q * 4 + 4)], dma_slot=("sth", q))
            self.out_ops.append(op)

    def rmsnorm(self, src, dst, n, gvec_d, tag, skey, dkey):
        S = self.S
        rstd = self.BIG[:, 0:2 * n].bitcast(F32)
        sq = [self.BIG[:, 2 * n + i * n: 2 * n + (i + 1) * n] for i in range(2)]
        S.add("sp", lambda e: [e.dma_start(out=self.gcol[:], in_=gvec_d)], writes=["gcol"], dma_slot="gcol")
        nh = (n + 511) // 512
        w = n // nh
        pss = [self.psum[6], self.psum[7]]
        for c in range(KC):
            S.add("act", lambda e, c=c: e.activation(out=sq[c % 2], in_=src[:, c, :], func=AF.Square),
                  reads=[kr(skey, c)], writes=[kr(tag + "sq", c % 2)])
            for th in range(nh):
                S.add("pe", lambda e, c=c, th=th: e.matmul(
                    pss[th][:, 0:w], lhsT=self.ones[:], rhs=sq[c % 2][:, th * w:(th + 1) * w],
                    start=(c == 0), stop=(c == KC - 1)),
                    reads=[kr(tag + "sq", c % 2), "ones"], writes=[kr("ps", 6 + th)])
        for th in range(nh):
            sl = slice(th * w, (th + 1) * w)
            S.add("act", lambda e, th=th, sl=sl: e.activation(
                out=rstd[:, sl], in_=pss[th][:, 0:w], func=AF.Sqrt, bias=self.epsc[:], scale=1.0 / D),
                reads=[kr("ps", 6 + th), "epsc"], writes=[kr(tag + "rstd", th)])
            S.add("dve", lambda e, sl=sl: e.reciprocal(out=rstd[:, sl], in_=rstd[:, sl]),
                  reads=[kr(tag + "rstd", th)], writes=[kr(tag + "rstd", th)])
        for c in range(KC):
            S.add("dve", lambda e, c=c: e.scalar_tensor_tensor(
                out=dst[:, c, :], in0=src[:, c, :], scalar=self.gcol[:, c:c + 1], in1=rstd,
                op0=ALU.mult, op1=ALU.mult),
                reads=[kr(skey, c), "gcol"] + [kr(tag + "rstd", th) for th in range(nh)], writes=[kr(dkey, c)])

    def ffn(self, gvec, wgu_d, wd_d):
        S = self.S
        GW, GRP = self.GW, self.GRP
        G = GW // 128
        hT, xn = self.hT, self.xn
        S.barrier()
        pre_b = {}
        for s_ in range(2):
            b_ = self.rot("wgu", 2)
            pre_b[("wgu", s_)] = b_
            S.add("pool", lambda e, s_=s_, b_=b_: [e.dma_start(out=self.WA[:, b_ * 8192:(b_ + 1) * 8192], in_=wgu_d[s_])],
                  writes=[kr("wgu", b_)], dma_slot=("wgu", b_))
        b_ = self.rot("wd", 2)
        pre_b[("wd", 0)] = b_
        S.add("pool", lambda e, b_=b_: [e.dma_start(out=self.WB[:, b_ * 8192:(b_ + 1) * 8192], in_=wd_d[0])],
              writes=[kr("wd", b_)], dma_slot=("wd", b_))
        self.rmsnorm(hT, xn, T, gvec, "f", "h", "xn")
        S.barrier()
        wgu = [self.WA[:, i * 8192:(i + 1) * 8192].rearrange("p (c f) -> p c f", c=KC) for i in range(2)]
        wd = [self.WB[:, i * 8192:(i + 1) * 8192].rearrange("p (c f) -> p c f", c=GRP) for i in range(2)]
        actb = [self.BIG[:, i * 4096:(i + 1) * 4096].rearrange("p (c f) -> p c f", c=GRP) for i in range(2)]
        sg = [self.BIG[:, 8192 + i * 1024: 8192 + (i + 1) * 1024].bitcast(F32) for i in range(2)]
        nslab = NFC // G
        ngrp = NFC // GRP
        pg = [self.psum[0], self.psum[1]]
        pu = [self.psum[2], self.psum[3]]
        pd = [self.psum[4], self.psum[5]]

        def load_wgu(s):
            b = self.rot("wgu", 2)
            S.add("pool", lambda e, s=s, b=b: [e.dma_start(
                out=self.WA[:, b * 8192:(b + 1) * 8192], in_=wgu_d[s])],
                writes=[kr("wgu", b)], dma_slot=("wgu", b))
            return b

        def load_wd(g):
            b = self.rot("wd", 2)
            S.add("pool", lambda e, g=g, b=b: [e.dma_start(
                out=self.WB[:, b * 8192:(b + 1) * 8192], in_=wd_d[g])],
                writes=[kr("wd", b)], dma_slot=("wd", b))
            return b

        slab_buf = {}
        wd_buf = {}
        slab_buf[0] = pre_b[("wgu", 0)]
        slab_buf[1] = pre_b[("wgu", 1)]
        wd_buf[0] = pre_b[("wd", 0)]
        for g in range(ngrp):
            ab = self.rot("act", 2)
            for j in range(GRP):
                fc = g * GRP + j
                s = fc // G
                jj = fc % G
                wb = slab_buf[s]
                for th in range(NTH):
                    tsl = slice(th * 512, (th + 1) * 512)

                    def mm_g(e, wb=wb, jj=jj, th=th, tsl=tsl):
                        r = None
                        for c in range(KC):
                            r = e.matmul(pg[th][:], lhsT=wgu[wb][:, c, jj * 128:(jj + 1) * 128],
                                         rhs=xn[:, c, tsl], start=(c == 0), stop=(c == KC - 1))
                        return r

                    def mm_u(e, wb=wb, jj=jj, th=th, tsl=tsl):
                        r = None
                        for c in range(KC):
                            r = e.matmul(pu[th][:], lhsT=wgu[wb][:, c, GW + jj * 128:GW + (jj + 1) * 128],
                                         rhs=xn[:, c, tsl], start=(c == 0), stop=(c == KC - 1))
                        return r
                    xr = [kr("xn", c) for c in range(KC)]
                    S.add("pe", mm_g, reads=[kr("wgu", wb)] + xr, writes=[kr("ps", th)])
                    S.add("pe", mm_u, reads=[kr("wgu", wb)] + xr, writes=[kr("ps", 2 + th)])
                    S.add("act", lambda e, th=th: e.activation(out=sg[th], in_=pg[th][:], func=AF.Silu),
                          reads=[kr("ps", th)], writes=[kr("sg", th)])
                    S.add("dve", lambda e, th=th, ab=ab, j=j, tsl=tsl: e.tensor_tensor(
                        out=actb[ab][:, j, tsl], in0=sg[th], in1=pu[th][:], op=ALU.mult),
                        reads=[kr("sg", th), kr("ps", 2 + th)], writes=[kr("act", ab, j, th)])
                if jj == G - 1 and s + 2 < nslab:
                    slab_buf[s + 2] = load_wgu(s + 2)
            if g + 1 < ngrp:
                wd_buf[g + 1] = load_wd(g + 1)
            db = wd_buf[g]
            for dc in range(KC):
                for th in range(NTH):
                    tsl = slice(th * 512, (th + 1) * 512)
                    pb = self.rot("pd", 2)

                    def mm_d(e, db=db, dc=dc, tsl=tsl, pb=pb, ab=ab):
                        r = None
                        for j in range(GRP):
                            r = e.matmul(pd[pb][:], lhsT=wd[db][:, j, dc * 128:(dc + 1) * 128],
                                         rhs=actb[ab][:, j, tsl], start=(j == 0), stop=(j == GRP - 1))
                        return r
                    S.add("pe", mm_d, reads=[kr("wd", db)] + [kr("act", ab, j, th) for j in range(GRP)],
                          writes=[kr("ps", 4 + pb)])
                    S.add("dve", lambda e, dc=dc, tsl=tsl, pb=pb: e.tensor_tensor(
                        out=hT[:, dc, tsl], in0=hT[:, dc, tsl], in1=pd[pb][:], op=ALU.add),
                        reads=[kr("ps", 4 + pb), kr("h", dc)], writes=[kr("h", dc)])

    def proj_headnorm(self, wv, srcs, dsts, gain_col, wkey):
        S = self.S
        for (rhs_fn, n, rkeys), (dst, wkeys) in zip(srcs, dsts):
            pb = self.rot("phn", 4)
            praw = self.psum[pb]
            pssq = self.psum[4 + pb]
            sqs = self.MISC[:, pb * 512: pb * 512 + n]
            rs = self.MISC[:, 2048 + pb * 1024: 2048 + pb * 1024 + 2 * n].bitcast(F32)

            def mm(e, rhs_fn=rhs_fn, n=n, praw=praw):
                r = None
                for c in range(KC):
                    r = e.matmul(praw[:, 0:n], lhsT=wv[:, c, :], rhs=rhs_fn(c), start=(c == 0), stop=(c == KC - 1))
                return r
            S.add("pe", mm, reads=[wkey] + list(rkeys), writes=[kr("ps", pb)])
            S.add("act", lambda e, n=n, praw=praw, sqs=sqs: e.activation(out=sqs, in_=praw[:, 0:n], func=AF.Square),
                  reads=[kr("ps", pb)], writes=[kr("hsq", pb)])
            S.add("pe", lambda e, n=n, pssq=pssq, sqs=sqs: e.matmul(pssq[:, 0:n], lhsT=self.bones[:], rhs=sqs,
                                                                     start=True, stop=True),
                  reads=[kr("hsq", pb), "bones"], writes=[kr("ps", 4 + pb)])
            S.add("act", lambda e, n=n, pssq=pssq, rs=rs: e.activation(
                out=rs, in_=pssq[:, 0:n], func=AF.Sqrt, bias=self.epsc[:], scale=1.0 / HD_A),
                reads=[kr("ps", 4 + pb), "epsc"], writes=[kr("hrs", pb)])
            S.add("dve", lambda e, rs=rs: e.reciprocal(out=rs, in_=rs), reads=[kr("hrs", pb)], writes=[kr("hrs", pb)])
            S.add("dve", lambda e, n=n, praw=praw, rs=rs, dst=dst: e.scalar_tensor_tensor(
                out=dst, in0=praw[:, 0:n], scalar=gain_col, in1=rs, op0=ALU.mult, op1=ALU.mult),
                reads=[kr("ps", pb), kr("hrs", pb), "gains"], writes=list(wkeys))

    def swa(self, p):
        S = self.S
        hT, xn = self.hT, self.xn
        S.barrier()
        halo_h = self.MISC[:, 0:4096].bitcast(F32).rearrange("p (c t) -> p c t", c=KC)
        kT = self.WB[:, 0:8 * TK].rearrange("p (c t) -> p c t", c=8)
        o_v = 8 * TK
        vext = self.WB[:, o_v:o_v + 9 * 4 * 80].rearrange("p (b h d) -> p b h d", b=9, h=4)
        o_x = o_v + 9 * 4 * 80
        xnh = self.WB[:, o_x:o_x + KC * 128].rearrange("p (c t) -> p c t", c=KC)
        assert o_x + KC * 128 <= 16384
        halo_src = p["haloT"].rearrange("(c p) t -> p c t", p=128)
        S.add("sp", lambda e: [e.dma_start(out=halo_h, in_=halo_src)], writes=[kr("hh", c) for c in range(KC)],
              dma_slot="halo")
        gains = self.smallf[:, 0:2]
        S.add("sp", lambda e: [e.dma_start(out=gains, in_=p["qkn"])], writes=["gains"], dma_slot="gains")
        sinks = self.smallf[:, 8:40]
        S.add("sp", lambda e: [e.dma_start(out=sinks, in_=p["sinks"])], writes=["sinks"], dma_slot="sinks")
        S.add("act", lambda e: e.activation(out=sinks, in_=sinks, func=AF.Exp), reads=["sinks"], writes=["sinks"])
        S.add("pool", lambda e: [e.dma_start(out=self.sb_masks[:], in_=p["masks"])], writes=["masks"], dma_slot="masks")
        masks = self.sb_masks[:, :].rearrange("p (m q) -> p m q", m=3)
        self.osb_full = [self.MISC[:, 2048:4096], self.MISC[:, 4096:6144]]
        self.rmsnorm(halo_h, xnh, 128, p["g_mix"], "hn", "hh", "xnh")
        S.barrier()
        self.rmsnorm(hT, xn, T, p["g_mix"], "m", "h", "xn")
        S.barrier()
        qT = self.BIG[:, :].rearrange("p (c t) -> p c t", c=KC)
        wa = [self.WA[:, i * 8192:(i + 1) * 8192].rearrange("p (c f) -> p c f", c=KC) for i in range(2)]

        def load_wa(src, ncol=512):
            b = self.rot("swa_wa", 2)
            S.add("pool", lambda e, b=b: [e.dma_start(out=self.WA[:, b * 8192: b * 8192 + KC * ncol], in_=src)],
                  writes=[kr("wa", b)], dma_slot=("wa", b))
            return b
        xr = [kr("xn", c) for c in range(KC)]
        xhr = [kr("xnh", c) for c in range(KC)]
        slabs = [("q", i) for i in range(4)] + [("k", i) for i in range(2)]
        bufs = {}
        bufs[0] = load_wa(p["wq"][0])
        for si, (kind, i) in enumerate(slabs):
            if si + 1 < len(slabs):
                k2, i2 = slabs[si + 1]
                bufs[si + 1] = load_wa(p["wq"][i2] if k2 == "q" else p["wk"][i2])
            b = bufs[si]
            for j in range(4):
                wv = wa[b][:, :, j * 128:(j + 1) * 128]
                if kind == "q":
                    ch = i * 4 + j
                    srcs = [(lambda c, th=th: xn[:, c, th * 512:(th + 1) * 512], 512, xr) for th in range(NTH)]
                    dsts = [(qT[:, ch, th * 512:(th + 1) * 512], [kr("qT", ch, th)]) for th in range(NTH)]
                    self.proj_headnorm(wv, srcs, dsts, gains[:, 0:1], kr("wa", b))
                else:
                    ch = i * 4 + j
                    srcs = [(lambda c: xnh[:, c, :], 128, xhr)] + \
                           [(lambda c, th=th: xn[:, c, th * 512:(th + 1) * 512], 512, xr) for th in range(NTH)]
                    dsts = [(kT[:, ch, 0:128], [kr("kT", ch, 0)])] + \
                           [(kT[:, ch, 128 + th * 512:128 + (th + 1) * 512], [kr("kT", ch, 1 + th)]) for th in range(NTH)]
                    self.proj_headnorm(wv, srcs, dsts, gains[:, 1:2], kr("wa", b))
        bv = load_wa(p["wv"], 256)
        wvv = self.WA[:, bv * 8192: bv * 8192 + KC * 256].rearrange("p (c f) -> p c f", c=KC)
        S.add("pool", lambda e: e.memset(vext[:, :, :, 64:65], 1.0), writes=["vones"])
        for blk in range(9):
            pb = 4 + self.rot("pv", 2)

            def mmv(e, blk=blk, pb=pb):
                r = None
                for c in range(KC):
                    lhs = xnh[:, c, :] if blk == 0 else xn[:, c, (blk - 1) * 128: blk * 128]
                    r = e.matmul(self.psum[pb][:, 0:256], lhsT=lhs, rhs=wvv[:, c, :], start=(c == 0), stop=(c == KC - 1))
                return r
            S.add("pe", mmv, reads=[kr("wa", bv)] + (xhr if blk == 0 else xr), writes=[kr("ps", pb)])
            S.add("act", lambda e, blk=blk, pb=pb: e.activation(
                out=vext[:, blk, :, 0:64], in_=self.psum[pb][:, 0:256].rearrange("p (h d) -> p h d", h=4),
                func=AF.Copy), reads=[kr("ps", pb)], writes=[kr("v", blk)])
        S.barrier()
        OT = xn
        et = [self.MISC[:, i * 512:(i + 1) * 512] for i in range(4)]
        den = self.smallf[:, 40:48]
        wo_bufs = {}
        wo_v = [self.WA[:, i * 8192:(i + 1) * 8192].rearrange("p (j c f) -> p j c f", j=4, c=KC) for i in range(2)]

        def load_wo(dg):
            b = self.rot("swa_wo", 2)
            S.add("pool", lambda e, b=b, dg=dg: [e.dma_start(out=self.WA[:, b * 8192:(b + 1) * 8192], in_=p["wo"][dg])],
                  writes=[kr("wo", b)], dma_slot=("wob", b))
            return b
        wo_bufs[0] = load_wo(0)
        wo_bufs[1] = load_wo(1)
        tp_bf = [self.psum[6][:].bitcast(BF16), self.psum[7][:].bitcast(BF16)]
        for n in range(8):
            qs = slice(n * 128, (n + 1) * 128)
            for hk in range(4):
                for half in range(2):
                    grp = hk * 2 + half
                    sp_i = self.rot("sps", 2)
                    ps_prev = self.psum[sp_i * 2]
                    ps_cur = self.psum[sp_i * 2 + 1]
                    e_prev = et[sp_i * 2]
                    e_cur = et[sp_i * 2 + 1]
                    for kb, pst in ((0, ps_prev), (1, ps_cur)):
                        ks = slice((n + kb) * 128, (n + kb + 1) * 128)

                        def mms(e, hk=hk, half=half, ks=ks, pst=pst, qs=qs):
                            r = None
                            for j in range(4):
                                h = hk * 8 + half * 4 + j
                                r = e.matmul(pst[:, j * 128:(j + 1) * 128], lhsT=kT[:, hk * 2 + (h % 2), ks],
                                             rhs=qT[:, h // 2, qs], start=True, stop=True)
                            return r
                        S.add("pe", mms, reads=["kTall", "qTall"], writes=[kr("ps", sp_i * 2 + kb)])
                    midx_prev = 0 if n == 0 else 1
                    for kb, pst, ee, mi in ((0, ps_prev, e_prev, midx_prev), (1, ps_cur, e_cur, 2)):
                        S.add("act", lambda e, pst=pst, ee=ee: e.activation(out=ee, in_=pst[:], func=AF.Exp, scale=HD_A ** -0.5),
                              reads=[kr("ps", sp_i * 2 + kb)], writes=[kr("et", sp_i * 2 + kb)])
                        S.add("pool", lambda e, ee=ee, mi=mi: e.tensor_tensor(
                            out=ee.rearrange("p (j q) -> p j q", j=4), in0=ee.rearrange("p (j q) -> p j q", j=4),
                            in1=masks[:, mi:mi + 1, :].to_broadcast([128, 4, 128]), op=ALU.mult),
                            reads=[kr("et", sp_i * 2 + kb), "masks"], writes=[kr("et", sp_i * 2 + kb)])
                    po_i = 4 + self.rot("spo", 2)
                    po = self.psum[po_i]

                    def mmo(e, n=n, hk=hk, po=po, e_prev=e_prev, e_cur=e_cur):
                        r = None
                        for j in range(4):
                            e.matmul(po[:, j * 128:j * 128 + 65], lhsT=e_prev[:, j * 128:(j + 1) * 128],
                                     rhs=vext[:, n, hk, 0:65], start=True, stop=False)
                            r = e.matmul(po[:, j * 128:j * 128 + 65], lhsT=e_cur[:, j * 128:(j + 1) * 128],
                                         rhs=vext[:, n + 1, hk, 0:65], start=False, stop=True)
                        return r
                    S.add("pe", mmo, reads=[kr("et", sp_i * 2), kr("et", sp_i * 2 + 1), "vall"], writes=[kr("ps", po_i)])
                    h0 = hk * 8 + half * 4
                    pov = po[:, :].rearrange("p (j d) -> p j d", j=4)
                    dn = den[:, 0:4] if (grp % 2 == 0) else den[:, 4:8]
                    dk = kr("den", grp % 2)
                    S.add("dve", lambda e, pov=pov, dn=dn, h0=h0: e.tensor_tensor(
                        out=dn.rearrange("p (j o) -> p j o", o=1), in0=pov[:, :, 64:65],
                        in1=sinks[:, h0:h0 + 4].rearrange("p (j o) -> p j o", o=1), op=ALU.add),
                        reads=[kr("ps", po_i), "sinks"], writes=[dk])
                    S.add("dve", lambda e, dn=dn: e.reciprocal(out=dn, in_=dn), reads=[dk], writes=[dk])
                    ob = self.osb_full[n % 2][:, h0 * 64:(h0 + 4) * 64].rearrange("p (j d) -> p j d", j=4)
                    S.add("dve", lambda e, pov=pov, dn=dn, ob=ob: e.tensor_tensor(
                        out=ob, in0=pov[:, :, 0:64], in1=dn.rearrange("p (j o) -> p j o", o=1).to_broadcast([128, 4, 64]),
                        op=ALU.mult),
                        reads=[kr("ps", po_i), dk], writes=[kr("osb", n % 2, grp)])
            for c4 in range(4):
                tp_i = self.rot("tp", 2)
                tpv = tp_bf[tp_i]

                def mmt(e, c4=c4, tpv=tpv, n=n):
                    r = None
                    for j in range(4):
                        c = c4 * 4 + j
                        r = e.transpose(tpv[:, j * 128:(j + 1) * 128], self.osb_full[n % 2][:, c * 128:(c + 1) * 128],
                                        self.ident[:])
                    return r
                S.add("pe", mmt, reads=[kr("osb", n % 2, g_) for g_ in range(8)] + ["ident"], writes=[kr("ps", 6 + tp_i)])
                S.add("act", lambda e, c4=c4, tpv=tpv, qs=qs: e.activation(
                    out=OT[:, c4 * 4:(c4 + 1) * 4, qs], in_=tpv[:, 0:512].rearrange("p (j q) -> p j q", j=4), func=AF.Copy),
                    reads=[kr("ps", 6 + tp_i)], writes=[kr("OT", n, c4)])
        S.barrier()
        for dg in range(4):
            b = wo_bufs[dg]
            for j in range(4):
                dc = dg * 4 + j
                for th in range(NTH):
                    tsl = slice(th * 512, (th + 1) * 512)
                    pb = self.rot("pwo", 2)

                    def mmw(e, b=b, j=j, tsl=tsl, pb=pb):
                        r = None
                        for c in range(KC):
                            r = e.matmul(self.psum[pb][:], lhsT=wo_v[b][:, j, c, :], rhs=OT[:, c, tsl],
                                         start=(c == 0), stop=(c == KC - 1))
                        return r
                    S.add("pe", mmw, reads=[kr("wo", b)], writes=[kr("ps", pb)])
                    S.add("dve", lambda e, dc=dc, tsl=tsl, pb=pb: e.tensor_tensor(
                        out=hT[:, dc, tsl], in0=hT[:, dc, tsl], in1=self.psum[pb][:], op=ALU.add),
                        reads=[kr("ps", pb), kr("h", dc)], writes=[kr("h", dc)])
            if dg + 2 < 4:
                wo_bufs[dg + 2] = load_wo(dg + 2)

    def gla(self, p, with_output):
        S = self.S
        hT, xn = self.hT, self.xn
        S.barrier()
        self.rmsnorm(hT, xn, T, p["g_mix"], "g", "h", "xn")
        S.barrier()
        BIG, MISC = self.BIG, self.MISC
        S32 = BIG[:, 13568:15616].bitcast(F32).rearrange("p (c f) -> p c f", c=2)
        Sbf = BIG[:, 8192:9216].rearrange("p (c f) -> p c f", c=2)
        OTh = BIG[:, 3072:7168].rearrange("p (c t) -> p c t", c=4)
        glT = BIG[0:64, 9216:10240]
        wgb = BIG[0:64, 10240:11264]
        wgl = BIG[:, 11264:11520].rearrange("p (c f) -> p c f", c=KC)
        gain = BIG[:, 11520:12544].bitcast(F32)
        Ltmp = BIG[:, 12544:13568].bitcast(F32)
        sp_tok = MISC[:, 0:256]
        eb = MISC[:, 256:768].bitcast(F32)
        enb = MISC[:, 768:1280].bitcast(F32)
        expf = MISC[:, 1280:1792].bitcast(F32)
        qd = MISC[:, 1792:2048].rearrange("p (c t) -> p c t", c=2)
        ki = MISC[:, 2048:2304].rearrange("p (c t) -> p c t", c=2)
        kend = MISC[:, 2304:2560]
        vsb = MISC[:, 2560:3072]
        am = MISC[:, 3072:3200]
        e1 = MISC[:, 3200:3712].bitcast(F32)
        on = MISC[:, 3712:4736].bitcast(F32)
        sr = MISC[:, 4736:5248]
        og = MISC[:, 5248:5760]
        dcols = self.smallf[:, 0:8]
        dprev = self.smallf[:, 8:32].rearrange("p (s c) -> p s c", s=3)
        ssq = self.smallf[:, 32:33]
        rstd = self.smallf[:, 33:34]
        masks = self.sb_masks[:, :].rearrange("p (m q) -> p m q", m=3)
        ps = self.psum
        import os
        dbg = int(os.environ.get("GLA_DBG", "0"))
        if dbg == 10:
            return
        S.add("pool", lambda e: [e.dma_start(out=self.sb_masks[:], in_=p["masks"])], writes=["masks"], dma_slot="masks")
        S.add("pool", lambda e: [e.dma_start(out=wgb, in_=p["wgb"])], writes=["wgb"], dma_slot="wgb")
        S.add("pool", lambda e: [e.dma_start(out=BIG[:, 11264:11520], in_=p["wgl"])], writes=["wgl"], dma_slot="wgl")
        S.add("sp", lambda e: [e.dma_start(out=gain, in_=p["onorm"])], writes=["gain"], dma_slot="gain")
        S.add("sp", lambda e: [e.dma_start(out=self.smallf[:, 8:32], in_=p["dprev"].rearrange("s p c -> p s c"))],
              writes=["dprev"], dma_slot="dprev") if False else None
        for s_ in range(3):
            S.add("sp", lambda e, s_=s_: [e.dma_start(out=dprev[:, s_, :], in_=p["dprev"][s_])], writes=[kr("dprev", s_)],
                  dma_slot=("dprev", s_))
        if dbg == 11:
            S.barrier()
            S.add("dve", lambda e: e.memset(dcols, 1.0), writes=["dcols"])
            return
        S.add("dve", lambda e: e.memset(glT, 0.0), writes=["glT"])
        S.add("dve", lambda e: e.memset(glT[32:64, :], 1.0), reads=["glT"], writes=["glT"])
        xr = [kr("xn", c) for c in range(KC)]
        for th in range(NTH):
            tsl = slice(th * 512, (th + 1) * 512)

            def mmgl(e, tsl=tsl, th=th):
                r = None
                for c in range(KC):
                    r = e.matmul(ps[th][0:16, :], lhsT=wgl[:, c, :], rhs=xn[:, c, tsl], start=(c == 0), stop=(c == KC - 1))
                return r
            S.add("pe", mmgl, reads=["wgl"] + xr, writes=[kr("ps", th)])
            S.add("act", lambda e, tsl=tsl, th=th: e.activation(out=glT[0:16, tsl], in_=ps[th][0:16, :], func=AF.Copy),
                  reads=[kr("ps", th), "glT"], writes=["glT"])
        S.add("dve", lambda e: e.memset(dcols, 1.0), writes=["dcols"])
        S.barrier()
        import os
        dbg = int(os.environ.get("GLA_DBG", "0"))
        if dbg == 1:
            return
        slabv = [self.WA[:, 0:8192], self.WA[:, 8192:16384], self.WB[:, 0:8192], self.WB[:, 8192:16384]]
        nbuf = 4

        def load_slab(src):
            b = self.rot("gla_slab", nbuf)
            S.add("pool", lambda e, b=b: [e.dma_start(out=slabv[b], in_=src)], writes=[kr("slab", b)], dma_slot=("gslab", b))
            return b
        for hd in range(4):
            bA = load_slab(p["wA"][hd])
            bB = load_slab(p["wB"][hd])
            if with_output:
                bC = load_slab(p["wC"][hd])
                bO = load_slab(p["woh"][hd])
            wA = slabv[bA].rearrange("p (c f) -> p c f", c=KC)
            wBv = slabv[bB].rearrange("p (c f) -> p c f", c=KC)
            if with_output:
                wCv = slabv[bC].rearrange("p (c f) -> p c f", c=KC)
                wOv = slabv[bO].rearrange("p (c f) -> p c f", c=4)
            for dkc in range(2):
                idx = hd * 2 + dkc
                if with_output:
                    S.add("sp", lambda e, idx=idx, dkc=dkc: [e.dma_start(out=S32[:, dkc, :], in_=p["sprev"][0, idx])],
                          writes=[kr("S32", dkc)], dma_slot=("s32", dkc))
                    for s_ in (1, 2):
                        S.add("sp", lambda e, idx=idx, s_=s_: [e.dma_start(out=Ltmp, in_=p["sprev"][s_, idx])],
                              writes=["Ltmp"], dma_slot="ltmp")
                        S.add("dve", lambda e, idx=idx, s_=s_, dkc=dkc: e.scalar_tensor_tensor(
                            out=S32[:, dkc, :], in0=S32[:, dkc, :], scalar=dprev[:, s_, idx:idx + 1], in1=Ltmp,
                            op0=ALU.mult, op1=ALU.add),
                            reads=[kr("S32", dkc), "Ltmp", kr("dprev", s_)], writes=[kr("S32", dkc)])
                elif dbg == 36:
                    altm = BIG[:, 0:2048].bitcast(F32).rearrange("p (c f) -> p c f", c=2)
                    S.add("dve", lambda e, dkc=dkc, altm=altm: e.memset(altm[:, dkc, :], 0.0), writes=[kr("alt", dkc)])
                    S.add("dve", lambda e, dkc=dkc: e.memset(S32[:, dkc, :], 0.0), writes=[kr("S32", dkc)])
                elif dbg == 35:
                    S.add("dve", lambda e, dkc=dkc: e.tensor_copy(out=S32[:, dkc, :], in_=gain), reads=["gain"], writes=[kr("S32", dkc)])
                else:
                    S.add("dve", lambda e, dkc=dkc: e.memset(S32[:, dkc, :], 0.0), writes=[kr("S32", dkc)])
                if with_output:
                    S.add("act", lambda e, dkc=dkc: e.activation(out=Sbf[:, dkc, :], in_=S32[:, dkc, :], func=AF.Copy),
                          reads=[kr("S32", dkc)], writes=[kr("Sbf", dkc)])
            for blk in range(8):
                if dbg == 2 and (hd, blk) != (0, 0):
                    continue
                bs = slice(blk * 128, (blk + 1) * 128)
                if with_output:
                    def mmqk(e, bs=bs, wA=wA):
                        r = None
                        for j in range(4):
                            for c in range(KC):
                                r = e.matmul(ps[0][:, j * 128:(j + 1) * 128], lhsT=wA[:, c, j * 128:(j + 1) * 128],
                                             rhs=xn[:, c, bs], start=(c == 0), stop=(c == KC - 1))
                        return r
                    S.add("pe", mmqk, reads=[kr("slab", bA)] + xr, writes=[kr("ps", 0)])

                def mmk(e, bs=bs, wA=wA):
                    r = None
                    for c in range(KC):
                        r = e.matmul(ps[1][:, 0:256], lhsT=xn[:, c, bs], rhs=wA[:, c, 256:512], start=(c == 0), stop=(c == KC - 1))
                    return r
                S.add("pe", mmk, reads=[kr("slab", bA)] + xr, writes=[kr("ps", 1)])
                S.add("pe", lambda e, bs=bs, hd=hd: e.matmul(ps[1][:, 256:512], lhsT=glT[0:33, bs],
                                                             rhs=wgb[0:33, hd * 256:(hd + 1) * 256], start=True, stop=True),
                      reads=["glT", "wgb"], writes=[kr("ps", 1)])

                def mmv(e, bs=bs, wBv=wBv):
                    r = None
                    for c in range(KC):
                        r = e.matmul(ps[2][:], lhsT=xn[:, c, bs], rhs=wBv[:, c, :], start=(c == 0), stop=(c == KC - 1))
                    return r
                S.add("pe", mmv, reads=[kr("slab", bB)] + xr, writes=[kr("ps", 2)])
                S.add("act", lambda e: e.activation(out=vsb, in_=ps[2][:], func=AF.Copy), reads=[kr("ps", 2)], writes=["vsb"])
                if dbg == 20:
                    break
                S.add("act", lambda e: e.activation(out=e1, in_=ps[1][:, 256:512], func=AF.Exp, scale=-1.0),
                      reads=[kr("ps", 1)], writes=["e1"])
                S.add("act", lambda e: e.activation(out=sp_tok, in_=e1, func=AF.Ln, bias=1.0), reads=["e1"], writes=["sp"])
                if dbg == 21:
                    break

                def mmb(e):
                    r = None
                    for dkc in range(2):
                        r = e.matmul(ps[4][:, dkc * 128:(dkc + 1) * 128], lhsT=sp_tok[:, dkc * 128:(dkc + 1) * 128],
                                     rhs=masks[:, 0, :], start=True, stop=True)
                    return r
                S.add("pe", mmb, reads=["sp", "masks"], writes=[kr("ps", 4)])
                S.add("pe", lambda e: e.matmul(ps[4][:, 256:512], lhsT=masks[:, 1, :], rhs=sp_tok, start=True, stop=True),
                      reads=["sp", "masks"], writes=[kr("ps", 4)])
                S.add("act", lambda e: e.activation(out=eb, in_=ps[4][:, 0:256], func=AF.Exp), reads=[kr("ps", 4)], writes=["eb"])
                S.add("act", lambda e: e.activation(out=expf, in_=ps[4][:, 256:512], func=AF.Exp), reads=[kr("ps", 4)],
                      writes=["expf"])
                S.add("dve", lambda e: e.tensor_tensor(out=kend, in0=ps[1][:, 0:256], in1=expf, op=ALU.mult),
                      reads=[kr("ps", 1), "expf"], writes=["kend"])
                if with_output:
                    S.add("act", lambda e: e.activation(out=enb, in_=ps[4][:, 0:256], func=AF.Exp, scale=-1.0),
                          reads=[kr("ps", 4)], writes=["enb"])
                    S.add("dve", lambda e: e.scalar_tensor_tensor(
                        out=qd, in0=ps[0][:, 0:256].rearrange("p (c t) -> p c t", c=2), scalar=256.0 ** -0.5,
                        in1=eb.rearrange("p (c t) -> p c t", c=2), op0=ALU.mult, op1=ALU.mult),
                        reads=[kr("ps", 0), "eb"], writes=["qd"])
                    S.add("dve", lambda e: e.tensor_tensor(
                        out=ki, in0=ps[0][:, 256:512].rearrange("p (c t) -> p c t", c=2),
                        in1=enb.rearrange("p (c t) -> p c t", c=2), op=ALU.mult),
                        reads=[kr("ps", 0), "enb"], writes=["ki"])

                    def mma(e):
                        r = None
                        for dkc in range(2):
                            r = e.matmul(ps[5][:, 0:128], lhsT=ki[:, dkc, :], rhs=qd[:, dkc, :], start=(dkc == 0), stop=(dkc == 1))
                        return r
                    S.add("pe", mma, reads=["ki", "qd"], writes=[kr("ps", 5)])
                    S.add("dve", lambda e: e.tensor_tensor(out=am, in0=ps[5][:, 0:128], in1=masks[:, 2, :], op=ALU.mult),
                          reads=[kr("ps", 5), "masks"], writes=["am"])

                    def mmo(e):
                        e.matmul(ps[6][:], lhsT=am, rhs=vsb, start=True, stop=False)
                        e.matmul(ps[6][:], lhsT=qd[:, 0, :], rhs=Sbf[:, 0, :], start=False, stop=False)
                        return e.matmul(ps[6][:], lhsT=qd[:, 1, :], rhs=Sbf[:, 1, :], start=False, stop=True)
                    S.add("pe", mmo, reads=["am", "vsb", "qd", kr("Sbf", 0), kr("Sbf", 1)], writes=[kr("ps", 6)])

                    def mmr(e, bs=bs, wCv=wCv):
                        r = None
                        for c in range(KC):
                            r = e.matmul(ps[3][:], lhsT=xn[:, c, bs], rhs=wCv[:, c, :], start=(c == 0), stop=(c == KC - 1))
                        return r
                    S.add("pe", mmr, reads=[kr("slab", bC)] + xr, writes=[kr("ps", 3)])
                if dbg == 22:
                    break
                for dkc in range(2):
                    S.add("pe", lambda e, dkc=dkc: e.matmul(ps[7][:], lhsT=kend[:, dkc * 128:(dkc + 1) * 128], rhs=vsb,
                                                            start=True, stop=True),
                          reads=["kend", "vsb"], writes=[kr("ps", 7)])
                    if dbg == 24:
                        continue
                    dec = self.smallf[:, 34 + dkc:35 + dkc]
                    S.add("dve", lambda e, dkc=dkc, dec=dec: e.tensor_copy(out=dec, in_=eb[:, dkc * 128 + 127: dkc * 128 + 128]),
                          reads=["eb"], writes=[kr("dec", dkc)])
                    if dbg == 25:
                        continue
                    if dbg == 26:
                        S.add("dve", lambda e, dkc=dkc, dec=dec: e.tensor_tensor(
                            out=S32[:, dkc, :], in0=S32[:, dkc, :], in1=ps[7][:], op=ALU.add),
                            reads=[kr("S32", dkc), kr("ps", 7)], writes=[kr("S32", dkc)])
                        continue
                    if dbg == 28:
                        S.add("dve", lambda e, dkc=dkc, dec=dec: e.tensor_copy(out=on, in_=ps[7][:]),
                            reads=[kr("ps", 7)], writes=["on"])
                        continue
                    if dbg == 32:
                        S.add("act", lambda e, dkc=dkc, dec=dec: e.activation(out=S32[:, dkc, :], in_=gain, func=AF.Copy),
                            reads=[kr("S32", dkc), "gain"], writes=[kr("S32", dkc)])
                        continue
                    if dbg in (33, 36):
                        alt = BIG[:, 0:2048].bitcast(F32).rearrange("p (c f) -> p c f", c=2)
                        S.add("dve", lambda e, dkc=dkc, alt=alt: e.tensor_copy(out=alt[:, dkc, :], in_=gain),
                            reads=[kr("alt", dkc), "gain"], writes=[kr("alt", dkc)])
                        continue
                    if dbg == 34:
                        S.add("dve", lambda e, dkc=dkc: e.tensor_copy(out=S32[:, dkc, :], in_=gain),
                            reads=["gain"], writes=[kr("S32x", dkc)])
                        continue
                    if dbg in (31, 35):
                        S.add("dve", lambda e, dkc=dkc, dec=dec: e.tensor_copy(out=S32[:, dkc, :], in_=gain),
                            reads=[kr("S32", dkc), "gain"], writes=[kr("S32", dkc)])
                        continue
                    if dbg in (29, 30):
                        S.add("dve", lambda e, dkc=dkc, dec=dec: e.tensor_copy(out=S32[:, dkc, :], in_=on),
                            reads=[kr("S32", dkc), "on"], writes=[kr("S32", dkc)])
                        continue
                    if dbg == 27:
                        S.add("dve", lambda e, dkc=dkc, dec=dec: e.tensor_copy(out=S32[:, dkc, :], in_=ps[7][:]),
                            reads=[kr("S32", dkc), kr("ps", 7)], writes=[kr("S32", dkc)])
                        continue
                    S.add("dve", lambda e, dkc=dkc, dec=dec: e.scalar_tensor_tensor(
                        out=S32[:, dkc, :], in0=S32[:, dkc, :], scalar=dec, in1=ps[7][:],
                        op0=ALU.mult, op1=ALU.add),
                        reads=[kr("S32", dkc), kr("dec", dkc), kr("ps", 7)], writes=[kr("S32", dkc)])
                    if with_output:
                        S.add("act", lambda e, dkc=dkc: e.activation(out=Sbf[:, dkc, :], in_=S32[:, dkc, :], func=AF.Copy),
                              reads=[kr("S32", dkc)], writes=[kr("Sbf", dkc)])
                    elif dbg != 23:
                        idx = hd * 2 + dkc
                        S.add("dve", lambda e, dkc=dkc, idx=idx: e.tensor_tensor(
                            out=dcols[:, idx:idx + 1], in0=dcols[:, idx:idx + 1],
                            in1=self.smallf[:, 34 + dkc:35 + dkc], op=ALU.mult),
                            reads=[kr("dec", dkc), "dcols"], writes=["dcols"])
                if with_output:
                    S.add("act", lambda e: e.activation(out=on, in_=ps[6][:], func=AF.Square, accum_out=ssq),
                          reads=[kr("ps", 6)], writes=["on", "ssq"])
                    S.add("act", lambda e: e.activation(out=rstd, in_=ssq, func=AF.Sqrt, bias=self.epsc[:], scale=1.0 / 512),
                          reads=["ssq"], writes=["rstd"])
                    S.add("dve", lambda e: e.reciprocal(out=rstd, in_=rstd), reads=["rstd"], writes=["rstd"])
                    S.add("dve", lambda e: e.scalar_tensor_tensor(out=on, in0=ps[6][:], scalar=rstd, in1=gain,
                                                                  op0=ALU.mult, op1=ALU.mult),
                          reads=[kr("ps", 6), "rstd", "gain", "on"], writes=["on"])
                    S.add("act", lambda e: e.activation(out=sr, in_=ps[3][:], func=AF.Silu), reads=[kr("ps", 3)], writes=["sr"])
                    S.add("dve", lambda e: e.tensor_tensor(out=og, in0=on, in1=sr, op=ALU.mult), reads=["on", "sr"], writes=["og"])
                    tpv = ps[5][:].bitcast(BF16)[:, 512:1024]

                    def mmt(e, tpv=tpv):
                        r = None
                        for j in range(4):
                            r = e.transpose(tpv[:, j * 128:(j + 1) * 128], og[:, j * 128:(j + 1) * 128], self.ident[:])
                        return r
                    S.add("pe", mmt, reads=["og", "ident"], writes=[kr("ps", 5)])
                    S.add("act", lambda e, bs=bs, tpv=tpv: e.activation(
                        out=OTh[:, :, bs], in_=tpv.rearrange("p (j q) -> p j q", j=4), func=AF.Copy),
                        reads=[kr("ps", 5)], writes=[kr("OTh", blk)])
            if with_output:
                for dc in range(KC):
                    for th in range(NTH):
                        tsl = slice(th * 512, (th + 1) * 512)
                        pb = 2 + self.rot("gwo", 2)

                        def mmw(e, dc=dc, tsl=tsl, pb=pb, wOv=wOv):
                            r = None
                            for c in range(4):
                                r = e.matmul(ps[pb][:], lhsT=wOv[:, c, dc * 128:(dc + 1) * 128], rhs=OTh[:, c, tsl],
                                             start=(c == 0), stop=(c == 3))
                            return r
                        S.add("pe", mmw, reads=[kr("slab", bO)] + [kr("OTh", b_) for b_ in range(8)], writes=[kr("ps", pb)])
                        S.add("dve", lambda e, dc=dc, tsl=tsl, pb=pb: e.tensor_tensor(
                            out=hT[:, dc, tsl], in0=hT[:, dc, tsl], in1=ps[pb][:], op=ALU.add),
                            reads=[kr("ps", pb), kr("h", dc)], writes=[kr("h", dc)])
            elif dbg != 30:
                for dkc in range(2):
                    idx = hd * 2 + dkc
                    op = S.add("sp", lambda e, idx=idx, dkc=dkc: [e.dma_start(out=p["sloc"][idx], in_=S32[:, dkc, :])],
                               reads=[kr("S32", dkc)], dma_slot=("sloc", dkc))
                    self.out_ops.append(op)
        if not with_output:
            op = S.add("sp", lambda e: [e.dma_start(out=p["dtot"], in_=dcols)], reads=["dcols"], dma_slot="dtot")
            self.out_ops.append(op)

    def wo_proj(self, wo_d):
        S = self.S
        hT, OT = self.hT, self.xn
        wo_v = [self.WA[:, i * 8192:(i + 1) * 8192].rearrange("p (j c f) -> p j c f", j=4, c=KC) for i in range(2)]
        bufs = {}

        def load_wo(dg):
            b = self.rot("wo_g", 2)
            S.add("pool", lambda e, b=b, dg=dg: [e.dma_start(out=self.WA[:, b * 8192:(b + 1) * 8192], in_=wo_d[dg])],
                  writes=[kr("wog", b)], dma_slot=("wog", b))
            return b
        bufs[0] = load_wo(0)
        bufs[1] = load_wo(1)
        for dg in range(4):
            b = bufs[dg]
            for j in range(4):
                dc = dg * 4 + j
                for th in range(NTH):
                    tsl = slice(th * 512, (th + 1) * 512)
                    pb = self.rot("pwog", 2)

                    def mmw(e, b=b, j=j, tsl=tsl, pb=pb):
                        r = None
                        for c in range(KC):
                            r = e.matmul(self.psum[pb][:], lhsT=wo_v[b][:, j, c, :], rhs=OT[:, c, tsl],
                                         start=(c == 0), stop=(c == KC - 1))
                        return r
                    S.add("pe", mmw, reads=[kr("wog", b)], writes=[kr("ps", pb)])
                    S.add("dve", lambda e, dc=dc, tsl=tsl, pb=pb: e.tensor_tensor(
                        out=hT[:, dc, tsl], in0=hT[:, dc, tsl], in1=self.psum[pb][:], op=ALU.add),
                        reads=[kr("ps", pb), kr("h", dc)], writes=[kr("h", dc)])
            if dg + 2 < 4:
                bufs[dg + 2] = load_wo(dg + 2)

    def dsa_prep(self, p):
        S = self.S
        S.barrier()
        self.rmsnorm(self.hT, self.xn, T, p["g_mix"], "dp", "h", "xn")
        S.barrier()
        xn, ps = self.xn, self.psum
        wkk = self.WA[:, 0:KC * 320].rearrange("p (c f) -> p c f", c=KC)
        S.add("pool", lambda e: [e.dma_start(out=self.WA[:, 0:KC * 320], in_=p["wkk"])], writes=["wkk"], dma_slot="wkk")
        gb_ = self.MISC[:, 0:768].bitcast(F32)
        S.add("sp", lambda e: [e.dma_start(out=gb_[:, 0:256], in_=p["kvg"])], writes=["kvg"], dma_slot="kvg")
        S.add("sp", lambda e: [e.dma_start(out=gb_[:, 256:320], in_=p["ikg"])], writes=["ikg"], dma_slot="ikg")
        S.add("sp", lambda e: [e.dma_start(out=gb_[:, 320:384], in_=p["ikb"])], writes=["ikb"], dma_slot="ikb")
        junk = self.MISC[:, 768:1280].bitcast(F32)
        st = self.smallf
        for blk in range(8):
            bs = slice(blk * 128, (blk + 1) * 128)
            pb = self.rot("dpp", 2)
            ob = self.rot("dpo", 2)
            oc = self.MISC[:, 1280 + ob * 640: 1280 + ob * 640 + 512].bitcast(F32)
            ok = self.MISC[:, 1280 + ob * 640 + 512: 1280 + (ob + 1) * 640].bitcast(F32)
            xc = self.MISC[:, 2560:2688].bitcast(F32)

            def mm(e, bs=bs, pb=pb):
                r = None
                for c in range(KC):
                    r = e.matmul(ps[pb][:, 0:320], lhsT=xn[:, c, bs], rhs=wkk[:, c, :], start=(c == 0), stop=(c == KC - 1))
                return r
            S.add("pe", mm, reads=["wkk"] + [kr("xn", c) for c in range(KC)], writes=[kr("ps", pb)])
            S.add("act", lambda e, pb=pb: e.activation(out=junk, in_=ps[pb][:, 0:256], func=AF.Square, accum_out=st[:, 0:1]),
                  reads=[kr("ps", pb)], writes=["junk", "st0"])
            S.add("act", lambda e: e.activation(out=st[:, 1:2], in_=st[:, 0:1], func=AF.Sqrt, bias=self.epsc[:], scale=1.0 / 256),
                  reads=["st0"], writes=["st1"])
            S.add("dve", lambda e: e.reciprocal(out=st[:, 1:2], in_=st[:, 1:2]), reads=["st1"], writes=["st1"])
            S.add("dve", lambda e, pb=pb, oc=oc: e.scalar_tensor_tensor(out=oc, in0=ps[pb][:, 0:256], scalar=st[:, 1:2],
                                                                        in1=gb_[:, 0:256], op0=ALU.mult, op1=ALU.mult),
                  reads=[kr("ps", pb), "st1", "kvg"], writes=[kr("oc", ob)])
            op = S.add("sp", lambda e, bs=bs, oc=oc: [e.dma_start(out=p["ckv"][bs, :], in_=oc)], reads=[kr("oc", ob)],
                       dma_slot=("oc", ob))
            self.out_ops.append(op)
            S.add("dve", lambda e, pb=pb: e.tensor_reduce(out=st[:, 2:3], in_=ps[pb][:, 256:320], axis=AX.X, op=ALU.add),
                  reads=[kr("ps", pb)], writes=["st2"])
            S.add("dve", lambda e: e.tensor_scalar(out=st[:, 2:3], in0=st[:, 2:3], scalar1=1.0 / 64, scalar2=None, op0=ALU.mult),
                  reads=["st2"], writes=["st2"])
            S.add("dve", lambda e, pb=pb, xc=xc: e.tensor_scalar(out=xc, in0=ps[pb][:, 256:320], scalar1=st[:, 2:3], scalar2=None,
                                                                 op0=ALU.subtract),
                  reads=[kr("ps", pb), "st2"], writes=["xc"])
            S.add("act", lambda e, xc=xc: e.activation(out=junk[:, 0:64], in_=xc, func=AF.Square, accum_out=st[:, 3:4]),
                  reads=["xc"], writes=["junk", "st3"])
            S.add("act", lambda e: e.activation(out=st[:, 4:5], in_=st[:, 3:4], func=AF.Sqrt, bias=self.epsc[:], scale=1.0 / 64),
                  reads=["st3"], writes=["st4"])
            S.add("dve", lambda e: e.reciprocal(out=st[:, 4:5], in_=st[:, 4:5]), reads=["st4"], writes=["st4"])
            S.add("dve", lambda e, xc=xc, ok=ok: e.scalar_tensor_tensor(out=ok, in0=xc, scalar=st[:, 4:5], in1=gb_[:, 256:320],
                                                                        op0=ALU.mult, op1=ALU.mult),
                  reads=["xc", "st4", "ikg"], writes=[kr("ok", ob)])
            S.add("dve", lambda e, ok=ok: e.tensor_tensor(out=ok, in0=ok, in1=gb_[:, 320:384], op=ALU.add),
                  reads=[kr("ok", ob), "ikb"], writes=[kr("ok", ob)])
            op = S.add("sp", lambda e, bs=bs, ok=ok: [e.dma_start(out=p["kidx"][bs, :], in_=ok)], reads=[kr("ok", ob)],
                       dma_slot=("ok", ob))
            self.out_ops.append(op)

    def dsa_main(self, p, xT):
        S = self.S
        hT, xn, ps = self.hT, self.xn, self.psum
        BIG, MISC, WA, WB = self.BIG, self.MISC, self.WA, self.WB
        nk = 4 * T
        S.barrier()
        self.rmsnorm(hT, xn, T, p["g_mix"], "dm", "h", "xn")
        S.barrier()
        Hraw = hT[:, :, :].rearrange("p c t -> p (c t)")
        rawq = Hraw[:, 0:4096].rearrange("p (c t) -> p c t", c=4)
        rstdq = Hraw[:, 4096:5120]
        cqT = BIG[:, 0:4096].rearrange("p (c t) -> p c t", c=4)
        selT = BIG[:, 4096:8192].rearrange("p (k q) -> p k q", q=128)
        qabs_bufs = [BIG[:, 8192:12288].rearrange("p (h c q) -> p h c q", h=16, c=2),
                     WB[:, 4096:8192].rearrange("p (h c q) -> p h c q", h=16, c=2)]
        qnT = BIG[:, 12288:14336].rearrange("p (h q) -> p h q", h=16)
        qidxT = BIG[:, 14336:16384].rearrange("p (h q) -> p h q", h=16)
        wcq = WA[:, 0:8192].rearrange("p (c f) -> p c f", c=KC)
        wwi = WA[:, 8192:8448].rearrange("p (c f) -> p c f", c=KC)
        S.add("pool", lambda e: [e.dma_start(out=WA[:, 0:8192], in_=p["wcq"])], writes=["wcq"], dma_slot="wcq")
        S.add("pool", lambda e: [e.dma_start(out=WA[:, 8192:8448], in_=p["wwi"])], writes=["wwi"], dma_slot="wwi")
        gq = self.smallf[:, 0:4]
        gqa = self.smallf[:, 4:6]
        S.add("sp", lambda e: [e.dma_start(out=gq, in_=p["gq"])], writes=["gq"], dma_slot="gq")
        S.add("sp", lambda e: [e.dma_start(out=gqa, in_=p["gqa"])], writes=["gqa"], dma_slot="gqa")
        widx = MISC[:, 5888:6144].bitcast(F32)
        sqt = [MISC[:, 0:512], MISC[:, 512:1024]]
        xr = [kr("xn", c) for c in range(KC)]
        for th in range(NTH):
            tsl = slice(th * 512, (th + 1) * 512)
            for jc in range(4):
                pb = self.rot("dq", 2)

                def mm(e, jc=jc, tsl=tsl, pb=pb):
                    r = None
                    for c in range(KC):
                        r = e.matmul(ps[pb][:], lhsT=wcq[:, c, jc * 128:(jc + 1) * 128], rhs=xn[:, c, tsl],
                                     start=(c == 0), stop=(c == KC - 1))
                    return r
                S.add("pe", mm, reads=["wcq"] + xr, writes=[kr("ps", pb)])
                S.add("act", lambda e, jc=jc, tsl=tsl, pb=pb: e.activation(out=rawq[:, jc, tsl], in_=ps[pb][:], func=AF.Copy),
                      reads=[kr("ps", pb)], writes=[kr("rawq", jc, th)])
                S.add("act", lambda e, jc=jc, pb=pb: e.activation(out=sqt[jc % 2], in_=ps[pb][:], func=AF.Square),
                      reads=[kr("ps", pb)], writes=[kr("sqt", jc % 2)])
                S.add("pe", lambda e, jc=jc: e.matmul(ps[2][:], lhsT=self.ones[:], rhs=sqt[jc % 2], start=(jc == 0), stop=(jc == 3)),
                      reads=[kr("sqt", jc % 2), "ones"], writes=[kr("ps", 2)])
            S.add("act", lambda e, tsl=tsl: e.activation(out=rstdq[:, tsl], in_=ps[2][:], func=AF.Sqrt, bias=self.epsc[:],
                                                         scale=1.0 / 512), reads=[kr("ps", 2)], writes=[kr("rstdq", th)])
            S.add("dve", lambda e, tsl=tsl: e.reciprocal(out=rstdq[:, tsl], in_=rstdq[:, tsl]), reads=[kr("rstdq", th)],
                  writes=[kr("rstdq", th)])
            for jc in range(4):
                S.add("dve", lambda e, jc=jc, tsl=tsl: e.scalar_tensor_tensor(
                    out=cqT[:, jc, tsl], in0=rawq[:, jc, tsl], scalar=gq[:, jc:jc + 1], in1=rstdq[:, tsl],
                    op0=ALU.mult, op1=ALU.mult), reads=[kr("rawq", jc, th), kr("rstdq", th), "gq"], writes=[kr("cqT", jc, th)])
        for blk in range(8):
            bs = slice(blk * 128, (blk + 1) * 128)
            pb = 3

            def mmwi(e, bs=bs):
                r = None
                for c in range(KC):
                    r = e.matmul(ps[3][:, 0:16], lhsT=xn[:, c, bs], rhs=wwi[:, c, :], start=(c == 0), stop=(c == KC - 1))
                return r
            S.add("pe", mmwi, reads=["wwi"] + xr, writes=[kr("ps", 3)])
            S.add("act", lambda e, blk=blk: e.activation(out=widx[:, blk * 16:(blk + 1) * 16], in_=ps[3][:, 0:16], func=AF.Copy,
                                                         scale=16 ** -0.5 * 64 ** -0.5),
                  reads=[kr("ps", 3)], writes=[kr("widx", blk)])
        S.barrier()
        H16 = Hraw.bitcast(BF16)
        Wsc = Hraw[:, 0:4096]
        ckvT = H16[:, 8192:8192 + 2 * nk].rearrange("p (c k) -> p c k", c=2)
        ckv1 = H16[:, 16384:16384 + (nk // 128) * 260].rearrange("p (k f) -> p k f", f=260)
        kidx = H16[0:64, 24832:24832 + nk]
        dmt = [H16[:, 29184:29696], H16[:, 30208:30720]]
        nmt = [WB[:, 8192:8704], WB[:, 9216:9728]]
        S.add("pool", lambda e: [e.dma_start(out=H16[:, 8192:8192 + 2 * nk], in_=p["ckvT"])], writes=["ckvT"], dma_slot="ckvT")
        S.add("pool", lambda e: [e.dma_start(out=H16[:, 16384:16384 + (nk // 128) * 260], in_=p["ckv1"])], writes=["ckv1"],
              dma_slot="ckv1")
        S.add("pool", lambda e: [e.dma_start(out=kidx, in_=p["kidxT"])], writes=["kidx"], dma_slot="kidxT")
        wuq = WA[:, 0:8192].rearrange("p (c f) -> p c f", c=4)
        wuk = WA[:, 8192:12288].rearrange("p (h c) -> p h c", h=16)
        wiq = WA[:, 12288:16384].rearrange("p (c f) -> p c f", c=4)
        wuv = WB[:, 0:4096].rearrange("p (h c v) -> p h c v", h=16, c=2)
        S.add("pool", lambda e: [e.dma_start(out=WA[:, 0:8192], in_=p["wuq"])], writes=["wuq"], dma_slot="wuq")
        S.add("pool", lambda e: [e.dma_start(out=WA[:, 8192:12288], in_=p["wuk"])], writes=["wuk"], dma_slot="wuk")
        S.add("pool", lambda e: [e.dma_start(out=WA[:, 12288:16384], in_=p["wiq"])], writes=["wiq"], dma_slot="wiq")
        S.add("pool", lambda e: [e.dma_start(out=WB[:, 0:4096], in_=p["wuv"])], writes=["wuv"], dma_slot="wuv")
        S.barrier()
        OT = xn
        rt = [MISC[:, 0:1024].bitcast(F32), MISC[:, 1024:2048].bitcast(F32)]
        et = [MISC[:, 2048:2560], MISC[:, 2560:3072]]
        selt = MISC[:, 3072:3584]
        olat = MISC[:, 3584:4608].rearrange("p (h c) -> p h c", h=4)
        olT = MISC[:, 4608:5632].rearrange("p (h c q) -> p h c q", h=4, c=2)
        sqa = MISC[:, 5632:5888]
        mx8 = self.smallf[:, 8:16]
        den = self.smallf[:, 16:20]
        rstda = self.smallf2[:, :]
        def emit_bcd(blk):
            bs = slice(blk * 128, (blk + 1) * 128)
            nkt = blk + 1
            qabsT = qabs_bufs[blk % 2]
            for h4 in range(4):
                pb = self.rot("qi", 2)

                def mmqi(e, h4=h4, pb=pb, bs=bs):
                    r = None
                    for jh in range(4):
                        h = h4 * 4 + jh
                        for c in range(4):
                            r = e.matmul(ps[pb][0:64, jh * 128:(jh + 1) * 128], lhsT=wiq[:, c, h * 64:(h + 1) * 64],
                                         rhs=cqT[:, c, bs], start=(c == 0), stop=(c == 3))
                    return r
                S.add("pe", mmqi, reads=["wiq"], writes=[kr("ps", pb)])
                S.add("act", lambda e, h4=h4, pb=pb: e.activation(
                    out=qidxT[0:64, h4 * 4:(h4 + 1) * 4, :], in_=ps[pb][0:64, :].rearrange("p (h q) -> p h q", h=4), func=AF.Copy),
                    reads=[kr("ps", pb)], writes=[kr("qidxT", h4)])
            for h4 in range(4):
                pb = self.rot("qi", 2)

                def mmqn(e, h4=h4, pb=pb, bs=bs):
                    r = None
                    for jh in range(4):
                        h = h4 * 4 + jh
                        for c in range(4):
                            r = e.matmul(ps[pb][:, jh * 128:(jh + 1) * 128], lhsT=wuq[:, c, h * 128:(h + 1) * 128],
                                         rhs=cqT[:, c, bs], start=(c == 0), stop=(c == 3))
                    return r
                S.add("pe", mmqn, reads=["wuq"], writes=[kr("ps", pb)])
                S.add("act", lambda e, h4=h4, pb=pb: e.activation(
                    out=qnT[:, h4 * 4:(h4 + 1) * 4, :], in_=ps[pb][:, :].rearrange("p (h q) -> p h q", h=4), func=AF.Copy),
                    reads=[kr("ps", pb)], writes=[kr("qnT", h4)])
            for h in range(16):
                pb = 2 + self.rot("qa", 2)
                pq = 4 + self.rot("qas", 2)

                def mmqa(e, h=h, pb=pb):
                    r = None
                    for cc in range(2):
                        r = e.matmul(ps[pb][:, cc * 128:(cc + 1) * 128], lhsT=wuk[:, h, cc * 128:(cc + 1) * 128], rhs=qnT[:, h, :],
                                     start=True, stop=True)
                    return r
                S.add("pe", mmqa, reads=["wuk", kr("qnT", h // 4)], writes=[kr("ps", pb)])
                S.add("act", lambda e, pb=pb: e.activation(out=sqa, in_=ps[pb][:, 0:256], func=AF.Square), reads=[kr("ps", pb)],
                      writes=["sqa"])

                def mmss(e, pq=pq):
                    e.matmul(ps[pq][:, 0:128], lhsT=self.ones[:], rhs=sqa[:, 0:128], start=True, stop=False)
                    return e.matmul(ps[pq][:, 0:128], lhsT=self.ones[:], rhs=sqa[:, 128:256], start=False, stop=True)
                S.add("pe", mmss, reads=["sqa", "ones"], writes=[kr("ps", pq)])
                S.add("act", lambda e, pq=pq: e.activation(out=rstda[:, 0:128], in_=ps[pq][:, 0:128], func=AF.Sqrt, bias=self.epsc[:],
                                                           scale=1.0 / 256), reads=[kr("ps", pq)], writes=["rstda"])
                S.add("dve", lambda e: e.reciprocal(out=rstda[:, 0:128], in_=rstda[:, 0:128]), reads=["rstda"], writes=["rstda"])
                for cc in range(2):
                    S.add("dve", lambda e, h=h, cc=cc, pb=pb: e.scalar_tensor_tensor(
                        out=qabsT[:, h, cc, :], in0=ps[pb][:, cc * 128:(cc + 1) * 128], scalar=gqa[:, cc:cc + 1],
                        in1=rstda[:, 0:128], op0=ALU.mult, op1=ALU.mult),
                        reads=[kr("ps", pb), "rstda", "gqa"], writes=[kr("qabsT", blk % 2, h)])
            for kt in range(nkt):
                ksl = slice(kt * 512, (kt + 1) * 512)
                for h in range(16):
                    pb = self.rot("sc", 2)
                    S.add("pe", lambda e, h=h, ksl=ksl, pb=pb: e.matmul(ps[pb][:], lhsT=qidxT[0:64, h, :], rhs=kidx[:, ksl],
                                                                        start=True, stop=True),
                          reads=["kidx", kr("qidxT", h // 4)], writes=[kr("ps", pb)])
                    S.add("act", lambda e, pb=pb: e.activation(out=rt[pb], in_=ps[pb][:], func=AF.Relu), reads=[kr("ps", pb)],
                          writes=[kr("rt", pb)])
                    wcol = widx[:, blk * 16 + h: blk * 16 + h + 1]
                    if h == 0:
                        S.add("dve", lambda e, pb=pb, ksl=ksl, wcol=wcol: e.tensor_scalar(
                            out=Wsc[:, ksl], in0=rt[pb], scalar1=wcol, scalar2=None, op0=ALU.mult),
                            reads=[kr("rt", pb), kr("widx", blk)], writes=[kr("W", kt)])
                    else:
                        S.add("dve", lambda e, pb=pb, ksl=ksl, wcol=wcol: e.scalar_tensor_tensor(
                            out=Wsc[:, ksl], in0=rt[pb], scalar=wcol, in1=Wsc[:, ksl], op0=ALU.mult, op1=ALU.add),
                            reads=[kr("rt", pb), kr("widx", blk), kr("W", kt)], writes=[kr("W", kt)])
                mi = self.rot("nmt", 2)
                S.add("pool", lambda e, mi=mi, blk=blk, kt=kt: [e.dma_start(out=nmt[mi], in_=p["negm"][blk, kt])],
                      writes=[kr("nmt", mi)], dma_slot=("nmt", mi))
                S.add("dve", lambda e, ksl=ksl, mi=mi: e.tensor_tensor(out=Wsc[:, ksl], in0=Wsc[:, ksl], in1=nmt[mi], op=ALU.min),
                      reads=[kr("W", kt), kr("nmt", mi)], writes=[kr("W", kt)])

        def emit_topk(blk, its):
            nkt = blk + 1
            wkeys = [kr("W", kt) for kt in range(nkt)]
            Wv = Wsc[:, 0:nkt * 512]
            for it in its:
                S.add("dve", lambda e, Wv=Wv: e.max(out=mx8, in_=Wv), reads=wkeys, writes=["mx8"])
                S.add("dve", lambda e, Wv=Wv: e.match_replace(out=Wv, in_to_replace=mx8, in_values=Wv, imm_value=-3.0e38),
                      reads=wkeys + ["mx8"], writes=wkeys)

        def emit_sel(blk):
            nkt = blk + 1
            for kt in range(nkt):
                ksl = slice(kt * 512, (kt + 1) * 512)
                S.add("dve", lambda e, ksl=ksl: e.tensor_scalar(out=selt, in0=Wsc[:, ksl], scalar1=-2.0e38, scalar2=None, op0=ALU.is_le),
                      reads=[kr("W", kt)], writes=["selt"])
                di = self.rot("dmt", 2)
                S.add("pool", lambda e, di=di, blk=blk, kt=kt: [e.dma_start(out=dmt[di], in_=p["dmk"][blk, kt])],
                      writes=[kr("dmt", di)], dma_slot=("dmt", di))
                S.add("dve", lambda e, di=di: e.tensor_tensor(out=selt, in0=selt, in1=dmt[di], op=ALU.mult),
                      reads=["selt", kr("dmt", di)], writes=["selt"])
                nb = 4
                tp_i = 2 + self.rot("st", 2)
                tpv = ps[tp_i][:].bitcast(BF16)

                def mmt(e, nb=nb, tpv=tpv):
                    r = None
                    for i in range(nb):
                        r = e.transpose(tpv[:, i * 128:(i + 1) * 128], selt[:, i * 128:(i + 1) * 128], self.ident[:])
                    return r
                S.add("pe", mmt, reads=["selt", "ident"], writes=[kr("ps", tp_i)])
                S.add("act", lambda e, kt=kt, nb=nb, tpv=tpv: e.activation(
                    out=selT[:, kt * 4:kt * 4 + nb, :], in_=tpv[:, 0:nb * 128].rearrange("p (k q) -> p k q", q=128), func=AF.Copy),
                    reads=[kr("ps", tp_i)], writes=[kr("selT", kt)])

        def emit_att(blk, hg):
            bs = slice(blk * 128, (blk + 1) * 128)
            qabsT = qabs_bufs[blk % 2]
            nkb = 4 * blk + 4
            for kb in range(nkb):
                kbs = slice(kb * 128, (kb + 1) * 128)
                pb = self.rot("lg", 2)

                def mml(e, hg=hg, kbs=kbs, pb=pb):
                    r = None
                    for jh in range(4):
                        for cc in range(2):
                            r = e.matmul(ps[pb][:, jh * 128:(jh + 1) * 128], lhsT=ckvT[:, cc, kbs], rhs=qabsT[:, hg * 4 + jh, cc, :],
                                         start=(cc == 0), stop=(cc == 1))
                    return r
                S.add("pe", mml, reads=["ckvT"] + [kr("qabsT", blk % 2, hg * 4 + jh_) for jh_ in range(4)], writes=[kr("ps", pb)])
                S.add("act", lambda e, pb=pb: e.activation(out=et[pb], in_=ps[pb][:], func=AF.Exp, scale=256.0 ** -0.5),
                      reads=[kr("ps", pb)], writes=[kr("et", pb)])
                S.add("pool", lambda e, pb=pb, kb=kb: e.tensor_tensor(
                    out=et[pb].rearrange("p (j q) -> p j q", j=4), in0=et[pb].rearrange("p (j q) -> p j q", j=4),
                    in1=selT[:, kb:kb + 1, :].to_broadcast([128, 4, 128]), op=ALU.mult),
                    reads=[kr("et", pb), kr("selT", kb // 4)], writes=[kr("et", pb)])

                def mmo(e, pb=pb, kb=kb, gb=nkb - 1):
                    r = None
                    for jh in range(4):
                        r = e.matmul(ps[4 + jh][:, 0:257], lhsT=et[pb][:, jh * 128:(jh + 1) * 128], rhs=ckv1[:, kb, 0:257],
                                     start=(kb == 0), stop=(kb == gb))
                    return r
                S.add("pe", mmo, reads=[kr("et", pb), "ckv1"], writes=[kr("ps", 4 + jh_) for jh_ in range(4)])
            for jh in range(4):
                S.add("dve", lambda e, jh=jh: e.reciprocal(out=den[:, jh:jh + 1], in_=ps[4 + jh][:, 256:257]),
                      reads=[kr("ps", 4 + jh)], writes=[kr("den", jh)])
                S.add("dve", lambda e, jh=jh: e.tensor_scalar(out=olat[:, jh, :], in0=ps[4 + jh][:, 0:256], scalar1=den[:, jh:jh + 1],
                                                             scalar2=None, op0=ALU.mult),
                      reads=[kr("ps", 4 + jh), kr("den", jh)], writes=[kr("olat", jh)])
            tpv = ps[2][:].bitcast(BF16)

            def mmt2(e, tpv=tpv):
                r = None
                for jh in range(4):
                    for cc in range(2):
                        r = e.transpose(tpv[:, (jh * 2 + cc) * 128:(jh * 2 + cc + 1) * 128], olat[:, jh, cc * 128:(cc + 1) * 128],
                                        self.ident[:])
                return r
            S.add("pe", mmt2, reads=[kr("olat", jh_) for jh_ in range(4)] + ["ident"], writes=[kr("ps", 2)])
            S.add("act", lambda e, tpv=tpv: e.activation(out=olT, in_=tpv.rearrange("p (h c q) -> p h c q", h=4, c=2), func=AF.Copy),
                  reads=[kr("ps", 2)], writes=["olT"])

            def mmv(e, hg=hg):
                r = None
                for jh in range(4):
                    for cc in range(2):
                        r = e.matmul(ps[3][:, jh * 128:(jh + 1) * 128], lhsT=wuv[:, hg * 4 + jh, cc, :], rhs=olT[:, jh, cc, :],
                                     start=(cc == 0), stop=(cc == 1))
                return r
            S.add("pe", mmv, reads=["olT", "wuv"], writes=[kr("ps", 3)])
            S.add("act", lambda e, hg=hg, bs=bs: e.activation(
                out=OT[:, hg * 4:(hg + 1) * 4, bs], in_=ps[3][:, :].rearrange("p (h q) -> p h q", h=4), func=AF.Copy),
                reads=[kr("ps", 3)], writes=[kr("OTd", blk, hg)])

        for blk in range(8):
            emit_bcd(blk)
            for q4 in range(4):
                emit_topk(blk, range(q4 * 8, q4 * 8 + 8))
                if blk > 0:
                    emit_att(blk - 1, q4)
            emit_sel(blk)
        for q4 in range(4):
            emit_att(7, q4)
        S.barrier()
        self.load_h(xT)
        S.barrier()
        self.wo_proj(p["wo"])

    def finish(self):
        S = self.S
        nc = self.nc
        with nc.Block() as block:
            @block.tensor
            def _(e):
                S.emit("pe", e)

            @block.scalar
            def _(e):
                S.emit("act", e)

            @block.vector
            def _(e):
                S.emit("dve", e)

            @block.gpsimd
            def _(e):
                S.emit("pool", e)

            @block.sync
            def _(e):
                S.emit("sp", e)
                S.final_wait(e, self.out_ops)
                import os
                if os.environ.get("GLA_FW"):
                    for en in os.environ["GLA_FW"].split(","):
                        cands = [o for o in S.ops[en] if not o.is_dma]
                        if cands:
                            S.final_wait(e, [cands[-1]])
        self.stack.close()
        return nc


def lay_vec(g):
    return np.ascontiguousarray(np.asarray(g, np.float32).reshape(KC, 128).T)


def lay_kslab(w, ncol):
    w = np.asarray(w, np.float32)
    K_, N_ = w.shape
    ns = N_ // ncol
    out = w.reshape(K_ // 128, 128, ns, ncol).transpose(2, 1, 0, 3).reshape(ns, 128, (K_ // 128) * ncol)
    return np.ascontiguousarray(out)


def lay_wgu(w, GW=256):
    w = np.asarray(w, np.float32)
    nslab = DFF // GW
    g = w[:, :DFF].reshape(KC, 128, nslab, GW)
    u = w[:, DFF:].reshape(KC, 128, nslab, GW)
    gu = np.stack([g, u], axis=3)
    out = gu.transpose(2, 1, 0, 3, 4).reshape(nslab, 128, KC * 2 * GW)
    return np.ascontiguousarray(out)


def lay_wd(w, GRP=4):
    w = np.asarray(w, np.float32)
    ngrp = NFC // GRP
    out = w.reshape(ngrp, GRP, 128, D).transpose(0, 2, 1, 3).reshape(ngrp, 128, GRP * D)
    return np.ascontiguousarray(out)


def lay_wo(w):
    w = np.asarray(w, np.float32)
    out = w.reshape(KC, 128, 4, 4, 128).transpose(2, 1, 3, 0, 4).reshape(4, 128, 4 * KC * 128)
    return np.ascontiguousarray(out)


def swa_host_inputs(pre, inp, halo_rows, first):
    w_in = np.asarray(inp[pre + "w_in"], np.float32)
    wq = w_in[:, :2048]
    wk = w_in[:, 2048:2304]
    wv = w_in[:, 2304:2560]
    wkp = np.zeros((2048, 8, 128), np.float32)
    for hk in range(4):
        wkp[:, hk * 2 + 0, 0:64] = wk[:, hk * 64:(hk + 1) * 64]
        wkp[:, hk * 2 + 1, 64:128] = wk[:, hk * 64:(hk + 1) * 64]
    s_idx = np.arange(128)[:, None]
    q_idx = np.arange(128)[None, :]
    mprev = (s_idx > q_idx).astype(np.float32)
    mcur = (s_idx <= q_idx).astype(np.float32)
    m0 = np.zeros_like(mprev) if first else mprev
    masks = np.stack([m0, mprev, mcur], axis=1).reshape(128, 3 * 128)
    qkn = np.stack([np.tile(np.asarray(inp[pre + "q_norm"], np.float32), 2),
                    np.tile(np.asarray(inp[pre + "k_norm"], np.float32), 2)], axis=1)
    return {
        "g_mix": lay_vec(inp[pre + "norm_mix"]),
        "haloT": np.ascontiguousarray(halo_rows.T),
        "wq": lay_kslab(wq, 512),
        "wk": lay_kslab(wkp.reshape(2048, 1024), 512),
        "wv": lay_kslab(wv, 256)[0],
        "wo": lay_wo(inp[pre + "wo"]),
        "qkn": np.ascontiguousarray(qkn),
        "sinks": np.ascontiguousarray(np.broadcast_to(np.asarray(inp[pre + "sinks"], np.float32)[None, :], (128, 32))),
        "masks": np.ascontiguousarray(masks),
    }


def swa_dram(B, pfx):
    return {
        "g_mix": B.din(pfx + "g_mix", [128, KC]),
        "haloT": B.din(pfx + "haloT", [D, 128]),
        "wq": B.din(pfx + "wq", [4, 128, KC * 512]),
        "wk": B.din(pfx + "wk", [2, 128, KC * 512]),
        "wv": B.din(pfx + "wv", [128, KC * 256]),
        "wo": B.din(pfx + "wo", [4, 128, 4 * KC * 128]),
        "qkn": B.din(pfx + "qkn", [128, 2]),
        "sinks": B.din(pfx + "sinks", [128, 32]),
        "masks": B.din(pfx + "masks", [128, 3 * 128]),
    }


def gla_host_inputs(pre, inp, sprev, dprev):
    w_in = np.asarray(inp[pre + "w_in"], np.float32)
    q = w_in[:, 0:1024]; k = w_in[:, 1024:2048]; v = w_in[:, 2048:4096]; r = w_in[:, 4096:6144]; gl = w_in[:, 6144:6160]
    wA = np.concatenate([np.concatenate([q[:, h * 256:(h + 1) * 256], k[:, h * 256:(h + 1) * 256]], axis=1) for h in range(4)], axis=1)
    wgb = np.zeros((64, 1024), np.float32)
    wgb[0:16] = np.asarray(inp[pre + "w_gate_b"], np.float32)
    wgb[32] = np.asarray(inp[pre + "gate_bias"], np.float32)
    s_idx = np.arange(128)[:, None]; t_idx = np.arange(128)[None, :]
    uneg = np.where(s_idx <= t_idx, -1.0 / 16, 0.0).astype(np.float32)
    ugt = np.where(s_idx > t_idx, -1.0 / 16, 0.0).astype(np.float32)
    cz = (s_idx <= t_idx).astype(np.float32)
    masks = np.stack([uneg, ugt, cz], axis=1).reshape(128, 384)
    wo = np.asarray(inp[pre + "wo"], np.float32)
    woh = wo.reshape(4, 4, 128, 2048).transpose(0, 2, 1, 3).reshape(4, 128, 4 * 2048)
    return {
        "g_mix": lay_vec(inp[pre + "norm_mix"]),
        "wA": lay_kslab(wA, 512), "wB": lay_kslab(v, 512), "wC": lay_kslab(r, 512),
        "wgl": lay_kslab(gl, 16)[0], "wgb": wgb,
        "onorm": np.ascontiguousarray(np.broadcast_to(np.asarray(inp[pre + "o_norm"], np.float32)[None, :], (128, 512))),
        "woh": np.ascontiguousarray(woh), "masks": np.ascontiguousarray(masks),
        "sprev": np.ascontiguousarray(sprev, dtype=np.float32), "dprev": np.ascontiguousarray(dprev, dtype=np.float32),
    }


def gla_dram(B, pfx, with_output):
    d = {
        "g_mix": B.din(pfx + "g_mix", [128, KC]),
        "wA": B.din(pfx + "wA", [4, 128, KC * 512]), "wB": B.din(pfx + "wB", [4, 128, KC * 512]),
        "wgl": B.din(pfx + "wgl", [128, KC * 16]), "wgb": B.din(pfx + "wgb", [64, 1024]),
        "masks": B.din(pfx + "masks", [128, 384]),
        "onorm": B.din(pfx + "onorm", [128, 512]),
        "dprev": B.din(pfx + "dprev", [3, 128, 8]),
    }
    if with_output:
        d["wC"] = B.din(pfx + "wC", [4, 128, KC * 512])
        d["woh"] = B.din(pfx + "woh", [4, 128, 4 * 2048])
        d["sprev"] = B.din(pfx + "sprev", [3, 8, 128, 512])
    else:
        d["sloc"] = B.dout(pfx + "sloc", [8, 128, 512])
        d["dtot"] = B.dout(pfx + "dtot", [128, 8])
    return d


def dsa_prep_host(pre, inp):
    w_in = np.asarray(inp[pre + "w_in"], np.float32)
    wkk = np.concatenate([w_in[:, 512:768], w_in[:, 768:832]], axis=1)
    bc = lambda v, n: np.ascontiguousarray(np.broadcast_to(np.asarray(v, np.float32)[None, :], (128, n)))
    return {"g_mix": lay_vec(inp[pre + "norm_mix"]), "wkk": lay_kslab(wkk, 320)[0],
            "kvg": bc(inp[pre + "kv_norm"], 256), "ikg": bc(inp[pre + "idx_k_norm"], 64), "ikb": bc(inp[pre + "idx_k_bias"], 64)}


def dsa_prep_dram(B, pfx):
    return {"g_mix": B.din(pfx + "g_mix", [128, KC]), "wkk": B.din(pfx + "wkk", [128, KC * 320]),
            "kvg": B.din(pfx + "kvg", [128, 256]), "ikg": B.din(pfx + "ikg", [128, 64]), "ikb": B.din(pfx + "ikb", [128, 64]),
            "ckv": B.dout(pfx + "ckv", [T, 256]), "kidx": B.dout(pfx + "kidx", [T, 64])}


def dsa_main_host(pre, inp, ckv_seq, kidx_seq):
    nk = 4 * T
    w_in = np.asarray(inp[pre + "w_in"], np.float32)
    ck = np.asarray(ckv_seq[:nk], np.float32)
    ckvT = np.ascontiguousarray(ck.T.reshape(2, 128, nk).transpose(1, 0, 2).reshape(128, 2 * nk))
    ckv1 = np.zeros((nk // 128, 128, 260), np.float32)
    ckv1[:, :, 0:256] = ck.reshape(nk // 128, 128, 256)
    ckv1[:, :, 256] = 1.0
    ckv1 = np.ascontiguousarray(ckv1.transpose(1, 0, 2).reshape(128, (nk // 128) * 260))
    kidxT = np.ascontiguousarray(np.asarray(kidx_seq[:nk], np.float32).T)
    w_uk = np.asarray(inp[pre + "w_uk"], np.float32)
    w_uv = np.asarray(inp[pre + "w_uv"], np.float32)
    w_uq = np.asarray(inp[pre + "w_uq"], np.float32)
    w_iq = np.asarray(inp[pre + "w_iq"], np.float32)
    return {
        "g_mix": lay_vec(inp[pre + "norm_mix"]),
        "wcq": lay_kslab(w_in[:, 0:512], 512)[0], "wwi": lay_kslab(w_in[:, 832:848], 16)[0],
        "gq": np.ascontiguousarray(np.asarray(inp[pre + "q_lat_norm"], np.float32).reshape(4, 128).T),
        "gqa": np.ascontiguousarray(np.asarray(inp[pre + "q_abs_norm"], np.float32).reshape(2, 128).T),
        "wuq": lay_kslab(w_uq, 2048)[0], "wiq": lay_kslab(w_iq, 1024)[0],
        "wuk": np.ascontiguousarray(w_uk.transpose(1, 0, 2).reshape(128, 16 * 256)),
        "wuv": np.ascontiguousarray(w_uv.reshape(16, 2, 128, 128).transpose(2, 0, 1, 3).reshape(128, 16 * 2 * 128)),
        "wo": lay_wo(inp[pre + "wo"]),
        "ckvT": ckvT, "ckv1": ckv1, "kidxT": kidxT,
    }


def dsa_masks_host(j):
    q = np.arange(128)[:, None]
    kcol = np.arange(512)[None, :]
    diag = ((kcol // 128 < j) | ((kcol // 128 == j) & (kcol % 128 <= q))).astype(np.float32)
    ones = np.ones((128, 512), np.float32)
    zeros = np.zeros((128, 512), np.float32)
    dmk = np.empty((8, 8, 128, 512), np.float32)
    for blk in range(8):
        for kt in range(8):
            dmk[blk, kt] = ones if kt < blk else (diag if kt == blk else zeros)
    negm = np.where(dmk > 0, np.float32(3e38), np.float32(-1e30)).astype(np.float32)
    return {"dmk": dmk, "negm": negm}


def dsa_main_dram(B, pfx):
    nk = 4 * T
    return {"g_mix": B.din(pfx + "g_mix", [128, KC]), "wcq": B.din(pfx + "wcq", [128, KC * 512]),
            "wwi": B.din(pfx + "wwi", [128, KC * 16]), "gq": B.din(pfx + "gq", [128, 4]), "gqa": B.din(pfx + "gqa", [128, 2]),
            "wuq": B.din(pfx + "wuq", [128, 4 * 2048]), "wiq": B.din(pfx + "wiq", [128, 4 * 1024]),
            "wuk": B.din(pfx + "wuk", [128, 16 * 256]), "wuv": B.din(pfx + "wuv", [128, 16 * 2 * 128]),
            "wo": B.din(pfx + "wo", [4, 128, 4 * KC * 128]),
            "ckvT": B.din(pfx + "ckvT", [128, 2 * nk]), "ckv1": B.din(pfx + "ckv1", [128, (nk // 128) * 260]),
            "kidxT": B.din(pfx + "kidxT", [64, nk]), "dmk": B.din(pfx + "dmk", [8, 8, 128, 512]),
            "negm": B.din(pfx + "negm", [8, 8, 128, 512])}


def ffn_dram(B, pfx):
    return {"g": B.din(pfx + "g", [128, KC]), "wgu": B.din(pfx + "wgu", [NFC // 2, 128, KC * 2 * 256]),
            "wd": B.din(pfx + "wd", [NFC // 4, 128, 4 * D])}


def ffn_host_inputs(pre, inp):
    return {"g": lay_vec(inp[pre + "norm_ffn"]), "wgu": lay_wgu(inp[pre + "w_gate_up"]), "wd": lay_wd(inp[pre + "w_down"])}


IDENT = np.eye(128, dtype=np.float32)


def _prog(stages_fn, extra_out=None):
    B = Builder()
    xT = B.din("xT", [D, T])
    ident = B.din("ident", [128, 128])
    outT = B.dout("outT", [D, T])
    B.alloc()
    B.load_ident(ident)
    B.load_h(xT)
    stages_fn(B, xT)
    B.S.barrier()
    B.store_h(outT)
    return B.finish()


def _run(nc, in_maps):
    n = len(in_maps)
    res = run_bass_kernel_spmd(nc, in_maps, core_ids=list(range(n)))
    return res.results


def _pfx(p, d):
    return {p + k: v for k, v in d.items()}


def kernel(**inputs):
    inp = {k: np.asarray(v) for k, v in inputs.items()}
    h = np.ascontiguousarray(inp["x"], dtype=np.float32)
    cores = [(c // 4, c % 4) for c in range(NCORES)]

    def own(hh, b, j):
        return np.ascontiguousarray(hh[b, j * T:(j + 1) * T].T)

    def gather(results, key="outT"):
        out = np.empty((2, 4 * T, D), np.float32)
        for c, (b, j) in enumerate(cores):
            out[b, j * T:(j + 1) * T] = results[c][key].T
        return out

    def swa_maps(li, hh):
        pre = f"l{li}_"
        shared = None
        maps = []
        for (b, j) in cores:
            halo = hh[b, j * T - 128:j * T] if j > 0 else np.zeros((128, D), np.float32)
            hi = swa_host_inputs(pre, inp, halo, j == 0)
            if shared is None:
                shared = hi
            else:
                for k in ("wq", "wk", "wv", "wo", "g_mix", "qkn", "sinks"):
                    hi[k] = shared[k]
            maps.append(_pfx("a_", hi))
        return maps

    zs = np.zeros((3, 8, 128, 512), np.float32)
    zd = np.ones((3, 128, 8), np.float32)
    g1 = gla_host_inputs("l1_", inp, zs, zd)
    g1p = {k: v for k, v in g1.items() if k not in ("wC", "woh", "sprev")}
    f0 = ffn_host_inputs("l0_", inp)

    def stA(B, xT):
        P = swa_dram(B, "a_")
        F = ffn_dram(B, "f_")
        G = gla_dram(B, "b_", False)
        B.swa(P)
        B.ffn(F["g"], F["wgu"], F["wd"])
        B.gla(G, False)
    ncA = _prog(stA)
    sm = swa_maps(0, h)
    maps = []
    for c, (b, j) in enumerate(cores):
        im = {"xT": own(h, b, j), "ident": IDENT}
        im.update(sm[c]); im.update(_pfx("f_", f0)); im.update(_pfx("b_", g1p))
        maps.append(im)
    rA = _run(ncA, maps)
    h = gather(rA)
    del f0, sm, maps

    f1 = ffn_host_inputs("l1_", inp)
    dp = dsa_prep_host("l2_", inp)

    def stB(B, xT):
        G = gla_dram(B, "b_", True)
        F = ffn_dram(B, "f_")
        Pp = dsa_prep_dram(B, "c_")
        B.gla(G, True)
        B.ffn(F["g"], F["wgu"], F["wd"])
        B.dsa_prep(Pp)
    ncB = _prog(stB)
    maps = []
    for c, (b, j) in enumerate(cores):
        sp = np.zeros((3, 8, 128, 512), np.float32)
        dpv = np.ones((3, 128, 8), np.float32)
        for s_ in range(3):
            i = j - 3 + s_
            if i >= 0:
                sp[s_] = rA[b * 4 + i]["b_sloc"]
                dpv[s_] = rA[b * 4 + i]["b_dtot"]
        gi = dict(g1)
        gi["sprev"] = sp
        gi["dprev"] = dpv
        im = {"xT": own(h, b, j), "ident": IDENT}
        im.update(_pfx("b_", gi)); im.update(_pfx("f_", f1)); im.update(_pfx("c_", dp))
        maps.append(im)
    rB = _run(ncB, maps)
    h = gather(rB)
    ckv = [np.concatenate([rB[b * 4 + j]["c_ckv"] for j in range(4)], 0) for b in range(2)]
    kidx = [np.concatenate([rB[b * 4 + j]["c_kidx"] for j in range(4)], 0) for b in range(2)]
    del f1, g1, g1p, maps, rA

    f2 = ffn_host_inputs("l2_", inp)

    def stC(B, xT):
        P = dsa_main_dram(B, "c_")
        F = ffn_dram(B, "f_")
        B.dsa_main(P, xT)
        B.ffn(F["g"], F["wgu"], F["wd"])
    ncC = _prog(stC)
    hm = [dsa_main_host("l2_", inp, ckv[b], kidx[b]) for b in range(2)]
    for k in ("g_mix", "wcq", "wwi", "gq", "gqa", "wuq", "wiq", "wuk", "wuv", "wo"):
        hm[1][k] = hm[0][k]
    mk = [dsa_masks_host(j) for j in range(4)]
    def own_il(hh, b, j):
        rows = hh[b].reshape(8, 4, 128, D)[:, j].reshape(T, D)
        return np.ascontiguousarray(rows.T)
    maps = []
    for c, (b, j) in enumerate(cores):
        im = {"xT": own_il(h, b, j), "ident": IDENT}
        im.update(_pfx("c_", hm[b])); im.update(_pfx("c_", mk[j])); im.update(_pfx("f_", f2))
        maps.append(im)
    rC = _run(ncC, maps)
    h = np.empty((2, 4 * T, D), np.float32)
    for c, (b, j) in enumerate(cores):
        h[b].reshape(8, 4, 128, D)[:, j] = rC[c]["outT"].T.reshape(8, 128, D)
    del f2, hm, mk, maps

    f3 = ffn_host_inputs("l3_", inp)

    def stD(B, xT):
        P = swa_dram(B, "a_")
        F = ffn_dram(B, "f_")
        B.swa(P)
        B.ffn(F["g"], F["wgu"], F["wd"])
    ncD = _prog(stD)
    sm = swa_maps(3, h)
    maps = []
    for c, (b, j) in enumerate(cores):
        im = {"xT": own(h, b, j), "ident": IDENT}
        im.update(sm[c]); im.update(_pfx("f_", f3))
        maps.append(im)
    rD = _run(ncD, maps)
    return gather(rD)
```
